# Optimizing a Trainium2 kernel written in Bass

```python
import jax, jax.numpy as jnp
from jax import lax
import numpy as np

D_MODEL = 1024
BATCH = 2
SEQ = 8192
DEPTH = 2

GRID_W = 64
CTX_LEN = 256
HEAD_DIM = 64
BLOCK = 128
WINDOW = 128
A_HEADS = 4
A_KV_HEADS = 2
B_HEADS = 4
B_KV_HEADS = 2
C_HEADS = 8
C_Q_RANK = 256
C_KV_RANK = 128
C_NOPE_DIM = 64
C_ROPE_DIM = 32
C_V_DIM = 64
D_MIX = (A_HEADS + B_HEADS) * HEAD_DIM + C_HEADS * C_V_DIM
D_FF = 4 * D_MODEL
ROPE_THETA = 10000.0
NORM_EPS = 1e-6
NEG_INF = -1e30
DEEPNORM_ALPHA = (2 * DEPTH) ** 0.25
DEEPNORM_BETA = (8 * DEPTH) ** -0.25
IN_SPLITS = (A_HEADS * HEAD_DIM, A_KV_HEADS * HEAD_DIM, A_KV_HEADS * HEAD_DIM,
             B_HEADS * HEAD_DIM, B_KV_HEADS * HEAD_DIM, B_KV_HEADS * HEAD_DIM,
             C_Q_RANK, C_KV_RANK, C_ROPE_DIM)
IN_OFFSETS = tuple(int(o) for o in np.cumsum(IN_SPLITS)[:-1])
D_IN = int(sum(IN_SPLITS))

kernel_name = "hymba_style_hybrid_dit_block"

F32 = jnp.float32


def _rms_norm(x, g):
    xf = x.astype(F32)
    y = xf * lax.rsqrt(jnp.mean(xf * xf, axis=-1, keepdims=True) + NORM_EPS)
    return (y * g.astype(F32)).astype(x.dtype)


def _layer_norm(x, g, b):
    xf = x.astype(F32)
    mu = jnp.mean(xf, axis=-1, keepdims=True)
    var = jnp.mean(jnp.square(xf - mu), axis=-1, keepdims=True)
    y = (xf - mu) * lax.rsqrt(var + NORM_EPS) * g.astype(F32) + b.astype(F32)
    return y.astype(x.dtype)


def _axial_rope_tables(n_rows, rot_dim):
    row = jnp.repeat(jnp.arange(n_rows, dtype=F32), GRID_W)
    col = jnp.tile(jnp.arange(GRID_W, dtype=F32), n_rows)
    n_freq = rot_dim // 4
    inv = ROPE_THETA ** (-jnp.arange(n_freq, dtype=F32) / n_freq)
    ang = jnp.concatenate([row[:, None] * inv, col[:, None] * inv], axis=-1)
    return jnp.cos(ang), jnp.sin(ang)


def _apply_rope(x, cos, sin):
    half = x.shape[-1] // 2
    xf = x.astype(F32)
    x1, x2 = xf[..., :half], xf[..., half:]
    c, s = cos[None, :, None, :], sin[None, :, None, :]
    return jnp.concatenate([x1 * c - x2 * s, x1 * s + x2 * c], axis=-1).astype(x.dtype)


def _heads(t, n_heads):
    return t.reshape(t.shape[0], t.shape[1], n_heads, t.shape[-1] // n_heads)


def _merge(t):
    return t.reshape(t.shape[0], t.shape[1], -1)


def _gqa_core(q, k, v, sink=None):
    bsz, n_q, n_h, dh = q.shape
    n_kv = k.shape[2]
    grp = n_h // n_kv
    n_keys = k.shape[1]
    qg = q.reshape(bsz, n_q, n_kv, grp, dh)
    s = jnp.einsum('bqkgd,bskd->bkgqs', qg, k).astype(F32) * (dh ** -0.5)
    if sink is not None:
        sink_logit = jnp.broadcast_to(sink.astype(F32).reshape(n_kv, grp)[None, :, :, None, None], s.shape[:-1] + (1,))
        s = jnp.concatenate([s, sink_logit], axis=-1)
    p = jax.nn.softmax(s, axis=-1)[..., :n_keys].astype(v.dtype)
    out = jnp.einsum('bkgqs,bskd->bqkgd', p, v)
    return out.reshape(bsz, n_q, n_h, dh)


def _mla_core(q_nope, q_rope, k_nope, k_rope, v):
    scale = (C_NOPE_DIM + C_ROPE_DIM) ** -0.5
    s = jnp.einsum('bqhd,bshd->bhqs', q_nope, k_nope) + jnp.einsum('bqhr,bsr->bhqs', q_rope, k_rope)
    p = jax.nn.softmax(s.astype(F32) * scale, axis=-1).astype(v.dtype)
    return jnp.einsum('bhqs,bshd->bqhd', p, v)


def _sweep_query_blocks(fn, qs):
    nb = qs[0].shape[1] // BLOCK
    to_blocks = lambda t: jnp.moveaxis(t.reshape(t.shape[0], nb, BLOCK, *t.shape[2:]), 1, 0)
    out = lax.map(lambda qb: fn(*qb), tuple(to_blocks(t) for t in qs))
    out = jnp.moveaxis(out, 0, 1)
    return out.reshape(out.shape[0], nb * BLOCK, *out.shape[3:])


def _window_attn(q, k, v, k_ctx, v_ctx, sink):
    bsz, n_tok, n_h, dh = q.shape
    n_kv = k.shape[2]
    grp = n_h // n_kv
    nb = n_tok // BLOCK
    n_loc, n_ctx = 3 * BLOCK, k_ctx.shape[1]
    scale = dh ** -0.5
    qb = q.reshape(bsz, nb, BLOCK, n_kv, grp, dh)
    pad = ((0, 0), (BLOCK, BLOCK), (0, 0), (0, 0))
    band = (jnp.arange(nb) * BLOCK)[:, None] + jnp.arange(n_loc)[None, :]
    kb = jnp.pad(k, pad)[:, band]
    vb = jnp.pad(v, pad)[:, band]
    s_loc = jnp.einsum('bnqkgd,bnskd->bnkgqs', qb, kb).astype(F32) * scale
    s_ctx = jnp.einsum('bnqkgd,bckd->bnkgqc', qb, k_ctx).astype(F32) * scale
    q_pos = (jnp.arange(nb) * BLOCK)[:, None] + jnp.arange(BLOCK)[None, :]
    k_pos = band - BLOCK
    rel = k_pos[:, None, :] - q_pos[:, :, None]
    valid = (jnp.abs(rel) <= WINDOW) & (k_pos[:, None, :] >= 0) & (k_pos[:, None, :] < n_tok)
    s_loc = jnp.where(valid[None, :, None, None], s_loc, NEG_INF)
    sink_logit = jnp.broadcast_to(sink.astype(F32).reshape(n_kv, grp)[None, None, :, :, None, None], s_loc.shape[:-1] + (1,))
    p = jax.nn.softmax(jnp.concatenate([s_loc, s_ctx, sink_logit], axis=-1), axis=-1).astype(v.dtype)
    out = (jnp.einsum('bnkgqs,bnskd->bnqkgd', p[..., :n_loc], vb)
           + jnp.einsum('bnkgqc,bckd->bnqkgd', p[..., n_loc:n_loc + n_ctx], v_ctx))
    return out.reshape(bsz, n_tok, n_h, dh)


def _token_mixers(h, hc, w_in, sink_a, q_norm_b, k_norm_b, mla_q_norm, mla_kv_norm,
                  w_uq, w_uk, w_uv, w_out, rope64, rope32, with_ctx_out):
    cos64, sin64 = rope64
    cos32, sin32 = rope32
    a_q, a_k, a_v, b_q, b_k, b_v, c_q, c_kv, c_kr = jnp.split(h @ w_in, IN_OFFSETS, axis=-1)
    a_qc, a_kc, a_vc, b_qc, b_kc, b_vc, c_qc, c_kvc, c_krc = jnp.split(hc @ w_in, IN_OFFSETS, axis=-1)

    q_a = _apply_rope(_heads(a_q, A_HEADS), cos64, sin64)
    k_a = _apply_rope(_heads(a_k, A_KV_HEADS), cos64, sin64)
    v_a = _heads(a_v, A_KV_HEADS)
    k_a_ctx, v_a_ctx = _heads(a_kc, A_KV_HEADS), _heads(a_vc, A_KV_HEADS)
    y_a = _window_attn(q_a, k_a, v_a, k_a_ctx, v_a_ctx, sink_a)

    q_b = _apply_rope(_rms_norm(_heads(b_q, B_HEADS), q_norm_b), cos64, sin64)
    k_b = _apply_rope(_rms_norm(_heads(b_k, B_KV_HEADS), k_norm_b), cos64, sin64)
    v_b = _heads(b_v, B_KV_HEADS)
    k_b_ctx = _rms_norm(_heads(b_kc, B_KV_HEADS), k_norm_b)
    v_b_ctx = _heads(b_vc, B_KV_HEADS)
    k_b_all = jnp.concatenate([k_b_ctx, k_b], axis=1)
    v_b_all = jnp.concatenate([v_b_ctx, v_b], axis=1)
    y_b = _sweep_query_blocks(lambda qi: _gqa_core(qi, k_b_all, v_b_all), (q_b,))

    def mla_q(cq):
        q = _heads(_rms_norm(cq, mla_q_norm) @ w_uq, C_HEADS)
        return q[..., :C_NOPE_DIM], q[..., C_NOPE_DIM:]

    def mla_kv(ckv):
        ckv = _rms_norm(ckv, mla_kv_norm)
        return _heads(ckv @ w_uk, C_HEADS), _heads(ckv @ w_uv, C_HEADS)

    q_c_nope, q_c_rope = mla_q(c_q)
    q_c_rope = _apply_rope(q_c_rope, cos32, sin32)
    k_c_nope, v_c = mla_kv(c_kv)
    k_c_rope = _apply_rope(c_kr[:, :, None, :], cos32, sin32)[:, :, 0, :]
    k_c_nope_ctx, v_c_ctx = mla_kv(c_kvc)
    k_c_nope_all = jnp.concatenate([k_c_nope_ctx, k_c_nope], axis=1)
    k_c_rope_all = jnp.concatenate([c_krc, k_c_rope], axis=1)
    v_c_all = jnp.concatenate([v_c_ctx, v_c], axis=1)
    y_c = _sweep_query_blocks(lambda qn, qr: _mla_core(qn, qr, k_c_nope_all, k_c_rope_all, v_c_all),
                              (q_c_nope, q_c_rope))

    y = jnp.concatenate([_merge(y_a), _merge(y_b), _merge(y_c)], axis=-1) @ w_out
    if not with_ctx_out:
        return y, None

    yc_a = _gqa_core(_heads(a_qc, A_HEADS), k_a_ctx, v_a_ctx, sink_a)
    yc_b = _gqa_core(_rms_norm(_heads(b_qc, B_HEADS), q_norm_b), k_b_ctx, v_b_ctx)
    qc_nope, qc_rope = mla_q(c_qc)
    yc_c = _mla_core(qc_nope, qc_rope, k_c_nope_ctx, c_krc, v_c_ctx)
    yc = jnp.concatenate([_merge(yc_a), _merge(yc_b), _merge(yc_c)], axis=-1) @ w_out
    return y, yc


def _sq_relu_mlp(h, w1, w2):
    a = jnp.maximum(h @ w1, 0)
    return (a * a) @ w2


def setup_inputs(seed: int = 0) -> dict:
    key = jax.random.key(seed)
    ks = iter(jax.random.split(key, 32))
    nrm = lambda shape, scale: jax.random.normal(next(ks), shape, F32) * scale
    gain = lambda shape: 1.0 + nrm(shape, 0.02)
    L = DEPTH
    return {
        "x": nrm((BATCH, SEQ, D_MODEL), 1.0),
        "c": nrm((BATCH, D_MODEL), 1.0),
        "ctx": nrm((BATCH, CTX_LEN, D_MODEL), 1.0),
        "c_ctx": nrm((D_MODEL,), 1.0),
        "w_mod": nrm((L, D_MODEL, 6 * D_MODEL), 0.5 * D_MODEL ** -0.5),
        "b_mod": nrm((L, 6 * D_MODEL), 0.01),
        "w_in": nrm((L, D_MODEL, D_IN), D_MODEL ** -0.5),
        "sink_a": nrm((L, A_HEADS), 0.5),
        "q_norm_b": gain((L, HEAD_DIM)),
        "k_norm_b": gain((L, HEAD_DIM)),
        "mla_q_norm": gain((L, C_Q_RANK)),
        "mla_kv_norm": gain((L, C_KV_RANK)),
        "w_uq": nrm((L, C_Q_RANK, C_HEADS * (C_NOPE_DIM + C_ROPE_DIM)), C_Q_RANK ** -0.5),
        "w_uk": nrm((L, C_KV_RANK, C_HEADS * C_NOPE_DIM), C_KV_RANK ** -0.5),
        "w_uv": nrm((L, C_KV_RANK, C_HEADS * C_V_DIM), C_KV_RANK ** -0.5),
        "w_out": nrm((L, D_MIX, D_MODEL), DEEPNORM_BETA * D_MIX ** -0.5),
        "ln1_g": gain((L, D_MODEL)),
        "ln1_b": nrm((L, D_MODEL), 0.02),
        "w_fc1": nrm((L, D_MODEL, D_FF), D_MODEL ** -0.5),
        "w_fc2": nrm((L, D_FF, D_MODEL), DEEPNORM_BETA * D_FF ** -0.5),
        "ln2_g": gain((L, D_MODEL)),
        "ln2_b": nrm((L, D_MODEL), 0.02),
    }


def reference(x, c, ctx, c_ctx, w_mod, b_mod, w_in, sink_a, q_norm_b, k_norm_b, mla_q_norm, mla_kv_norm,
              w_uq, w_uk, w_uv, w_out, ln1_g, ln1_b, w_fc1, w_fc2, ln2_g, ln2_b):
    n_rows = x.shape[1] // GRID_W
    rope64 = _axial_rope_tables(n_rows, HEAD_DIM)
    rope32 = _axial_rope_tables(n_rows, C_ROPE_DIM)
    xc = ctx
    for l in range(DEPTH):
        last = l == DEPTH - 1
        mod = jax.nn.silu(c) @ w_mod[l] + b_mod[l]
        mod_c = jax.nn.silu(c_ctx) @ w_mod[l] + b_mod[l]
        sh1, sc1, g1, sh2, sc2, g2 = jnp.split(mod[:, None, :], 6, axis=-1)
        sh1c, sc1c, g1c, sh2c, sc2c, g2c = jnp.split(mod_c, 6, axis=-1)

        h = x * (1 + sc1) + sh1
        hc = xc * (1 + sc1c) + sh1c
        y, yc = _token_mixers(h, hc, w_in[l], sink_a[l], q_norm_b[l], k_norm_b[l], mla_q_norm[l], mla_kv_norm[l],
                              w_uq[l], w_uk[l], w_uv[l], w_out[l], rope64, rope32, not last)
        x = _layer_norm(DEEPNORM_ALPHA * x + g1 * y, ln1_g[l], ln1_b[l])

        h = x * (1 + sc2) + sh2
        x = _layer_norm(DEEPNORM_ALPHA * x + g2 * _sq_relu_mlp(h, w_fc1[l], w_fc2[l]), ln2_g[l], ln2_b[l])

        if not last:
            xc = _layer_norm(DEEPNORM_ALPHA * xc + g1c * yc, ln1_g[l], ln1_b[l])
            hc = xc * (1 + sc2c) + sh2c
            xc = _layer_norm(DEEPNORM_ALPHA * xc + g2c * _sq_relu_mlp(hc, w_fc1[l], w_fc2[l]), ln2_g[l], ln2_b[l])
    return x
```

```python
import numpy as np
import ml_dtypes
from contextlib import ExitStack
import concourse.bass as bass
import concourse.mybir as mybir
from concourse.bass_utils import run_bass_kernel_spmd

F32 = mybir.dt.float32
BF16 = mybir.dt.bfloat16
AF = mybir.ActivationFunctionType
ALU = mybir.AluOpType

L = 2
D = 1024
NL = 2048
NCX = 64
NT = NL + NCX
ALPHA = float((2 * L) ** 0.25)
EPS = 1e-6
EPOCH = 8000
NWIN = 15 * 128 + 64 + 256
GOFF = {g: g * 128 for g in range(15)}
KR_OFF = 1920
KRS_OFF = 1952
V_OFF = 1984
CHUNKS = [(0, 512), (512, 512), (1024, 512), (1536, 512), (2048, 64)]
KVROWS = 480
VREG = 288
NE = 320
DEBUG = {}


class _Stop(Exception):
    pass


def check_stop(name, l=0):
    return DEBUG.get("stop") == name and l == DEBUG.get("stop_layer", 0)


class Res:
    __slots__ = ("name", "w", "r", "dsem", "dcnt")

    def __init__(self, name=""):
        self.name = name
        self.w = None
        self.r = []
        self.dsem = None
        self.dcnt = 0


class Eng:
    def __init__(self, K, handle, name):
        self.K = K
        self.h = handle
        self.name = name
        self.sems = []
        self.count = 0
        self.waited = {}
        self.last = None

    def next_token(self):
        e = self.count // EPOCH
        while len(self.sems) <= e:
            self.sems.append(self.K.new_sem(f"{self.name}_e{len(self.sems)}"))
        v = self.count % EPOCH + 1
        self.count += 1
        self.last = (self.sems[e], v, self)
        return self.last

    def wait(self, tok):
        sem, val = tok[0], tok[1]
        key = id(sem)
        if self.waited.get(key, 0) >= val:
            return
        self.h.wait_ge(sem, val)
        self.waited[key] = val


class K:
    def __init__(self, nc, stack):
        self.nc = nc
        self.stack = stack
        self.nsem = 0
        self.pe = Eng(self, nc.tensor, "pe")
        self.act = Eng(self, nc.scalar, "act")
        self.dve = Eng(self, nc.vector, "dve")
        self.pool = Eng(self, nc.gpsimd, "pool")
        self.sp = Eng(self, nc.sync, "sp")
        self.engs = [self.pe, self.act, self.dve, self.pool, self.sp]
        self.dma_toks = {}
        self.extra_toks = []
        self.uid = 0

    def new_sem(self, name):
        self.nsem += 1
        return self.stack.enter_context(self.nc.semaphore(name))

    def _deps(self, eng, reads, writes):
        for r in reads:
            if r.w is not None:
                eng.wait(r.w)
        for w in writes:
            if w.w is not None:
                eng.wait(w.w)
            for t in w.r:
                eng.wait(t)

    def op(self, eng, build, reads=(), writes=()):
        self._deps(eng, reads, writes)
        inst = build()
        tok = eng.next_token()
        inst.then_inc(tok[0], 1)
        for r in reads:
            r.r.append(tok)
        for w in writes:
            w.w = tok
            w.r = []
        return tok

    def dma(self, q, pairs, reads=(), writes=(), sem_res=None):
        self._deps(q, reads, writes)
        sr = sem_res if sem_res is not None else writes[0]
        if sr.dsem is None:
            self.uid += 1
            sr.dsem = self.new_sem(f"d{self.uid}")
        if sr.dcnt > 0:
            q.wait((sr.dsem, sr.dcnt, None))
        for (o, i) in pairs:
            q.h.dma_start(out=o, in_=i).then_inc(sr.dsem, 16)
            sr.dcnt += 16
        tok = (sr.dsem, sr.dcnt, None)
        self.dma_toks[id(sr.dsem)] = tok
        for r in reads:
            r.r.append(tok)
        for w in writes:
            w.w = tok
            w.r = []
        return tok

    def wait_all_dma(self, eng):
        for t in self.dma_toks.values():
            eng.wait(t)

    def barrier(self):
        toks = [e.last for e in self.engs if e.last is not None]
        for e in self.engs:
            for t in toks:
                if e is not self.sp or t[2] is not e:
                    e.wait(t)
            for t in self.dma_toks.values():
                e.wait(t)
            for t in self.extra_toks:
                e.wait(t)


def build_program():
    nc = bass.Bass("TRN2", target_bir_lowering=False)

    def din(name, shape, dt=F32):
        return nc.dram_tensor(name, shape, dt, kind="ExternalInput").ap()

    x_in = din("x", [NL, D])
    ctx_in = din("ctx", [NCX, D])
    cvec = din("cvec", [16, 128])
    w_mod = din("w_mod", [L, D, 12 * 128])
    b_mod = din("b_mod", [L * 12, 128])
    w_inx = din("w_inx", [L, D, NWIN])
    w_uqx = din("w_uqx", [L, 256, 1536])
    w_uk = din("w_uk", [L, 128, 512])
    w_uv = din("w_uv", [L, 128, 512])
    w_outx = din("w_outx", [L, D, D])
    w_fc1 = din("w_fc1", [L, 8, 128, 8, 512])
    w_fc2 = din("w_fc2", [L, 8, 128, 32, 128])
    vecs = din("vecs", [L, 40, 128])
    sink = din("sink", [L, 1, 4])
    ropes = din("ropes", [4, 128, NT])
    amask = din("amask", [128, 10, 128], BF16)
    y_out = nc.dram_tensor("y", [NL, D], F32, kind="ExternalOutput").ap()
    QS = nc.dram_tensor("qs", [4 * 128 + 8 * 96, NT], BF16).ap()
    KVX = [nc.dram_tensor(f"kvx{c}", [KVROWS, Tc], BF16).ap() for c, (_, Tc) in enumerate(CHUNKS)]
    KVG = [nc.dram_tensor(f"kvg{c}", [4 * KVROWS, Tc], BF16).ap() for c, (_, Tc) in enumerate(CHUNKS)]
    XROW = {"kB": 0, "ckv": 128, "kr": 256}
    KAL = nc.dram_tensor("kal", [128, NT], BF16).ap()
    VAL = nc.dram_tensor("val", [NT, 192], BF16).ap()
    AEX = nc.dram_tensor("aex", [128 + 192, NE], BF16).ap()
    AEG = nc.dram_tensor("aeg", [4 * (128 + 192), NE], BF16).ap()
    MEX = nc.dram_tensor("mex", [128, L * 24], F32).ap()
    MEG = nc.dram_tensor("meg", [4 * 128, L * 24], F32).ap()

    def chunk_of(tok):
        return 4 if tok >= NL else tok // 512

    def dap(t, off, dims):
        return bass.AP(t.tensor, off, [list(d) for d in dims])

    with ExitStack() as stack:
        k = K(nc, stack)
        pe, act, dve, pool, sp = k.pe, k.act, k.dve, k.pool, k.sp
        V, S, G, T = nc.vector, nc.scalar, nc.gpsimd, nc.tensor

        uidc = [0]

        def sbt(st, name, shape, dt):
            uidc[0] += 1
            return st.enter_context(nc.sbuf_tensor(f"{name}_{uidc[0]}", shape, dt))

        XS = sbt(stack, "XS", [128, 8, NT], F32)
        ident = sbt(stack, "ident", [128, 128], F32)
        onesm = sbt(stack, "onesm", [128, 128], F32)
        ones1 = sbt(stack, "ones1", [128, 128], F32)
        bd64 = sbt(stack, "bd64", [128, 128], F32)
        epsc = sbt(stack, "epsc", [128, 1], F32)
        SL = sbt(stack, "SL", [128, 8, 2], BF16)
        VEC = sbt(stack, "VEC", [128, 40], F32)
        MODL = sbt(stack, "MODL", [128, 48], F32)
        MODC = sbt(stack, "MODC", [128, 48], F32)
        DER = sbt(stack, "DER", [128, 64], F32)
        ESROW = sbt(stack, "ESROW", [1, 4, 128], F32)
        SINKL = sbt(stack, "SINKL", [1, 192], F32)
        MG = sbt(stack, "MG", [128, 4, L * 24], F32)
        PP = [stack.enter_context(nc.psum_tensor(f"pp{i}", [128, 1024], F32)) for i in range(4)]

        def bank(i):
            return PP[i // 2][:, (i % 2) * 512:(i % 2) * 512 + 512]

        BR = [Res(f"bank{i}") for i in range(8)]
        rr = [0]

        def alt(*engs):
            rr[0] += 1
            return engs[rr[0] % len(engs)]

        def ew_tt(eng, out, in0, in1, op, reads=(), writes=()):
            h = eng.h
            return k.op(eng, lambda: h.tensor_tensor(out, in0, in1, op=op), reads, writes)

        def ew_ts(eng, out, in0, s1, s2, op0, op1, reads=(), writes=()):
            h = eng.h
            if s2 is None:
                return k.op(eng, lambda: h.tensor_scalar(out, in0, s1, None, op0), reads, writes)
            return k.op(eng, lambda: h.tensor_scalar(out, in0, s1, s2, op0, op1), reads, writes)

        def mm_group(out, pairs, reads, writes):
            n = len(pairs)

            def b():
                inst = None
                for i, (a, r) in enumerate(pairs):
                    inst = T.matmul(out, a, r, start=(i == 0), stop=(i == n - 1))
                return inst
            return k.op(pe, b, reads, writes)

        with nc.Block() as block:
            r_const = Res("const")
            k.op(pool, lambda: G.memset(ident[:], 0.0), writes=[r_const])
            k.op(pool, lambda: G.affine_select(ident[:], ident[:], pattern=[[-1, 128]], compare_op=ALU.not_equal,
                                               fill=1.0, base=0, channel_multiplier=1), reads=[r_const], writes=[r_const])
            k.op(pool, lambda: G.memset(onesm[:], 1.0 / D), writes=[r_const])
            k.op(pool, lambda: G.memset(ones1[:], 1.0), writes=[r_const])
            k.op(pool, lambda: G.memset(bd64[:], 0.0), writes=[r_const])
            k.op(pool, lambda: G.memset(bd64[0:64, 0:64], 1.0), writes=[r_const])
            k.op(pool, lambda: G.memset(bd64[64:128, 64:128], 1.0), writes=[r_const])
            k.op(pool, lambda: G.memset(epsc[:], EPS), writes=[r_const])
            k.op(pool, lambda: G.memset(SINKL[:], 0.0), writes=[r_const])
            k.op(pool, lambda: G.memset(SINKL[:, 64:128], 1.0), writes=[r_const])

            with ExitStack() as ph:
                XST = [sbt(ph, f"xst{i}", [128, D], F32) for i in range(2)]
                CVS = sbt(ph, "cvs", [16, 128], F32)
                CT = sbt(ph, "ct", [128, 16], F32)
                CT2 = sbt(ph, "ct2", [128, 16], F32)
                rx = [Res("xst0"), Res("xst1")]
                rcv = Res("cvs")
                k.dma(sp, [(CVS[:], cvec)], writes=[rcv])
                WMS = [sbt(ph, f"wms{i}", [128, 8, 1536], BF16) for i in range(L)]
                BMS = sbt(ph, "bms", [L * 12, 128], F32)
                BMT = sbt(ph, "bmt", [128, L * 12], F32)
                MSH = sbt(ph, "msh", [128, L * 12, 2], F32)
                rwms = [Res(f"wms{i}") for i in range(L)]
                rbms = Res("bms")
                for ll in range(L):
                    wsrc = w_mod[ll].rearrange("(c p) n -> p c n", p=128)
                    k.dma(pool, [(WMS[ll][:, 0:4, :], wsrc[:, 0:4, :]), (WMS[ll][:, 4:8, :], wsrc[:, 4:8, :])], writes=[rwms[ll]])
                k.dma(sp, [(BMS[:], b_mod)], writes=[rbms])
                k.op(pe, lambda: T.transpose(bank(4)[:, 0:16], CVS[:], ident[0:16, 0:16]), reads=[rcv, r_const], writes=[BR[4]])
                rct = Res("ct")
                k.op(dve, lambda: V.tensor_copy(CT[:], bank(4)[:, 0:16]), reads=[BR[4]], writes=[rct])
                k.op(act, lambda: S.activation(CT2[:], CT[:], AF.Exp, scale=-1.0), reads=[rct], writes=[rct])
                ew_ts(dve, CT2[:], CT2[:], 1.0, None, ALU.add, None, reads=[rct], writes=[rct])
                k.op(dve, lambda: V.reciprocal(CT2[:], CT2[:]), reads=[rct], writes=[rct])
                ew_tt(dve, SL[:].rearrange("p a b -> p (a b)"), CT[:], CT2[:], ALU.mult, reads=[rct], writes=[rct])
                ccm = k.new_sem("ccm")
                rmex = Res("mex")
                def emit_mod_shard():
                    k.op(pe, lambda: T.transpose(bank(5)[:, 0:L * 12], BMS[:], ident[0:L * 12, 0:L * 12]), reads=[rbms, r_const], writes=[BR[5]])
                    rbmt = Res("bmt")
                    k.op(dve, lambda: V.tensor_copy(BMT[:], bank(5)[:, 0:L * 12]), reads=[BR[5]], writes=[rbmt])
                    MP = bank(6)
                    for ll in range(L):
                        for t_ in range(12):
                            col = (ll * 12 + t_) * 2
                            mm_group(MP[:, col:col + 2], [(WMS[ll][:, kc, t_ * 128:(t_ + 1) * 128], SL[:, kc, :]) for kc in range(8)],
                                     reads=[rwms[ll], rct], writes=[BR[6]])
                    rmsh = Res("msh")
                    ew_tt(dve, MSH[:], MP[:, 0:L * 24].rearrange("p (f v) -> p f v", v=2), BMT[:].unsqueeze(2).to_broadcast([128, L * 12, 2]), ALU.add,
                          reads=[BR[6], rbmt], writes=[rmsh])
                    k.dma(pool, [(MEX, MSH[:].rearrange("p f v -> p (f v)"))], reads=[rmsh], writes=[rmex])
                    k.wait_all_dma(pool)
                    if not DEBUG.get("nocc"):
                        G.collective_compute("AllGather", ALU.bypass, replica_groups=[[0, 1, 2, 3], [4, 5, 6, 7]],
                                             ins=[MEX.opt()], outs=[MEG.opt()]).then_inc(ccm)
                for t in range(17):
                    if t == 8:
                        emit_mod_shard()
                    rows = 128 if t < 16 else 64
                    src = x_in[t * 128:(t + 1) * 128, :] if t < 16 else ctx_in
                    buf = t % 2
                    k.dma(sp, [(XST[buf][0:rows, :], src)], writes=[rx[buf]])
                    for half in range(2):
                        bi = (t * 2 + half) % 4
                        bk = bank(bi)

                        def tr(bk=bk, buf=buf, half=half, rows=rows):
                            inst = None
                            for c in range(4):
                                inst = T.transpose(bk[:, c * 128:c * 128 + rows],
                                                   XST[buf][0:rows, (half * 4 + c) * 128:(half * 4 + c + 1) * 128],
                                                   ident[0:rows, 0:rows])
                            return inst
                        k.op(pe, tr, reads=[rx[buf], r_const], writes=[BR[bi]])
                        src_v = bk.rearrange("p (c t) -> p c t", c=4)[:, :, 0:rows]
                        dst_v = XS[:, half * 4:half * 4 + 4, t * 128:t * 128 + rows]
                        if (t + half) % 2 == 0:
                            k.op(act, lambda s=src_v, d=dst_v: S.activation(d, s, AF.Identity, scale=ALPHA), reads=[BR[bi]])
                        else:
                            ew_ts(dve, dst_v, src_v, ALPHA, None, ALU.mult, None, reads=[BR[bi]])
                rmg = Res("mg")
                if DEBUG.get("nocc"):
                    rmeg = Res("megdbg")
                    k.dma(sp, [(MEG[r_ * 128:(r_ + 1) * 128, :], MEX) for r_ in range(4)], reads=[rmex], writes=[rmeg])
                    k.dma(sp, [(MG[:], MEG.rearrange("(r p) n -> p r n", p=128))], reads=[rmeg], writes=[rmg])
                else:
                    sp.wait((ccm, 1, None))
                    k.dma(sp, [(MG[:], MEG.rearrange("(r p) n -> p r n", p=128))], writes=[rmg])
                k.barrier()

            def run_layers():
              for l in range(L):
                last = (l == L - 1)
                if check_stop('setup', l):
                    return
                ph1 = ExitStack()
                WIN = sbt(ph1, "win", [128, 8, NWIN], BF16)
                WUQ = sbt(ph1, "wuq", [128, 2, 1536], BF16)
                rwin, rwuq = Res("win"), Res("wuq")
                wsrc = w_inx[l].rearrange("(c p) n -> p c n", p=128)
                k.dma(pool, [(WIN[:, 2 * i:2 * i + 2, :], wsrc[:, 2 * i:2 * i + 2, :]) for i in range(4)], writes=[rwin])
                k.dma(pool, [(WUQ[:], w_uqx[l].rearrange("(c p) n -> p c n", p=128))], writes=[rwuq])
                with ExitStack() as ph:
                    VST = sbt(ph, "vst", [40, 128], F32)
                    SKS = sbt(ph, "sks", [1, 4], F32)
                    rv = Res("vst")
                    k.dma(sp, [(VST[:], vecs[l]), (SKS[:], sink[l])], writes=[rv])
                    k.op(pe, lambda: T.transpose(bank(5)[:, 0:40], VST[:], ident[0:40, 0:40]), reads=[rv, r_const], writes=[BR[5]])
                    k.op(dve, lambda: V.tensor_copy(VEC[:], bank(5)[:, 0:40]), reads=[BR[5]])
                    k.op(act, lambda: S.activation(SKS[:], SKS[:], AF.Exp), reads=[rv], writes=[rv])
                    k.op(dve, lambda: V.tensor_copy(ESROW[:], SKS[:].unsqueeze(2).to_broadcast([1, 4, 128])), reads=[rv])
                    for r_ in range(4):
                        for v_, dstt in ((0, MODL), (1, MODC)):
                            srcv = MG[:, r_, l * 24:(l + 1) * 24].rearrange("p (s j v) -> p s j v", s=6, j=2, v=2)[:, :, :, v_]
                            dstv = dstt[:].rearrange("p (s r j) -> p s r j", s=6, r=4, j=2)[:, :, r_, :]
                            k.op(dve, lambda srcv=srcv, dstv=dstv: V.tensor_copy(dstv, srcv), reads=[])
                    k.barrier()
                    ew_ts(dve, DER[:, 0:8], MODL[:, 8:16], 1.0, 1.0 / ALPHA, ALU.add, ALU.mult)
                    ew_ts(dve, DER[:, 8:16], MODC[:, 8:16], 1.0, 1.0 / ALPHA, ALU.add, ALU.mult)
                    ew_ts(dve, DER[:, 16:24], MODL[:, 32:40], 1.0, 1.0 / ALPHA, ALU.add, ALU.mult)
                    ew_ts(dve, DER[:, 24:32], MODC[:, 32:40], 1.0, 1.0 / ALPHA, ALU.add, ALU.mult)
                    ew_ts(dve, DER[:, 32:48], VEC[:, 0:16], ALPHA, None, ALU.mult, None)
                    s2 = 1.0 if last else ALPHA
                    ew_ts(dve, DER[:, 48:64], VEC[:, 16:32], s2, None, ALU.mult, None)
                    k.barrier()
                    if check_stop('mod', l):
                        return
                A1 = (DER[:, 0:8], DER[:, 8:16])
                A2 = (DER[:, 16:24], DER[:, 24:32])
                MOD = (MODL, MODC)

                with ExitStack() as ph:
                    HT = [sbt(ph, f"ht{i}", [128, 8, 512], BF16) for i in range(2)]
                    RT = [sbt(ph, f"rt{i}", [128, 4, 512], F32) for i in range(2)]
                    OST = [sbt(ph, f"ost{i}", [128, 512], BF16) for i in range(4)]
                    TMP = [sbt(ph, f"tmp{i}", [128, 512], F32) for i in range(6)]
                    CQN = sbt(ph, "cqn", [128, 2, 512], BF16)
                    VSTG = sbt(ph, "vstg", [128, 4, 384], BF16)
                    rht = [[Res(f"ht0_{c}") for c in range(8)], [Res(f"ht1_{c}") for c in range(8)]]
                    rrt = [Res("rt0"), Res("rt1")]
                    rost = [Res(f"ost{i}") for i in range(4)]
                    rtmp = [Res(f"tmp{i}") for i in range(6)]
                    rcqn, rvstg = Res("cqn"), Res("vstg")
                    rkvx = Res("kvx")
                    k.op(dve, lambda: V.memset(VSTG[:, :, 64:128], 1.0), writes=[rvstg])
                    k.op(dve, lambda: V.memset(VSTG[:, :, 256:320], 1.0), writes=[rvstg])
                    ost_i = [0]
                    tmp_i = [0]
                    ccs = {c_: k.new_sem(f"cc{l}_{c_}") for c_ in list(range(len(CHUNKS))) + ["edge"]}

                    def next_ost():
                        ost_i[0] = (ost_i[0] + 1) % 4
                        return ost_i[0]

                    def next_tmp():
                        tmp_i[0] = (tmp_i[0] + 1) % 6
                        return tmp_i[0]

                    gq = VEC[:, 35:36]; gqs = VEC[:, 36:37]; gk = VEC[:, 37:38]; gks = VEC[:, 38:39]
                    pbank = [0]

                    def proj(col, M, hb, Tn):
                        bi = pbank[0]
                        pbank[0] = (pbank[0] + 1) % 6
                        mm_group(bank(bi)[0:M, 0:Tn], [(WIN[:, kc, col:col + M], HT[hb][:, kc, 0:Tn]) for kc in range(8)],
                                 reads=[rwin] + rht[hb], writes=[BR[bi]])
                        return bi

                    def rstd_from(bi_list, ones_mat, scale, Tn, M=128):
                        sqs = []
                        for bi in bi_list:
                            ti = next_tmp()
                            k.op(act, lambda bi=bi, ti=ti: S.activation(TMP[ti][:, 0:Tn], bank(bi)[:, 0:Tn], AF.Square),
                                 reads=[BR[bi]], writes=[rtmp[ti]])
                            sqs.append(ti)
                        mm_group(bank(6)[:, 0:Tn], [(ones_mat, TMP[ti][:, 0:Tn]) for ti in sqs],
                                 reads=[rtmp[ti] for ti in sqs] + [r_const], writes=[BR[6]])
                        tr_ = next_tmp()
                        k.op(act, lambda: S.activation(TMP[tr_][:, 0:Tn], bank(6)[:, 0:Tn], AF.Ln, bias=epsc[:], scale=scale),
                             reads=[BR[6], r_const], writes=[rtmp[tr_]])
                        k.op(act, lambda: S.activation(TMP[tr_][:, 0:Tn], TMP[tr_][:, 0:Tn], AF.Exp, scale=-0.5),
                             reads=[rtmp[tr_]], writes=[rtmp[tr_]])
                        return tr_

                    def store(oi, M, Tn, dst_rows, c0):
                        if isinstance(dst_rows, str):
                            dst = KVX[chunk_of(c0)][XROW[dst_rows]:XROW[dst_rows] + M, 0:Tn]
                        else:
                            dst = dst_rows[:, c0:c0 + Tn]
                        k.dma(sp, [(dst, OST[oi][0:M, 0:Tn])], reads=[rost[oi]], writes=[], sem_res=rost[oi])

                    for pos_, ci in enumerate((3, 4, 0, 1, 2)):
                        c0, Tn = CHUNKS[ci]
                        hb = pos_ % 2
                        isctx = 1 if ci == 4 else 0
                        for c in range(8):
                            ew_ts(dve if c % 2 == 0 else pool, HT[hb][:, c, 0:Tn], XS[:, c, c0:c0 + Tn], A1[isctx][:, c:c + 1],
                                  MOD[isctx][:, c:c + 1], ALU.mult, ALU.add, writes=[rht[hb][c]])
                        k.dma(sp, [(RT[hb][:, :, 0:Tn], ropes[:, :, c0:c0 + Tn].rearrange("a p t -> p a t"))], writes=[rrt[hb]])
                        C64 = RT[hb][:, 0, 0:Tn]; S64 = RT[hb][:, 1, 0:Tn]
                        C32 = RT[hb][:, 2, 0:Tn]; S32 = RT[hb][:, 3, 0:Tn]
                        for gi, dst in ((0, QS[0:128, :]), (1, QS[128:256, :]), (2, KAL)):
                            b0 = proj(GOFF[gi], 128, hb, Tn)
                            b1 = proj(GOFF[gi + 3], 128, hb, Tn)
                            t1 = next_tmp(); t2 = next_tmp(); oi = next_ost()
                            ew_tt(dve, TMP[t1][:, 0:Tn], bank(b0)[:, 0:Tn], C64, ALU.mult, reads=[BR[b0], rrt[hb]], writes=[rtmp[t1]])
                            ew_tt(dve, TMP[t2][:, 0:Tn], bank(b1)[:, 0:Tn], S64, ALU.mult, reads=[BR[b1], rrt[hb]], writes=[rtmp[t2]])
                            ew_tt(pool, OST[oi][:, 0:Tn], TMP[t1][:, 0:Tn], TMP[t2][:, 0:Tn], ALU.add,
                                  reads=[rtmp[t1], rtmp[t2]], writes=[rost[oi]])
                            store(oi, 128, Tn, dst, c0)
                        for gi, dst, g_, gs_ in ((6, QS[256:384, :], gq, gqs), (7, QS[384:512, :], gq, gqs), (8, 'kB', gk, gks)):
                            b0 = proj(GOFF[gi], 128, hb, Tn)
                            b1 = proj(GOFF[gi + 3], 128, hb, Tn)
                            tr_ = rstd_from([b0], bd64[:], 1.0 / 64, Tn)
                            t1 = next_tmp(); t2 = next_tmp(); oi = next_ost()
                            k.op(dve, lambda: V.scalar_tensor_tensor(TMP[t1][:, 0:Tn], bank(b0)[:, 0:Tn], g_, TMP[tr_][:, 0:Tn], ALU.mult, ALU.mult),
                                 reads=[BR[b0], rtmp[tr_]], writes=[rtmp[t1]])
                            k.op(dve, lambda: V.scalar_tensor_tensor(TMP[t2][:, 0:Tn], bank(b1)[:, 0:Tn], gs_, TMP[tr_][:, 0:Tn], ALU.mult, ALU.mult),
                                 reads=[BR[b1], rtmp[tr_]], writes=[rtmp[t2]])
                            ew_tt(pool, TMP[t1][:, 0:Tn], TMP[t1][:, 0:Tn], C64, ALU.mult, reads=[rtmp[t1], rrt[hb]], writes=[rtmp[t1]])
                            ew_tt(pool, TMP[t2][:, 0:Tn], TMP[t2][:, 0:Tn], S64, ALU.mult, reads=[rtmp[t2], rrt[hb]], writes=[rtmp[t2]])
                            ew_tt(pool, OST[oi][:, 0:Tn], TMP[t1][:, 0:Tn], TMP[t2][:, 0:Tn], ALU.add,
                                  reads=[rtmp[t1], rtmp[t2]], writes=[rost[oi]])
                            store(oi, 128, Tn, dst, c0)
                        b0 = proj(GOFF[14], 128, hb, Tn)
                        tr_ = rstd_from([b0], ones1[:], 1.0 / 128, Tn)
                        oi = next_ost()
                        k.op(dve, lambda: V.scalar_tensor_tensor(OST[oi][:, 0:Tn], bank(b0)[:, 0:Tn], VEC[:, 34:35], TMP[tr_][:, 0:Tn], ALU.mult, ALU.mult),
                             reads=[BR[b0], rtmp[tr_]], writes=[rost[oi]])
                        store(oi, 128, Tn, 'ckv', c0)
                        b0 = proj(KR_OFF, 32, hb, Tn)
                        b1 = proj(KRS_OFF, 32, hb, Tn)
                        t1 = next_tmp(); t2 = next_tmp(); oi = next_ost()
                        ew_tt(dve, TMP[t1][0:32, 0:Tn], bank(b0)[0:32, 0:Tn], C32[0:32], ALU.mult, reads=[BR[b0], rrt[hb]], writes=[rtmp[t1]])
                        ew_tt(dve, TMP[t2][0:32, 0:Tn], bank(b1)[0:32, 0:Tn], S32[0:32], ALU.mult, reads=[BR[b1], rrt[hb]], writes=[rtmp[t2]])
                        ew_tt(pool, OST[oi][0:32, 0:Tn], TMP[t1][0:32, 0:Tn], TMP[t2][0:32, 0:Tn], ALU.add,
                              reads=[rtmp[t1], rtmp[t2]], writes=[rost[oi]])
                        store(oi, 32, Tn, 'kr', c0)
                        b0 = proj(GOFF[12], 128, hb, Tn)
                        b1 = proj(GOFF[13], 128, hb, Tn)
                        tr_ = rstd_from([b0, b1], ones1[:], 1.0 / 256, Tn)
                        k.op(dve, lambda: V.scalar_tensor_tensor(CQN[:, 0, 0:Tn], bank(b0)[:, 0:Tn], VEC[:, 32:33], TMP[tr_][:, 0:Tn], ALU.mult, ALU.mult),
                             reads=[BR[b0], rtmp[tr_]], writes=[rcqn])
                        k.op(dve, lambda: V.scalar_tensor_tensor(CQN[:, 1, 0:Tn], bank(b1)[:, 0:Tn], VEC[:, 33:34], TMP[tr_][:, 0:Tn], ALU.mult, ALU.mult),
                             reads=[BR[b1], rtmp[tr_]], writes=[rcqn])
                        for h in range(8):
                            bq = pbank[0]; pbank[0] = (pbank[0] + 1) % 6
                            bs = pbank[0]; pbank[0] = (pbank[0] + 1) % 6
                            mm_group(bank(bq)[0:96, 0:Tn], [(WUQ[:, kc, h * 96:(h + 1) * 96], CQN[:, kc, 0:Tn]) for kc in range(2)],
                                     reads=[rwuq, rcqn], writes=[BR[bq]])
                            mm_group(bank(bs)[0:96, 0:Tn], [(WUQ[:, kc, 768 + h * 96:768 + (h + 1) * 96], CQN[:, kc, 0:Tn]) for kc in range(2)],
                                     reads=[rwuq, rcqn], writes=[BR[bs]])
                            t1 = next_tmp(); t2 = next_tmp(); oi = next_ost()
                            k.op(act, lambda: S.copy(OST[oi][0:64, 0:Tn], bank(bq)[0:64, 0:Tn]), reads=[BR[bq]], writes=[rost[oi]])
                            ew_tt(dve, TMP[t1][64:96, 0:Tn], bank(bq)[64:96, 0:Tn], C32[64:96], ALU.mult, reads=[BR[bq], rrt[hb]], writes=[rtmp[t1]])
                            ew_tt(dve, TMP[t2][64:96, 0:Tn], bank(bs)[64:96, 0:Tn], S32[64:96], ALU.mult, reads=[BR[bs], rrt[hb]], writes=[rtmp[t2]])
                            ew_tt(pool, OST[oi][64:96, 0:Tn], TMP[t1][64:96, 0:Tn], TMP[t2][64:96, 0:Tn], ALU.add,
                                  reads=[rtmp[t1], rtmp[t2]], writes=[rost[oi]])
                            store(oi, 96, Tn, QS[512 + h * 96:512 + (h + 1) * 96, :], c0)
                        ntile = (Tn + 127) // 128
                        for tt in range(ntile):
                            rows = min(128, Tn - tt * 128)
                            mm_group(bank(7)[0:rows, 0:256], [(HT[hb][:, kc, tt * 128:tt * 128 + rows], WIN[:, kc, V_OFF:V_OFF + 256]) for kc in range(8)],
                                     reads=[rwin] + rht[hb], writes=[BR[7]])
                            for mx in range(2):
                                srcv = bank(7)[0:rows, mx * 128:(mx + 1) * 128].rearrange("p (a b) -> p a b", a=2)
                                dstv = VSTG[0:rows, tt, mx * 192:(mx + 1) * 192].rearrange("p (a b) -> p a b", a=3)[:, 0:3:2, :]
                                k.op(act, lambda s=srcv, d=dstv: S.copy(d, s), reads=[BR[7]], writes=[rvstg])
                        rows = min(128, Tn)
                        vdst = dap(KVX[ci], VREG * Tn, [[192, rows], [128 * 192, ntile], [1, 192]])
                        vadst = dap(VAL, c0 * 192, [[192, rows], [128 * 192, ntile], [1, 192]])
                        k.dma(sp, [(vdst, VSTG[0:rows, 0:ntile, 192:384]), (vadst, VSTG[0:rows, 0:ntile, 0:192])],
                              reads=[rvstg], writes=[], sem_res=rvstg)
                        k.wait_all_dma(pool)
                        if DEBUG.get("nocc"):
                            k.wait_all_dma(sp)
                            k.dma(sp, [(KVG[ci][r_ * KVROWS:(r_ + 1) * KVROWS, :], KVX[ci]) for r_ in range(4)], writes=[Res("kvgdbg")])
                        else:
                            G.collective_compute("AllGather", ALU.bypass, replica_groups=[[0, 1, 2, 3], [4, 5, 6, 7]],
                                                 ins=[KVX[ci].opt()], outs=[KVG[ci].opt()]).then_inc(ccs[ci])
                        if pos_ == 2:
                            k.wait_all_dma(sp)
                            redge = Res("aex")
                            k.dma(sp, [(AEX[0:128, 0:128], KAL[:, 0:128]), (AEX[0:128, 128:256], KAL[:, NL - 128:NL]), (AEX[0:128, 256:NE], KAL[:, NL:NT]),
                                       (dap(AEX, 128 * NE, [[192, 128], [1, 192]]), VAL[0:128, :]),
                                       (dap(AEX, 128 * NE + 128 * 192, [[192, 128], [1, 192]]), VAL[NL - 128:NL, :]),
                                       (dap(AEX, 128 * NE + 256 * 192, [[192, 64], [1, 192]]), VAL[NL:NT, :])], writes=[redge])
                            k.wait_all_dma(pool)
                            if DEBUG.get("nocc"):
                                k.wait_all_dma(sp)
                                k.dma(sp, [(AEG[r_ * 320:(r_ + 1) * 320, :], AEX) for r_ in range(4)], writes=[Res("aegdbg")])
                            else:
                                G.collective_compute("AllGather", ALU.bypass, replica_groups=[[0, 1, 2, 3], [4, 5, 6, 7]],
                                                     ins=[AEX.opt()], outs=[AEG.opt()]).then_inc(ccs["edge"])
                    k.extra_toks = [] if DEBUG.get('nocc') else [(ccs["edge"], 1, None)]
                    late_toks = [] if DEBUG.get('nocc') else [(ccs[c_], 1, None) for c_ in range(len(CHUNKS))]
                    k.barrier()
                    if check_stop('p1', l):
                        return
                ph1.close()

                with ExitStack() as pha:
                    ATT = sbt(pha, "att", [128, 8, NT], BF16)
                    PT = [sbt(pha, f"pt{i}", [128, 2, 512], BF16) for i in range(4)]
                    rpt = [Res(f"pt{i}") for i in range(4)]
                    QT = [sbt(pha, f"qt{i}", [128, 512], BF16) for i in range(2)]
                    rqt = [Res("qt0"), Res("qt1")]
                    RL = [sbt(pha, f"rl{i}", [128, 512], F32) for i in range(2)]
                    rrl = [Res("rl0"), Res("rl1")]
                    cnt = {"pt": 0, "q": 0, "s": 0, "o": 0, "rl": 0}

                    def kv_pairs(dst, dcol, r, grp, nrows, c0, n):
                        out = []
                        t = c0
                        while t < c0 + n:
                            ci = chunk_of(t)
                            cb, cT = CHUNKS[ci]
                            m = min(c0 + n, cb + cT) - t
                            row0 = r * KVROWS + XROW[grp]
                            out.append((dst[:, dcol + (t - c0):dcol + (t - c0) + m], KVG[ci][row0:row0 + nrows, t - cb:t - cb + m]))
                            t += m
                        return out

                    def v_src(r, tok0, ntok, col0, ncol, tiles=None, own=False):
                        ci = chunk_of(tok0)
                        cb, cT = CHUNKS[ci]
                        assert tok0 + (ntok if tiles is None else tiles * 128) <= cb + cT
                        buf = KVX[ci] if own else KVG[ci]
                        base = ((0 if own else r * KVROWS) + VREG) * cT + (tok0 - cb) * 192 + col0
                        if tiles is None:
                            return dap(buf, base, [[192, ntok], [1, ncol]])
                        return dap(buf, base, [[192, 128], [128 * 192, tiles], [1, ncol]])

                    def run_pipeline(items, front, back, depth=1):
                        pend = []
                        for it in items:
                            front(it)
                            pend.append(it)
                            if len(pend) > depth:
                                back(pend.pop(0))
                        for it in pend:
                            back(it)

                    def finalize(ob, e, chunk, c0, Tn):
                        ri = cnt["rl"] % 2
                        cnt["rl"] += 1
                        o0, l0 = (0, 64) if e == 0 else (64, 0)
                        k.op(dve, lambda: V.reciprocal(RL[ri][o0:o0 + 64, 0:Tn], bank(ob)[l0:l0 + 64, 0:Tn]), reads=[BR[ob]], writes=[rrl[ri]])
                        ew_tt(dve, ATT[o0:o0 + 64, chunk, c0:c0 + Tn], bank(ob)[o0:o0 + 64, 0:Tn], RL[ri][o0:o0 + 64, 0:Tn], ALU.mult,
                              reads=[BR[ob], rrl[ri]])

                    with ExitStack() as ph:
                        KA = sbt(ph, "ka", [128, NL], BF16)
                        KAC = sbt(ph, "kac", [128, 8, 128], BF16)
                        KAX = sbt(ph, "kax", [128, 256], BF16)
                        VA = sbt(ph, "va", [128, 16, 192], BF16)
                        VAC = sbt(ph, "vac", [128, 8, 192], BF16)
                        VAX = sbt(ph, "vax", [128, 2, 192], BF16)
                        MSK = sbt(ph, "msk", [128, 10, 128], BF16)
                        rka = Res("ka")
                        k.dma(sp, [(KA[:], KAL[:, 0:NL]),
                                   (VA[:], dap(VAL, 0, [[192, 128], [128 * 192, 16], [1, 192]])),
                                   (MSK[:], amask)], writes=[rka])
                        prs = []
                        for r in range(4):
                            eb = r * 320
                            vb_ = (eb + 128) * NE
                            prs.append((KAC[:, r, :], AEG[eb:eb + 128, 128:256]))
                            prs.append((KAC[:, 4 + r, :], AEG[eb:eb + 128, 0:128]))
                            prs.append((KAX[:, r * 64:(r + 1) * 64], AEG[eb:eb + 128, 256:NE]))
                            prs.append((VAC[:, r, :], dap(AEG, vb_ + 128 * 192, [[192, 128], [1, 192]])))
                            prs.append((VAC[:, 4 + r, :], dap(AEG, vb_, [[192, 128], [1, 192]])))
                            prs.append((VAX[(r % 2) * 64:(r % 2) * 64 + 64, r // 2, :], dap(AEG, vb_ + 256 * 192, [[192, 64], [1, 192]])))
                        k.dma(sp, prs, writes=[rka])
                        nblk = 16 if last else 17
                        itemsA = []
                        for pc in range(2):
                            for n in range(nblk):
                                c0 = n * 128
                                Tn = 128 if n < 16 else 64
                                tiles = []
                                if n < 16:
                                    if n == 0:
                                        for r in range(4):
                                            tiles.append((KAC[:, r, :], VAC[:, r, :], 2 + r))
                                    else:
                                        tiles.append((KA[:, (n - 1) * 128:n * 128], VA[:, n - 1, :], 0))
                                    tiles.append((KA[:, n * 128:(n + 1) * 128], VA[:, n, :], None))
                                    if n == 15:
                                        for r in range(4):
                                            tiles.append((KAC[:, 4 + r, :], VAC[:, 4 + r, :], 6 + r))
                                    else:
                                        tiles.append((KA[:, (n + 1) * 128:(n + 2) * 128], VA[:, n + 1, :], 1))
                                tiles.append((KAX[:, 0:128], VAX[:, 0, :], None))
                                tiles.append((KAX[:, 128:256], VAX[:, 1, :], None))
                                blk = {"pc": pc, "c0": c0, "Tn": Tn}
                                for ti, (kt, vt, mi) in enumerate(tiles):
                                    itemsA.append({"blk": blk, "kt": kt, "vt": vt, "mi": mi, "first": ti == 0, "last": ti == len(tiles) - 1})

                        def frontA(it):
                            blk = it["blk"]
                            Tn = blk["Tn"]
                            if it["first"]:
                                qi = cnt["q"] % 2
                                cnt["q"] += 1
                                blk["qi"] = qi
                                k.dma(sp, [(QT[qi][:, 0:Tn], QS[blk["pc"] * 128:(blk["pc"] + 1) * 128, blk["c0"]:blk["c0"] + Tn])], writes=[rqt[qi]])
                                blk["ob"] = [4 + (cnt["o"] % 2) * 2, 5 + (cnt["o"] % 2) * 2]
                                cnt["o"] += 1
                            qi = blk["qi"]
                            sp_ = cnt["s"] % 2
                            cnt["s"] += 1
                            s0, s1 = 2 * sp_, 2 * sp_ + 1
                            kt = it["kt"]

                            def qk():
                                T.matmul(bank(s0)[:, 0:Tn], kt[0:64, :], QT[qi][0:64, 0:Tn], start=True, stop=True)
                                return T.matmul(bank(s1)[:, 0:Tn], kt[64:128, :], QT[qi][64:128, 0:Tn], start=True, stop=True)
                            k.op(pe, qk, reads=[rka, rqt[qi]], writes=[BR[s0], BR[s1]])
                            pi = cnt["pt"] % 3
                            cnt["pt"] += 1
                            it["pi"] = pi
                            sv = PP[sp_][:].rearrange("p (e t) -> p e t", e=2)[:, :, 0:Tn]
                            pv_ = PT[pi][:, :, 0:Tn]
                            k.op(act, lambda: S.activation(pv_, sv, AF.Exp, scale=0.125), reads=[BR[s0], BR[s1]], writes=[rpt[pi]])
                            if it["mi"] is not None:
                                mk = MSK[:, it["mi"], 0:Tn].unsqueeze(1).to_broadcast([128, 2, Tn])
                                ew_tt(dve, pv_, pv_, mk, ALU.mult, reads=[rpt[pi], rka], writes=[rpt[pi]])

                        def backA(it):
                            blk = it["blk"]
                            Tn, ob, pi, vt = blk["Tn"], blk["ob"], it["pi"], it["vt"]
                            for e in range(2):
                                h = blk["pc"] + 2 * e

                                def pvm():
                                    inst = T.matmul(bank(ob[e])[:, 0:Tn], vt[:, e * 64:e * 64 + 128], PT[pi][:, e, 0:Tn], start=it["first"], stop=False)
                                    if it["last"]:
                                        sl = SINKL[0:1, 0:128] if e == 0 else SINKL[0:1, 64:192]
                                        inst = T.matmul(bank(ob[e])[:, 0:Tn], sl, ESROW[0:1, h, 0:Tn], start=False, stop=True)
                                    return inst
                                k.op(pe, pvm, reads=[rpt[pi], rka, r_const], writes=[BR[ob[e]]])
                            if it["last"]:
                                for e in range(2):
                                    finalize(ob[e], e, blk["pc"], blk["c0"], Tn)
                        run_pipeline(itemsA, frontA, backA)
                        k.extra_toks = k.extra_toks + late_toks
                        k.barrier()
                        if check_stop('a', l):
                            return

                    with ExitStack() as ph:
                        KB = sbt(ph, "kb", [128, 8448], BF16)
                        VB = sbt(ph, "vb", [128, 66, 192], BF16)
                        OSB = [sbt(ph, f"osb{i}", [128, 512], F32) for i in range(2)]
                        RLB = [sbt(ph, f"rlb{i}", [128, 512], F32) for i in range(2)]
                        rosb = [Res("osb0"), Res("osb1")]
                        rrlb = [Res("rlb0"), Res("rlb1")]
                        rkb = Res("kb")
                        prs = []
                        for r in range(4):
                            prs += kv_pairs(KB, r * NL, r, 'kB', 128, 0, NL)
                            prs += kv_pairs(KB, 8192 + r * 64, r, 'kB', 128, NL, 64)
                            for c_ in range(4):
                                prs.append((VB[:, r * 16 + c_ * 4:r * 16 + c_ * 4 + 4, :], v_src(r, c_ * 512, 128, 0, 192, tiles=4)))
                            prs.append((VB[(r % 2) * 64:(r % 2) * 64 + 64, 64 + r // 2, :], v_src(r, NL, 64, 0, 192)))
                        k.dma(sp, prs, writes=[rkb])
                        itemsB = []
                        blocksB = []
                        for pc in range(2):
                            for ci, (c0, Tn) in enumerate(CHUNKS):
                                if ci == 4 and last:
                                    continue
                                jl = list(range(66)) if ci < 4 else [64, 65]
                                blk = {"pc": pc, "c0": c0, "Tn": Tn, "bidx": len(blocksB)}
                                blocksB.append(blk)
                                for ji, j in enumerate(jl):
                                    itemsB.append({"blk": blk, "j": j, "first": ji == 0, "last": ji == len(jl) - 1})

                        def loadqB(blk):
                            qi = blk["bidx"] % 2
                            blk["qi"] = qi
                            Tn = blk["Tn"]
                            k.dma(sp, [(QT[qi][:, 0:Tn], QS[256 + blk["pc"] * 128:256 + (blk["pc"] + 1) * 128, blk["c0"]:blk["c0"] + Tn])], writes=[rqt[qi]])
                        loadqB(blocksB[0])

                        def frontB(it):
                            blk = it["blk"]
                            Tn, j = blk["Tn"], it["j"]
                            if it["first"] and blk["bidx"] + 1 < len(blocksB):
                                loadqB(blocksB[blk["bidx"] + 1])
                            qi = blk["qi"]
                            sp_ = cnt["s"] % 3
                            cnt["s"] += 1
                            s0, s1 = 2 * sp_, 2 * sp_ + 1

                            def qk():
                                T.matmul(bank(s0)[:, 0:Tn], KB[0:64, j * 128:(j + 1) * 128], QT[qi][0:64, 0:Tn], start=True, stop=True)
                                return T.matmul(bank(s1)[:, 0:Tn], KB[64:128, j * 128:(j + 1) * 128], QT[qi][64:128, 0:Tn], start=True, stop=True)
                            k.op(pe, qk, reads=[rkb, rqt[qi]], writes=[BR[s0], BR[s1]])
                            pi = cnt["pt"] % 4
                            cnt["pt"] += 1
                            it["pi"] = pi
                            sv = PP[sp_][:].rearrange("p (e t) -> p e t", e=2)[:, :, 0:Tn]
                            k.op(act, lambda: S.activation(PT[pi][:, :, 0:Tn], sv, AF.Exp, scale=0.125),
                                 reads=[BR[s0], BR[s1]], writes=[rpt[pi]])

                        def backB(it):
                            blk = it["blk"]
                            Tn, pi, j = blk["Tn"], it["pi"], it["j"]
                            ob = [6, 7]

                            def pvm():
                                T.matmul(bank(ob[0])[:, 0:Tn], VB[:, j, 0:128], PT[pi][:, 0, 0:Tn], start=it["first"], stop=it["last"])
                                return T.matmul(bank(ob[1])[:, 0:Tn], VB[:, j, 64:192], PT[pi][:, 1, 0:Tn], start=it["first"], stop=it["last"])
                            k.op(pe, pvm, reads=[rpt[pi], rkb], writes=[BR[ob[0]], BR[ob[1]]])
                            if it["last"]:
                                for e in range(2):
                                    k.op(dve, lambda e=e: V.tensor_copy(OSB[e][:, 0:Tn], bank(ob[e])[:, 0:Tn]), reads=[BR[ob[e]]], writes=[rosb[e]])
                                for e in range(2):
                                    o0, l0 = (0, 64) if e == 0 else (64, 0)
                                    k.op(dve, lambda e=e, o0=o0, l0=l0: V.reciprocal(RLB[e][o0:o0 + 64, 0:Tn], OSB[e][l0:l0 + 64, 0:Tn]),
                                         reads=[rosb[e]], writes=[rrlb[e]])
                                    ew_tt(dve, ATT[o0:o0 + 64, 2 + blk["pc"], blk["c0"]:blk["c0"] + Tn], OSB[e][o0:o0 + 64, 0:Tn],
                                          RLB[e][o0:o0 + 64, 0:Tn], ALU.mult, reads=[rosb[e], rrlb[e]])
                        run_pipeline(itemsB, frontB, backB, depth=2)
                        k.barrier()
                        if check_stop('b', l):
                            return

                    with ExitStack() as ph:
                        CKV = sbt(ph, "ckv", [128, 8448], BF16)
                        KC = [sbt(ph, f"kc{i}", [96, 8448], BF16) for i in range(2)]
                        VC = [sbt(ph, f"vc{i}", [128, 66, 128], BF16) for i in range(2)]
                        WUK = sbt(ph, "wuk", [128, 512], BF16)
                        WUV = sbt(ph, "wuv", [128, 512], BF16)
                        rckv, rw = Res("ckv"), Res("wukv")
                        rkc = [Res("kc0"), Res("kc1")]
                        rvc = [Res("vc0"), Res("vc1")]
                        k.dma(pool, [(WUK[:], w_uk[l]), (WUV[:], w_uv[l])], writes=[rw])
                        prs = []
                        for r in range(4):
                            prs += kv_pairs(CKV, r * NL, r, 'ckv', 128, 0, NL)
                            prs += kv_pairs(CKV, 8192 + r * 64, r, 'ckv', 128, NL, 64)
                        k.dma(sp, prs, writes=[rckv])
                        for i in range(2):
                            prs = []
                            for r in range(4):
                                prs += kv_pairs(KC[i][64:96, :], r * NL, r, 'kr', 32, 0, NL)
                                prs += kv_pairs(KC[i][64:96, :], 8192 + r * 64, r, 'kr', 32, NL, 64)
                            k.dma(sp, prs, writes=[rkc[i]])
                        k.op(pool, lambda: G.memset(VC[0][:, :, 64:128], 1.0), writes=[rvc[0]])
                        k.op(pool, lambda: G.memset(VC[1][:, :, 0:64], 1.0), writes=[rvc[1]])

                        def prep_pieces(h):
                            nb = h % 2
                            voff = 0 if nb == 0 else 64
                            pcs = []
                            for kc0 in range(0, 8448, 512):
                                n = min(512, 8448 - kc0)

                                def pk(bi, kc0=kc0, n=n):
                                    mm_group(bank(bi)[0:64, 0:n], [(WUK[:, h * 64:(h + 1) * 64], CKV[:, kc0:kc0 + n])], reads=[rckv, rw], writes=[BR[bi]])
                                    k.op(dve, lambda: V.tensor_copy(KC[nb][0:64, kc0:kc0 + n], bank(bi)[0:64, 0:n]), reads=[BR[bi]], writes=[rkc[nb]])
                                pcs.append(pk)
                            for j0 in range(0, 66, 8):
                                nj = min(8, 66 - j0)

                                def pv(bi, j0=j0, nj=nj):
                                    def vmm():
                                        inst = None
                                        for a in range(nj):
                                            inst = T.matmul(bank(bi)[:, a * 64:(a + 1) * 64], CKV[:, (j0 + a) * 128:(j0 + a + 1) * 128],
                                                            WUV[:, h * 64:(h + 1) * 64], start=True, stop=True)
                                        return inst
                                    k.op(pe, vmm, reads=[rckv, rw], writes=[BR[bi]])
                                    srcv = bank(bi)[:, 0:nj * 64].rearrange("p (a c) -> p a c", c=64)
                                    dstv = VC[nb][:, j0:j0 + nj, voff:voff + 64]
                                    k.op(dve, lambda: V.tensor_copy(dstv, srcv), reads=[BR[bi]], writes=[rvc[nb]])
                                pcs.append(pv)
                            return pcs

                        for pi_, pc_ in enumerate(prep_pieces(0)):
                            pc_(6 + pi_ % 2)
                        itemsC = []
                        blocksC = []
                        for h in range(8):
                            hitems = []
                            for ci, (c0, Tn) in enumerate(CHUNKS):
                                if ci == 4 and last:
                                    continue
                                jl = list(range(0, 66, 2)) if ci < 4 else [64]
                                blk = {"h": h, "c0": c0, "Tn": Tn, "bidx": len(blocksC)}
                                blocksC.append(blk)
                                for ji, j in enumerate(jl):
                                    hitems.append({"blk": blk, "j": j, "ji": ji, "first": ji == 0, "last": ji == len(jl) - 1, "prep": None})
                            if h < 7:
                                pcs = prep_pieces(h + 1)
                                cand = [it for it in hitems if it["ji"] >= 4 and not it["last"]]
                                step = max(1, len(cand) // len(pcs))
                                for pi_, pc_ in enumerate(pcs):
                                    cand[min(pi_ * step, len(cand) - 1 - (len(pcs) - 1 - pi_))]["prep"] = pc_
                            itemsC.extend(hitems)

                        def loadqC(blk):
                            qi = blk["bidx"] % 2
                            blk["qi"] = qi
                            Tn, h = blk["Tn"], blk["h"]
                            k.dma(sp, [(QT[qi][0:96, 0:Tn], QS[512 + h * 96:512 + (h + 1) * 96, blk["c0"]:blk["c0"] + Tn])], writes=[rqt[qi]])
                        loadqC(blocksC[0])

                        def frontC(it):
                            blk = it["blk"]
                            Tn, j, h = blk["Tn"], it["j"], blk["h"]
                            kb_ = h % 2
                            if it["first"] and blk["bidx"] + 1 < len(blocksC):
                                loadqC(blocksC[blk["bidx"] + 1])
                            if it["prep"] is not None:
                                it["prep"](6 + (blk["bidx"] + 1) % 2)
                            qi = blk["qi"]
                            sp_ = cnt["s"] % 3
                            cnt["s"] += 1
                            s0, s1 = 2 * sp_, 2 * sp_ + 1

                            def qk():
                                T.matmul(bank(s0)[:, 0:Tn], KC[kb_][:, j * 128:(j + 1) * 128], QT[qi][0:96, 0:Tn], start=True, stop=True)
                                return T.matmul(bank(s1)[:, 0:Tn], KC[kb_][:, (j + 1) * 128:(j + 2) * 128], QT[qi][0:96, 0:Tn], start=True, stop=True)
                            k.op(pe, qk, reads=[rkc[kb_], rqt[qi]], writes=[BR[s0], BR[s1]])
                            pi = cnt["pt"] % 4
                            cnt["pt"] += 1
                            it["pi"] = pi
                            sv = PP[sp_][:].rearrange("p (e t) -> p e t", e=2)[:, :, 0:Tn]
                            k.op(act, lambda: S.activation(PT[pi][:, :, 0:Tn], sv, AF.Exp, scale=float(96 ** -0.5)),
                                 reads=[BR[s0], BR[s1]], writes=[rpt[pi]])

                        def backC(it):
                            blk = it["blk"]
                            Tn, pi, j, h = blk["Tn"], it["pi"], it["j"], blk["h"]
                            ob = 6 + blk["bidx"] % 2
                            nb = h % 2

                            def pvm():
                                T.matmul(bank(ob)[:, 0:Tn], VC[nb][:, j, :], PT[pi][:, 0, 0:Tn], start=it["first"], stop=False)
                                return T.matmul(bank(ob)[:, 0:Tn], VC[nb][:, j + 1, :], PT[pi][:, 1, 0:Tn], start=False, stop=it["last"])
                            k.op(pe, pvm, reads=[rpt[pi], rvc[nb]], writes=[BR[ob]])
                            if it["last"]:
                                finalize(ob, nb, 4 + h // 2, blk["c0"], Tn)
                        run_pipeline(itemsC, frontC, backC, depth=2)
                        k.barrier()
                        if check_stop('c', l):
                            return

                    with ExitStack() as ph:
                        WOUT = sbt(ph, "wout", [128, 8, D], BF16)
                        SQ2 = [sbt(ph, f"sq{i}", [128, 8, 512], F32) for i in range(2)]
                        RS2 = [sbt(ph, f"rs{i}", [128, 512], F32) for i in range(2)]
                        lnr2 = [([Res(f"sq{i}") for i in range(8)], Res("rs")) for _ in range(2)]
                        rwo = Res("wout")
                        k.dma(pool, [(WOUT[:, 0:4, :], w_outx[l].rearrange("(c p) n -> p c n", p=128)[:, 0:4, :]),
                                     (WOUT[:, 4:8, :], w_outx[l].rearrange("(c p) n -> p c n", p=128)[:, 4:8, :])], writes=[rwo])
                        pend = None
                        for ci, (c0, Tn) in enumerate(CHUNKS):
                            if ci == 4 and last:
                                continue
                            isctx = 1 if ci == 4 else 0
                            rz = [Res(f"z{c}") for c in range(8)]
                            for oc in range(8):
                                bi = oc % 4
                                mm_group(bank(bi)[:, 0:Tn], [(WOUT[:, ic, oc * 128:(oc + 1) * 128], ATT[:, ic, c0:c0 + Tn]) for ic in range(8)],
                                         reads=[rwo], writes=[BR[bi]])
                                k.op(dve, lambda oc=oc, bi=bi: V.scalar_tensor_tensor(XS[:, oc, c0:c0 + Tn], bank(bi)[:, 0:Tn],
                                                                                       MOD[isctx][:, 16 + oc:17 + oc], XS[:, oc, c0:c0 + Tn], ALU.mult, ALU.add),
                                     reads=[BR[bi]], writes=[rz[oc]])
                            pp_ = ci % 2
                            largs = (k, nc, bank, BR, XS, onesm, epsc, r_const, c0, Tn, rz, SQ2[pp_], RS2[pp_], DER[:, 32:40], DER[:, 40:48], lnr2[pp_])
                            lkw = dict(mb=6 - 2 * pp_, vb=7 - 2 * pp_)
                            _ln_stage1(*largs, **lkw)
                            if pend is not None:
                                _ln_stage2(*pend[0], **pend[1])
                            pend = (largs, lkw)
                        if pend is not None:
                            _ln_stage2(*pend[0], **pend[1])
                        k.barrier()
                        if check_stop('p3', l):
                            return

                sblocks = [[CHUNKS[0], CHUNKS[1]], [CHUNKS[2], CHUNKS[3]] + ([] if last else [CHUNKS[4]])]
                with ExitStack() as p4:
                    W1S = [sbt(p4, f"w1s{i}", [128, 8, 512], BF16) for i in range(2)]
                    W2S = [sbt(p4, f"w2s{i}", [128, 32, 128], BF16) for i in range(2)]
                    rw1 = [Res("w1s0"), Res("w1s1")]
                    rw2 = [Res("w2s0"), Res("w2s1")]
                    nsb = len(sblocks)

                    def load_w1(idx):
                        if idx >= nsb * 8:
                            return
                        fs, wb = idx % 8, idx % 2
                        k.dma(pool, [(W1S[wb][:, 0:4, :], w_fc1[l, fs][:, 0:4, :]), (W1S[wb][:, 4:8, :], w_fc1[l, fs][:, 4:8, :])], writes=[rw1[wb]])

                    def load_w2(idx):
                        if idx >= nsb * 8:
                            return
                        oc, wb = idx % 8, idx % 2
                        k.dma(pool, [(W2S[wb][:, 0:16, :], w_fc2[l, oc][:, 0:16, :]), (W2S[wb][:, 16:32, :], w_fc2[l, oc][:, 16:32, :])], writes=[rw2[wb]])
                    load_w1(0)
                    load_w1(1)
                    load_w2(0)
                    load_w2(1)
                    for sbi, sbk_ in enumerate(sblocks):
                        base = sbk_[0][0]
                        rzs = {}
                        with ExitStack() as ph:
                            H2 = sbt(ph, "h2", [128, 8, 1088], BF16)
                            HID = sbt(ph, "hid", [128, 32, 1088], BF16)
                            RLU = [sbt(ph, f"rlu{i}", [128, 512], F32) for i in range(2)]
                            rh2 = [Res(f"h2_{c}") for c in range(8)]
                            rhid = [Res(f"hid{c}") for c in range(32)]
                            rrlu = [Res("rlu0"), Res("rlu1")]
                            for (c0, Tn) in sbk_:
                                isctx = 1 if c0 == NL else 0
                                for c in range(8):
                                    ew_ts(dve if c % 2 == 0 else pool, H2[:, c, c0 - base:c0 - base + Tn], XS[:, c, c0:c0 + Tn], A2[isctx][:, c:c + 1],
                                          MOD[isctx][:, 24 + c:25 + c], ALU.mult, ALU.add, writes=[rh2[c]])
                            bi_ = 0
                            ru = 0
                            for fs in range(8):
                                idx = sbi * 8 + fs
                                wb = idx % 2
                                for fj in range(4):
                                    fc = fs * 4 + fj
                                    for (c0, Tn) in sbk_:
                                        o0 = c0 - base
                                        bi = bi_ % 4
                                        bi_ += 1
                                        mm_group(bank(bi)[:, 0:Tn], [(W1S[wb][:, kc, fj * 128:(fj + 1) * 128], H2[:, kc, o0:o0 + Tn]) for kc in range(8)],
                                                 reads=[rw1[wb]] + rh2, writes=[BR[bi]])
                                        ri = ru % 2
                                        ru += 1
                                        k.op(act, lambda bi=bi, ri=ri, Tn=Tn: S.activation(RLU[ri][:, 0:Tn], bank(bi)[:, 0:Tn], AF.Relu),
                                             reads=[BR[bi]], writes=[rrlu[ri]])
                                        ew_tt(dve if fc % 2 == 0 else pool, HID[:, fc, o0:o0 + Tn], RLU[ri][:, 0:Tn], RLU[ri][:, 0:Tn], ALU.mult,
                                              reads=[rrlu[ri]], writes=[rhid[fc]])
                                load_w1(idx + 2)
                            for oc in range(8):
                                idx = sbi * 8 + oc
                                wb = idx % 2
                                for (c0, Tn) in sbk_:
                                    isctx = 1 if c0 == NL else 0
                                    o0 = c0 - base
                                    bi = 4 + bi_ % 4
                                    bi_ += 1
                                    mm_group(bank(bi)[:, 0:Tn], [(W2S[wb][:, fc, :], HID[:, fc, o0:o0 + Tn]) for fc in range(32)],
                                             reads=[rw2[wb]] + rhid, writes=[BR[bi]])
                                    rzs[(c0, oc)] = Res("z")
                                    k.op(dve, lambda oc=oc, bi=bi, c0=c0, Tn=Tn, isctx=isctx: V.scalar_tensor_tensor(
                                        XS[:, oc, c0:c0 + Tn], bank(bi)[:, 0:Tn], MOD[isctx][:, 40 + oc:41 + oc], XS[:, oc, c0:c0 + Tn], ALU.mult, ALU.add),
                                        reads=[BR[bi]], writes=[rzs[(c0, oc)]])
                                load_w2(idx + 2)
                            k.barrier()
                        with ExitStack() as ph:
                            SQ2 = [sbt(ph, f"sq2{i}", [128, 8, 512], F32) for i in range(2)]
                            RS2 = [sbt(ph, f"rs2{i}", [128, 512], F32) for i in range(2)]
                            lnr2 = [([Res(f"sq{i}") for i in range(8)], Res("rs")) for _ in range(2)]
                            pend = None
                            for cj, (c0, Tn) in enumerate(sbk_):
                                rz = [rzs[(c0, oc)] for oc in range(8)]
                                pp_ = cj % 2
                                largs = (k, nc, bank, BR, XS, onesm, epsc, r_const, c0, Tn, rz, SQ2[pp_], RS2[pp_], DER[:, 48:56], DER[:, 56:64], lnr2[pp_])
                                lkw = dict(mb=6 - 2 * pp_, vb=7 - 2 * pp_)
                                _ln_stage1(*largs, **lkw)
                                if pend is not None:
                                    _ln_stage2(*pend[0], **pend[1])
                                pend = (largs, lkw)
                            if pend is not None:
                                _ln_stage2(*pend[0], **pend[1])
                            k.barrier()

            run_layers()
            with ExitStack() as ph:
                YST = [sbt(ph, f"yst{i}", [128, D], F32) for i in range(2)]
                ry = [Res("yst0"), Res("yst1")]
                rout = Res("yout")
                for t in range(16):
                    buf = t % 2
                    for half in range(2):
                        bi = (t * 2 + half) % 4

                        def tr(bi=bi, half=half, t=t):
                            inst = None
                            for c in range(4):
                                inst = T.transpose(bank(bi)[:, c * 128:(c + 1) * 128], XS[:, half * 4 + c, t * 128:(t + 1) * 128], ident[:])
                            return inst
                        k.op(pe, tr, reads=[r_const], writes=[BR[bi]])
                        if half == 0:
                            k.op(act, lambda bi=bi, buf=buf, half=half: S.copy(YST[buf][:, half * 512:(half + 1) * 512], bank(bi)[:, 0:512]),
                                 reads=[BR[bi]], writes=[ry[buf]])
                        else:
                            k.op(dve, lambda bi=bi, buf=buf, half=half: V.tensor_copy(YST[buf][:, half * 512:(half + 1) * 512], bank(bi)[:, 0:512]),
                                 reads=[BR[bi]], writes=[ry[buf]])
                    k.dma(sp, [(y_out[t * 128:(t + 1) * 128, :], YST[buf][:])], reads=[ry[buf]], writes=[rout], sem_res=ry[buf])
                k.wait_all_dma(sp)
                k.barrier()
        print("nsem", k.nsem, "counts", {e.name: e.count for e in k.engs})
    return nc


def _ln_stage1(k, nc, bank, BR, XS, onesm, epsc, r_const, c0, Tn, rz, SQ, RS, Gc, Bc, lnr, mb=6, vb=7):
    V, S, G, T = nc.vector, nc.scalar, nc.gpsimd, nc.tensor
    pe, act, dve, pool = k.pe, k.act, k.dve, k.pool

    def mean_mm():
        inst = None
        for oc in range(8):
            inst = T.matmul(bank(mb)[:, 0:Tn], onesm[:], XS[:, oc, c0:c0 + Tn], start=(oc == 0), stop=(oc == 7))
        return inst
    k.op(pe, mean_mm, reads=list(rz) + [r_const], writes=[BR[mb]])
    rsq = lnr[0]
    for oc in range(8):
        k.op(dve, lambda oc=oc: V.tensor_tensor(XS[:, oc, c0:c0 + Tn], XS[:, oc, c0:c0 + Tn], bank(mb)[:, 0:Tn], op=ALU.subtract),
             reads=[rz[oc], BR[mb]], writes=[rz[oc]])
        k.op(pool, lambda oc=oc: G.tensor_tensor(SQ[:, oc, 0:Tn], XS[:, oc, c0:c0 + Tn], XS[:, oc, c0:c0 + Tn], op=ALU.mult),
             reads=[rz[oc]], writes=[rsq[oc]])


def _ln_stage2(k, nc, bank, BR, XS, onesm, epsc, r_const, c0, Tn, rz, SQ, RS, Gc, Bc, lnr, mb=6, vb=7):
    V, S, G, T = nc.vector, nc.scalar, nc.gpsimd, nc.tensor
    pe, act, dve, pool = k.pe, k.act, k.dve, k.pool
    rsq = lnr[0]

    def var_mm():
        inst = None
        for oc in range(8):
            inst = T.matmul(bank(vb)[:, 0:Tn], onesm[:], SQ[:, oc, 0:Tn], start=(oc == 0), stop=(oc == 7))
        return inst
    k.op(pe, var_mm, reads=rsq + [r_const], writes=[BR[vb]])
    rrs = lnr[1]
    k.op(act, lambda: S.activation(RS[:, 0:Tn], bank(vb)[:, 0:Tn], AF.Ln, bias=epsc[:], scale=1.0), reads=[BR[vb], r_const], writes=[rrs])
    k.op(act, lambda: S.activation(RS[:, 0:Tn], RS[:, 0:Tn], AF.Exp, scale=-0.5), reads=[rrs], writes=[rrs])
    for oc in range(8):
        k.op(dve, lambda oc=oc: V.scalar_tensor_tensor(XS[:, oc, c0:c0 + Tn], XS[:, oc, c0:c0 + Tn], Gc[:, oc:oc + 1], RS[:, 0:Tn], ALU.mult, ALU.mult),
             reads=[rz[oc], rrs], writes=[rz[oc]])
        k.op(act, lambda oc=oc: S.activation(XS[:, oc, c0:c0 + Tn], XS[:, oc, c0:c0 + Tn], AF.Identity, bias=Bc[:, oc:oc + 1], scale=1.0),
             reads=[rz[oc]], writes=[rz[oc]])


def _rope_tables(qd):
    GRID_W = 64
    pos = np.arange(NL, dtype=np.int64) + qd * NL
    row = (pos // GRID_W).astype(np.float32)
    col = (pos % GRID_W).astype(np.float32)

    def tab(rot_dim):
        nf = rot_dim // 4
        inv = (np.float32(10000.0) ** (-np.arange(nf, dtype=np.float32) / np.float32(nf))).astype(np.float32)
        ang = np.concatenate([row[:, None] * inv, col[:, None] * inv], axis=-1).astype(np.float32)
        c = np.cos(ang).astype(np.float32).T
        s = np.sin(ang).astype(np.float32).T
        C = np.concatenate([c, c], 0)
        Sg = np.concatenate([-s, s], 0)
        Cf = np.ones((rot_dim, NT), np.float32)
        Sf = np.zeros((rot_dim, NT), np.float32)
        Cf[:, :NL] = C
        Sf[:, :NL] = Sg
        return Cf, Sf
    c64, s64 = tab(64)
    c32, s32 = tab(32)
    out = np.zeros((4, 128, NT), np.float32)
    out[0] = np.concatenate([c64, c64], 0)
    out[1] = np.concatenate([s64, s64], 0)
    out[2, 0:32] = c32
    out[2, 64:96] = c32
    out[3, 0:32] = s32
    out[3, 64:96] = s32
    return out


def _masks(qd):
    kk = np.arange(128)[:, None]
    qq = np.arange(128)[None, :]
    ML = (kk >= qq).astype(np.float32)
    MR = (kk <= qq).astype(np.float32)
    m = np.zeros((128, 10, 128), np.float32)
    m[:, 0] = ML
    m[:, 1] = MR
    for r in range(4):
        if r == qd - 1:
            m[:, 2 + r] = ML
        if r == qd + 1:
            m[:, 6 + r] = MR
    return m.astype(ml_dtypes.bfloat16)


def _prep_weights(w_in, w_uq, w_out, q_norm_b, k_norm_b, mla_q_norm, mla_kv_norm, ln1_g, ln1_b, ln2_g, ln2_b):
    def sw(cols, half):
        cols = np.asarray(cols).reshape(-1, 2, half)
        return cols[:, ::-1, :].reshape(-1)
    aq = np.arange(0, 256).reshape(4, 64)
    ak = np.arange(256, 384)
    av = np.arange(384, 512)
    bq = np.arange(512, 768).reshape(4, 64)
    bk = np.arange(768, 896)
    bv = np.arange(896, 1024)
    cq = np.arange(1024, 1280)
    ckv = np.arange(1280, 1408)
    ckr = np.arange(1408, 1440)
    gA = [np.concatenate([aq[0], aq[2]]), np.concatenate([aq[1], aq[3]]), ak]
    gB = [np.concatenate([bq[0], bq[2]]), np.concatenate([bq[1], bq[3]]), bk]
    groups = {}
    for i in range(3):
        groups[i] = gA[i]
        groups[3 + i] = sw(gA[i], 32)
        groups[6 + i] = gB[i]
        groups[9 + i] = sw(gB[i], 32)
    groups[12] = cq[:128]
    groups[13] = cq[128:]
    groups[14] = ckv
    cols = np.concatenate([groups[g] for g in range(15)] + [ckr, sw(ckr, 16), av, bv])
    assert cols.shape[0] == NWIN
    w_inx = np.ascontiguousarray(w_in[:, :, cols])
    uq_cols = np.arange(768).reshape(8, 96)
    uq_sw = uq_cols.copy()
    for h in range(8):
        uq_sw[h, 64:96] = sw(uq_cols[h, 64:96], 16)
    w_uqx = np.ascontiguousarray(np.concatenate([w_uq, w_uq[:, :, uq_sw.reshape(-1)]], axis=2))
    hr = lambda base, h: np.arange(base + h * 64, base + (h + 1) * 64)
    rows = np.concatenate([hr(0, 0), hr(0, 2), hr(0, 1), hr(0, 3), hr(256, 0), hr(256, 2), hr(256, 1), hr(256, 3), np.arange(512, 1024)])
    w_outx = np.ascontiguousarray(w_out[:, rows, :])
    Ln = w_in.shape[0]
    vecs = np.zeros((Ln, 40, 128), np.float32)
    vecs[:, 0:8] = ln1_g.reshape(Ln, 8, 128)
    vecs[:, 8:16] = ln1_b.reshape(Ln, 8, 128)
    vecs[:, 16:24] = ln2_g.reshape(Ln, 8, 128)
    vecs[:, 24:32] = ln2_b.reshape(Ln, 8, 128)
    vecs[:, 32:34] = mla_q_norm.reshape(Ln, 2, 128)
    vecs[:, 34] = mla_kv_norm
    swi = sw(np.arange(64), 32)
    vecs[:, 35] = np.concatenate([q_norm_b, q_norm_b], 1)
    vecs[:, 36] = np.concatenate([q_norm_b[:, swi], q_norm_b[:, swi]], 1)
    vecs[:, 37] = np.concatenate([k_norm_b, k_norm_b], 1)
    vecs[:, 38] = np.concatenate([k_norm_b[:, swi], k_norm_b[:, swi]], 1)
    return w_inx, w_uqx, w_outx, vecs


_CACHE = {}


def kernel(x, c, ctx, c_ctx, w_mod, b_mod, w_in, sink_a, q_norm_b, k_norm_b, mla_q_norm, mla_kv_norm,
           w_uq, w_uk, w_uv, w_out, ln1_g, ln1_b, w_fc1, w_fc2, ln2_g, ln2_b):
    f = lambda a: np.ascontiguousarray(np.asarray(a, dtype=np.float32))
    x, c, ctx, c_ctx = f(x), f(c), f(ctx), f(c_ctx)
    w_mod, b_mod, w_in, sink_a = f(w_mod), f(b_mod), f(w_in), f(sink_a)
    w_uq, w_uk, w_uv, w_out, w_fc1, w_fc2 = f(w_uq), f(w_uk), f(w_uv), f(w_out), f(w_fc1), f(w_fc2)
    w_fc1 = np.ascontiguousarray(w_fc1.reshape(L, 8, 128, 8, 512).transpose(0, 3, 2, 1, 4))
    w_fc2 = np.ascontiguousarray(w_fc2.reshape(L, 32, 128, 8, 128).transpose(0, 3, 2, 1, 4))
    w_inx, w_uqx, w_outx, vecs = _prep_weights(w_in, w_uq, w_out, f(q_norm_b), f(k_norm_b), f(mla_q_norm), f(mla_kv_norm),
                                               f(ln1_g), f(ln1_b), f(ln2_g), f(ln2_b))
    w_mod_sh, b_mod_sh = [], []
    for qd_ in range(4):
        fch = [s_ * 8 + 2 * qd_ + j_ for s_ in range(6) for j_ in range(2)]
        cols = np.concatenate([np.arange(f_ * 128, (f_ + 1) * 128) for f_ in fch])
        w_mod_sh.append(np.ascontiguousarray(w_mod[:, :, cols]))
        b_mod_sh.append(np.ascontiguousarray(b_mod[:, cols].reshape(L * 12, 128)))
    if "nc" not in _CACHE:
        _CACHE["nc"] = build_program()
    nc = _CACHE["nc"]
    in_maps = []
    for core in range(8):
        b, qd = core // 4, core % 4
        cv = np.stack([c[b].reshape(8, 128), c_ctx.reshape(8, 128)], axis=1).reshape(16, 128)
        in_maps.append({
            "x": np.ascontiguousarray(x[b, qd * NL:(qd + 1) * NL]),
            "ctx": np.ascontiguousarray(ctx[b, qd * NCX:(qd + 1) * NCX]),
            "cvec": np.ascontiguousarray(cv),
            "w_mod": w_mod_sh[qd], "b_mod": b_mod_sh[qd],
            "w_inx": w_inx, "w_uqx": w_uqx, "w_uk": w_uk, "w_uv": w_uv, "w_outx": w_outx,
            "w_fc1": w_fc1, "w_fc2": w_fc2, "vecs": vecs, "sink": sink_a.reshape(L, 1, 4),
            "ropes": _rope_tables(qd), "amask": _masks(qd),
        })
    res = run_bass_kernel_spmd(nc, in_maps, core_ids=list(range(8)))
    out = np.zeros((2, 8192, D), np.float32)
    for core in range(8):
        b, qd = core // 4, core % 4
        out[b, qd * NL:(qd + 1) * NL] = res.results[core]["y"]
    return out
```

```python
import numpy as np
import ml_dtypes
from contextlib import ExitStack
import concourse.bass as bass
import concourse.mybir as mybir
from concourse.bass_utils import run_bass_kernel_spmd

F32 = mybir.dt.float32
BF16 = mybir.dt.bfloat16
AF = mybir.ActivationFunctionType
ALU = mybir.AluOpType

L = 2
D = 1024
NL = 2048
NCX = 64
NT = NL + NCX
ALPHA = float((2 * L) ** 0.25)
EPS = 1e-6
EPOCH = 8000
NWIN = 15 * 128 + 64 + 256
GOFF = {g: g * 128 for g in range(15)}
KR_OFF = 1920
KRS_OFF = 1952
V_OFF = 1984
CHUNKS = [(0, 512), (512, 512), (1024, 512), (1536, 512), (2048, 64)]
KVROWS = 480
VREG = 288
NE = 320
DEBUG = {}


class _Stop(Exception):
    pass


def check_stop(name, l=0):
    return DEBUG.get("stop") == name and l == DEBUG.get("stop_layer", 0)


class Res:
    __slots__ = ("name", "w", "r", "dsem", "dcnt")

    def __init__(self, name=""):
        self.name = name
        self.w = None
        self.r = []
        self.dsem = None
        self.dcnt = 0


class Eng:
    def __init__(self, K, handle, name):
        self.K = K
        self.h = handle
        self.name = name
        self.sems = []
        self.count = 0
        self.waited = {}
        self.last = None

    def next_token(self):
        e = self.count // EPOCH
        while len(self.sems) <= e:
            self.sems.append(self.K.new_sem(f"{self.name}_e{len(self.sems)}"))
        v = self.count % EPOCH + 1
        self.count += 1
        self.last = (self.sems[e], v, self)
        return self.last

    def wait(self, tok):
        sem, val = tok[0], tok[1]
        key = id(sem)
        if self.waited.get(key, 0) >= val:
            return
        self.h.wait_ge(sem, val)
        self.waited[key] = val


class K:
    def __init__(self, nc, stack):
        self.nc = nc
        self.stack = stack
        self.nsem = 0
        self.pe = Eng(self, nc.tensor, "pe")
        self.act = Eng(self, nc.scalar, "act")
        self.dve = Eng(self, nc.vector, "dve")
        self.pool = Eng(self, nc.gpsimd, "pool")
        self.sp = Eng(self, nc.sync, "sp")
        self.engs = [self.pe, self.act, self.dve, self.pool, self.sp]
        self.dma_toks = {}
        self.extra_toks = []
        self.uid = 0

    def new_sem(self, name):
        self.nsem += 1
        return self.stack.enter_context(self.nc.semaphore(name))

    def _deps(self, eng, reads, writes):
        for r in reads:
            if r.w is not None:
                eng.wait(r.w)
        for w in writes:
            if w.w is not None:
                eng.wait(w.w)
            for t in w.r:
                eng.wait(t)

    def op(self, eng, build, reads=(), writes=()):
        self._deps(eng, reads, writes)
        inst = build()
        tok = eng.next_token()
        inst.then_inc(tok[0], 1)
        for r in reads:
            r.r.append(tok)
        for w in writes:
            w.w = tok
            w.r = []
        return tok

    def dma(self, q, pairs, reads=(), writes=(), sem_res=None):
        self._deps(q, reads, writes)
        sr = sem_res if sem_res is not None else writes[0]
        if sr.dsem is None:
            self.uid += 1
            sr.dsem = self.new_sem(f"d{self.uid}")
        if sr.dcnt > 0:
            q.wait((sr.dsem, sr.dcnt, None))
        for (o, i) in pairs:
            q.h.dma_start(out=o, in_=i).then_inc(sr.dsem, 16)
            sr.dcnt += 16
        tok = (sr.dsem, sr.dcnt, None)
        self.dma_toks[id(sr.dsem)] = tok
        for r in reads:
            r.r.append(tok)
        for w in writes:
            w.w = tok
            w.r = []
        return tok

    def wait_all_dma(self, eng):
        for t in self.dma_toks.values():
            eng.wait(t)

    def barrier(self):
        toks = [e.last for e in self.engs if e.last is not None]
        for e in self.engs:
            for t in toks:
                if e is not self.sp or t[2] is not e:
                    e.wait(t)
            for t in self.dma_toks.values():
                e.wait(t)
            for t in self.extra_toks:
                e.wait(t)


def build_program():
    nc = bass.Bass("TRN2", target_bir_lowering=False)

    def din(name, shape, dt=F32):
        return nc.dram_tensor(name, shape, dt, kind="ExternalInput").ap()

    x_in = din("x", [NL, D])
    ctx_in = din("ctx", [NCX, D])
    cvec = din("cvec", [16, 128])
    w_mod = din("w_mod", [L, D, 12 * 128])
    b_mod = din("b_mod", [L * 12, 128])
    w_inx = din("w_inx", [L, D, NWIN])
    w_uqx = din("w_uqx", [L, 256, 1536])
    w_uk = din("w_uk", [L, 128, 512])
    w_uv = din("w_uv", [L, 128, 512])
    w_outx = din("w_outx", [L, D, D])
    w_fc1 = din("w_fc1", [L, 8, 128, 8, 512])
    w_fc2 = din("w_fc2", [L, 8, 128, 32, 128])
    vecs = din("vecs", [L, 40, 128])
    sink = din("sink", [L, 1, 4])
    ropes = din("ropes", [4, 128, NT])
    amask = din("amask", [128, 10, 128], BF16)
    y_out = nc.dram_tensor("y", [NL, D], F32, kind="ExternalOutput").ap()
    QS = nc.dram_tensor("qs", [4 * 128 + 8 * 96, NT], BF16).ap()
    KVX = [nc.dram_tensor(f"kvx{c}", [KVROWS, Tc], BF16).ap() for c, (_, Tc) in enumerate(CHUNKS)]
    KVG = [nc.dram_tensor(f"kvg{c}", [4 * KVROWS, Tc], BF16).ap() for c, (_, Tc) in enumerate(CHUNKS)]
    XROW = {"kB": 0, "ckv": 128, "kr": 256}
    KAL = nc.dram_tensor("kal", [128, NT], BF16).ap()
    VAL = nc.dram_tensor("val", [NT, 192], BF16).ap()
    AEX = nc.dram_tensor("aex", [128 + 192, NE], BF16).ap()
    AEG = nc.dram_tensor("aeg", [4 * (128 + 192), NE], BF16).ap()
    MEX = nc.dram_tensor("mex", [128, L * 24], F32).ap()
    MEG = nc.dram_tensor("meg", [4 * 128, L * 24], F32).ap()

    def chunk_of(tok):
        return 4 if tok >= NL else tok // 512

    def dap(t, off, dims):
        return bass.AP(t.tensor, off, [list(d) for d in dims])

    with ExitStack() as stack:
        k = K(nc, stack)
        pe, act, dve, pool, sp = k.pe, k.act, k.dve, k.pool, k.sp
        V, S, G, T = nc.vector, nc.scalar, nc.gpsimd, nc.tensor

        uidc = [0]

        def sbt(st, name, shape, dt):
            uidc[0] += 1
            return st.enter_context(nc.sbuf_tensor(f"{name}_{uidc[0]}", shape, dt))

        XS = sbt(stack, "XS", [128, 8, NT], F32)
        ident = sbt(stack, "ident", [128, 128], F32)
        onesm = sbt(stack, "onesm", [128, 128], F32)
        ones1 = sbt(stack, "ones1", [128, 128], F32)
        bd64 = sbt(stack, "bd64", [128, 128], F32)
        epsc = sbt(stack, "epsc", [128, 1], F32)
        SL = sbt(stack, "SL", [128, 8, 2], BF16)
        VEC = sbt(stack, "VEC", [128, 40], F32)
        MODL = sbt(stack, "MODL", [128, 48], F32)
        MODC = sbt(stack, "MODC", [128, 48], F32)
        DER = sbt(stack, "DER", [128, 64], F32)
        ESROW = sbt(stack, "ESROW", [1, 4, 128], F32)
        SINKL = sbt(stack, "SINKL", [1, 192], F32)
        MG = sbt(stack, "MG", [128, 4, L * 24], F32)
        PP = [stack.enter_context(nc.psum_tensor(f"pp{i}", [128, 1024], F32)) for i in range(4)]

        def bank(i):
            return PP[i // 2][:, (i % 2) * 512:(i % 2) * 512 + 512]

        BR = [Res(f"bank{i}") for i in range(8)]
        rr = [0]

        def alt(*engs):
            rr[0] += 1
            return engs[rr[0] % len(engs)]

        def ew_tt(eng, out, in0, in1, op, reads=(), writes=()):
            h = eng.h
            return k.op(eng, lambda: h.tensor_tensor(out, in0, in1, op=op), reads, writes)

        def ew_ts(eng, out, in0, s1, s2, op0, op1, reads=(), writes=()):
            h = eng.h
            if s2 is None:
                return k.op(eng, lambda: h.tensor_scalar(out, in0, s1, None, op0), reads, writes)
            return k.op(eng, lambda: h.tensor_scalar(out, in0, s1, s2, op0, op1), reads, writes)

        def mm_group(out, pairs, reads, writes):
            n = len(pairs)

            def b():
                inst = None
                for i, (a, r) in enumerate(pairs):
                    inst = T.matmul(out, a, r, start=(i == 0), stop=(i == n - 1))
                return inst
            return k.op(pe, b, reads, writes)

        with nc.Block() as block:
            r_const = Res("const")
            k.op(pool, lambda: G.memset(ident[:], 0.0), writes=[r_const])
            k.op(pool, lambda: G.affine_select(ident[:], ident[:], pattern=[[-1, 128]], compare_op=ALU.not_equal,
                                               fill=1.0, base=0, channel_multiplier=1), reads=[r_const], writes=[r_const])
            k.op(pool, lambda: G.memset(onesm[:], 1.0 / D), writes=[r_const])
            k.op(pool, lambda: G.memset(ones1[:], 1.0), writes=[r_const])
            k.op(pool, lambda: G.memset(bd64[:], 0.0), writes=[r_const])
            k.op(pool, lambda: G.memset(bd64[0:64, 0:64], 1.0), writes=[r_const])
            k.op(pool, lambda: G.memset(bd64[64:128, 64:128], 1.0), writes=[r_const])
            k.op(pool, lambda: G.memset(epsc[:], EPS), writes=[r_const])
            k.op(pool, lambda: G.memset(SINKL[:], 0.0), writes=[r_const])
            k.op(pool, lambda: G.memset(SINKL[:, 64:128], 1.0), writes=[r_const])

            with ExitStack() as ph:
                XST = [sbt(ph, f"xst{i}", [128, D], F32) for i in range(2)]
                CVS = sbt(ph, "cvs", [16, 128], F32)
                CT = sbt(ph, "ct", [128, 16], F32)
                CT2 = sbt(ph, "ct2", [128, 16], F32)
                rx = [Res("xst0"), Res("xst1")]
                rcv = Res("cvs")
                k.dma(sp, [(CVS[:], cvec)], writes=[rcv])
                WMS = [sbt(ph, f"wms{i}", [128, 8, 1536], BF16) for i in range(L)]
                BMS = sbt(ph, "bms", [L * 12, 128], F32)
                BMT = sbt(ph, "bmt", [128, L * 12], F32)
                MSH = sbt(ph, "msh", [128, L * 12, 2], F32)
                rwms = [Res(f"wms{i}") for i in range(L)]
                rbms = Res("bms")
                for ll in range(L):
                    wsrc = w_mod[ll].rearrange("(c p) n -> p c n", p=128)
                    k.dma(pool, [(WMS[ll][:, 0:4, :], wsrc[:, 0:4, :]), (WMS[ll][:, 4:8, :], wsrc[:, 4:8, :])], writes=[rwms[ll]])
                k.dma(sp, [(BMS[:], b_mod)], writes=[rbms])
                k.op(pe, lambda: T.transpose(bank(4)[:, 0:16], CVS[:], ident[0:16, 0:16]), reads=[rcv, r_const], writes=[BR[4]])
                rct = Res("ct")
                k.op(dve, lambda: V.tensor_copy(CT[:], bank(4)[:, 0:16]), reads=[BR[4]], writes=[rct])
                k.op(act, lambda: S.activation(CT2[:], CT[:], AF.Exp, scale=-1.0), reads=[rct], writes=[rct])
                ew_ts(dve, CT2[:], CT2[:], 1.0, None, ALU.add, None, reads=[rct], writes=[rct])
                k.op(dve, lambda: V.reciprocal(CT2[:], CT2[:]), reads=[rct], writes=[rct])
                ew_tt(dve, SL[:].rearrange("p a b -> p (a b)"), CT[:], CT2[:], ALU.mult, reads=[rct], writes=[rct])
                ccm = k.new_sem("ccm")
                rmex = Res("mex")
                def emit_mod_shard():
                    k.op(pe, lambda: T.transpose(bank(5)[:, 0:L * 12], BMS[:], ident[0:L * 12, 0:L * 12]), reads=[rbms, r_const], writes=[BR[5]])
                    rbmt = Res("bmt")
                    k.op(dve, lambda: V.tensor_copy(BMT[:], bank(5)[:, 0:L * 12]), reads=[BR[5]], writes=[rbmt])
                    MP = bank(6)
                    for ll in range(L):
                        for t_ in range(12):
                            col = (ll * 12 + t_) * 2
                            mm_group(MP[:, col:col + 2], [(WMS[ll][:, kc, t_ * 128:(t_ + 1) * 128], SL[:, kc, :]) for kc in range(8)],
                                     reads=[rwms[ll], rct], writes=[BR[6]])
                    rmsh = Res("msh")
                    ew_tt(dve, MSH[:], MP[:, 0:L * 24].rearrange("p (f v) -> p f v", v=2), BMT[:].unsqueeze(2).to_broadcast([128, L * 12, 2]), ALU.add,
                          reads=[BR[6], rbmt], writes=[rmsh])
                    k.dma(pool, [(MEX, MSH[:].rearrange("p f v -> p (f v)"))], reads=[rmsh], writes=[rmex])
                    k.wait_all_dma(pool)
                    if not DEBUG.get("nocc"):
                        G.collective_compute("AllGather", ALU.bypass, replica_groups=[[0, 1, 2, 3], [4, 5, 6, 7]],
                                             ins=[MEX.opt()], outs=[MEG.opt()]).then_inc(ccm)
                for t in range(17):
                    if t == 8:
                        emit_mod_shard()
                    rows = 128 if t < 16 else 64
                    src = x_in[t * 128:(t + 1) * 128, :] if t < 16 else ctx_in
                    buf = t % 2
                    k.dma(sp, [(XST[buf][0:rows, :], src)], writes=[rx[buf]])
                    for half in range(2):
                        bi = (t * 2 + half) % 4
                        bk = bank(bi)

                        def tr(bk=bk, buf=buf, half=half, rows=rows):
                            inst = None
                            for c in range(4):
                                inst = T.transpose(bk[:, c * 128:c * 128 + rows],
                                                   XST[buf][0:rows, (half * 4 + c) * 128:(half * 4 + c + 1) * 128],
                                                   ident[0:rows, 0:rows])
                            return inst
                        k.op(pe, tr, reads=[rx[buf], r_const], writes=[BR[bi]])
                        src_v = bk.rearrange("p (c t) -> p c t", c=4)[:, :, 0:rows]
                        dst_v = XS[:, half * 4:half * 4 + 4, t * 128:t * 128 + rows]
                        if (t + half) % 2 == 0:
                            k.op(act, lambda s=src_v, d=dst_v: S.activation(d, s, AF.Identity, scale=ALPHA), reads=[BR[bi]])
                        else:
                            ew_ts(dve, dst_v, src_v, ALPHA, None, ALU.mult, None, reads=[BR[bi]])
                rmg = Res("mg")
                if DEBUG.get("nocc"):
                    rmeg = Res("megdbg")
                    k.dma(sp, [(MEG[r_ * 128:(r_ + 1) * 128, :], MEX) for r_ in range(4)], reads=[rmex], writes=[rmeg])
                    k.dma(sp, [(MG[:], MEG.rearrange("(r p) n -> p r n", p=128))], reads=[rmeg], writes=[rmg])
                else:
                    sp.wait((ccm, 1, None))
                    k.dma(sp, [(MG[:], MEG.rearrange("(r p) n -> p r n", p=128))], writes=[rmg])
                k.barrier()

            def run_layers():
              for l in range(L):
                last = (l == L - 1)
                if check_stop('setup', l):
                    return
                ph1 = ExitStack()
                WIN = sbt(ph1, "win", [128, 8, NWIN], BF16)
                WUQ = sbt(ph1, "wuq", [128, 2, 1536], BF16)
                rwin, rwuq = Res("win"), Res("wuq")
                wsrc = w_inx[l].rearrange("(c p) n -> p c n", p=128)
                k.dma(pool, [(WIN[:, 2 * i:2 * i + 2, :], wsrc[:, 2 * i:2 * i + 2, :]) for i in range(4)], writes=[rwin])
                k.dma(pool, [(WUQ[:], w_uqx[l].rearrange("(c p) n -> p c n", p=128))], writes=[rwuq])
                with ExitStack() as ph:
                    VST = sbt(ph, "vst", [40, 128], F32)
                    SKS = sbt(ph, "sks", [1, 4], F32)
                    rv = Res("vst")
                    k.dma(sp, [(VST[:], vecs[l]), (SKS[:], sink[l])], writes=[rv])
                    k.op(pe, lambda: T.transpose(bank(5)[:, 0:40], VST[:], ident[0:40, 0:40]), reads=[rv, r_const], writes=[BR[5]])
                    k.op(dve, lambda: V.tensor_copy(VEC[:], bank(5)[:, 0:40]), reads=[BR[5]])
                    k.op(act, lambda: S.activation(SKS[:], SKS[:], AF.Exp), reads=[rv], writes=[rv])
                    k.op(dve, lambda: V.tensor_copy(ESROW[:], SKS[:].unsqueeze(2).to_broadcast([1, 4, 128])), reads=[rv])
                    for r_ in range(4):
                        for v_, dstt in ((0, MODL), (1, MODC)):
                            srcv = MG[:, r_, l * 24:(l + 1) * 24].rearrange("p (s j v) -> p s j v", s=6, j=2, v=2)[:, :, :, v_]
                            dstv = dstt[:].rearrange("p (s r j) -> p s r j", s=6, r=4, j=2)[:, :, r_, :]
                            k.op(dve, lambda srcv=srcv, dstv=dstv: V.tensor_copy(dstv, srcv), reads=[])
                    k.barrier()
                    ew_ts(dve, DER[:, 0:8], MODL[:, 8:16], 1.0, 1.0 / ALPHA, ALU.add, ALU.mult)
                    ew_ts(dve, DER[:, 8:16], MODC[:, 8:16], 1.0, 1.0 / ALPHA, ALU.add, ALU.mult)
                    ew_ts(dve, DER[:, 16:24], MODL[:, 32:40], 1.0, 1.0 / ALPHA, ALU.add, ALU.mult)
                    ew_ts(dve, DER[:, 24:32], MODC[:, 32:40], 1.0, 1.0 / ALPHA, ALU.add, ALU.mult)
                    ew_ts(dve, DER[:, 32:48], VEC[:, 0:16], ALPHA, None, ALU.mult, None)
                    s2 = 1.0 if last else ALPHA
                    ew_ts(dve, DER[:, 48:64], VEC[:, 16:32], s2, None, ALU.mult, None)
                    k.barrier()
                    if check_stop('mod', l):
                        return
                A1 = (DER[:, 0:8], DER[:, 8:16])
                A2 = (DER[:, 16:24], DER[:, 24:32])
                MOD = (MODL, MODC)

                with ExitStack() as ph:
                    HT = [sbt(ph, f"ht{i}", [128, 8, 512], BF16) for i in range(2)]
                    RT = [sbt(ph, f"rt{i}", [128, 4, 512], F32) for i in range(2)]
                    OST = [sbt(ph, f"ost{i}", [128, 512], BF16) for i in range(4)]
                    TMP = [sbt(ph, f"tmp{i}", [128, 512], F32) for i in range(6)]
                    CQN = sbt(ph, "cqn", [128, 2, 512], BF16)
                    VSTG = sbt(ph, "vstg", [128, 4, 384], BF16)
                    rht = [[Res(f"ht0_{c}") for c in range(8)], [Res(f"ht1_{c}") for c in range(8)]]
                    rrt = [Res("rt0"), Res("rt1")]
                    rost = [Res(f"ost{i}") for i in range(4)]
                    rtmp = [Res(f"tmp{i}") for i in range(6)]
                    rcqn, rvstg = Res("cqn"), Res("vstg")
                    rkvx = Res("kvx")
                    k.op(dve, lambda: V.memset(VSTG[:, :, 64:128], 1.0), writes=[rvstg])
                    k.op(dve, lambda: V.memset(VSTG[:, :, 256:320], 1.0), writes=[rvstg])
                    ost_i = [0]
                    tmp_i = [0]
                    ccs = {c_: k.new_sem(f"cc{l}_{c_}") for c_ in list(range(len(CHUNKS))) + ["edge"]}

                    def next_ost():
                        ost_i[0] = (ost_i[0] + 1) % 4
                        return ost_i[0]

                    def next_tmp():
                        tmp_i[0] = (tmp_i[0] + 1) % 6
                        return tmp_i[0]

                    gq = VEC[:, 35:36]; gqs = VEC[:, 36:37]; gk = VEC[:, 37:38]; gks = VEC[:, 38:39]
                    pbank = [0]

                    def proj(col, M, hb, Tn):
                        bi = pbank[0]
                        pbank[0] = (pbank[0] + 1) % 6
                        mm_group(bank(bi)[0:M, 0:Tn], [(WIN[:, kc, col:col + M], HT[hb][:, kc, 0:Tn]) for kc in range(8)],
                                 reads=[rwin] + rht[hb], writes=[BR[bi]])
                        return bi

                    def rstd_from(bi_list, ones_mat, scale, Tn, M=128):
                        sqs = []
                        for bi in bi_list:
                            ti = next_tmp()
                            k.op(act, lambda bi=bi, ti=ti: S.activation(TMP[ti][:, 0:Tn], bank(bi)[:, 0:Tn], AF.Square),
                                 reads=[BR[bi]], writes=[rtmp[ti]])
                            sqs.append(ti)
                        mm_group(bank(6)[:, 0:Tn], [(ones_mat, TMP[ti][:, 0:Tn]) for ti in sqs],
                                 reads=[rtmp[ti] for ti in sqs] + [r_const], writes=[BR[6]])
                        tr_ = next_tmp()
                        k.op(act, lambda: S.activation(TMP[tr_][:, 0:Tn], bank(6)[:, 0:Tn], AF.Ln, bias=epsc[:], scale=scale),
                             reads=[BR[6], r_const], writes=[rtmp[tr_]])
                        k.op(act, lambda: S.activation(TMP[tr_][:, 0:Tn], TMP[tr_][:, 0:Tn], AF.Exp, scale=-0.5),
                             reads=[rtmp[tr_]], writes=[rtmp[tr_]])
                        return tr_

                    def store(oi, M, Tn, dst_rows, c0):
                        if isinstance(dst_rows, str):
                            dst = KVX[chunk_of(c0)][XROW[dst_rows]:XROW[dst_rows] + M, 0:Tn]
                        else:
                            dst = dst_rows[:, c0:c0 + Tn]
                        k.dma(sp, [(dst, OST[oi][0:M, 0:Tn])], reads=[rost[oi]], writes=[], sem_res=rost[oi])

                    for pos_, ci in enumerate((3, 4, 0, 1, 2)):
                        c0, Tn = CHUNKS[ci]
                        hb = pos_ % 2
                        isctx = 1 if ci == 4 else 0
                        for c in range(8):
                            ew_ts(dve if c % 2 == 0 else pool, HT[hb][:, c, 0:Tn], XS[:, c, c0:c0 + Tn], A1[isctx][:, c:c + 1],
                                  MOD[isctx][:, c:c + 1], ALU.mult, ALU.add, writes=[rht[hb][c]])
                        k.dma(sp, [(RT[hb][:, :, 0:Tn], ropes[:, :, c0:c0 + Tn].rearrange("a p t -> p a t"))], writes=[rrt[hb]])
                        C64 = RT[hb][:, 0, 0:Tn]; S64 = RT[hb][:, 1, 0:Tn]
                        C32 = RT[hb][:, 2, 0:Tn]; S32 = RT[hb][:, 3, 0:Tn]
                        for gi, dst in ((0, QS[0:128, :]), (1, QS[128:256, :]), (2, KAL)):
                            b0 = proj(GOFF[gi], 128, hb, Tn)
                            b1 = proj(GOFF[gi + 3], 128, hb, Tn)
                            t1 = next_tmp(); t2 = next_tmp(); oi = next_ost()
                            ew_tt(dve, TMP[t1][:, 0:Tn], bank(b0)[:, 0:Tn], C64, ALU.mult, reads=[BR[b0], rrt[hb]], writes=[rtmp[t1]])
                            ew_tt(dve, TMP[t2][:, 0:Tn], bank(b1)[:, 0:Tn], S64, ALU.mult, reads=[BR[b1], rrt[hb]], writes=[rtmp[t2]])
                            ew_tt(pool, OST[oi][:, 0:Tn], TMP[t1][:, 0:Tn], TMP[t2][:, 0:Tn], ALU.add,
                                  reads=[rtmp[t1], rtmp[t2]], writes=[rost[oi]])
                            store(oi, 128, Tn, dst, c0)
                        for gi, dst, g_, gs_ in ((6, QS[256:384, :], gq, gqs), (7, QS[384:512, :], gq, gqs), (8, 'kB', gk, gks)):
                            b0 = proj(GOFF[gi], 128, hb, Tn)
                            b1 = proj(GOFF[gi + 3], 128, hb, Tn)
                            tr_ = rstd_from([b0], bd64[:], 1.0 / 64, Tn)
                            t1 = next_tmp(); t2 = next_tmp(); oi = next_ost()
                            k.op(dve, lambda: V.scalar_tensor_tensor(TMP[t1][:, 0:Tn], bank(b0)[:, 0:Tn], g_, TMP[tr_][:, 0:Tn], ALU.mult, ALU.mult),
                                 reads=[BR[b0], rtmp[tr_]], writes=[rtmp[t1]])
                            k.op(dve, lambda: V.scalar_tensor_tensor(TMP[t2][:, 0:Tn], bank(b1)[:, 0:Tn], gs_, TMP[tr_][:, 0:Tn], ALU.mult, ALU.mult),
                                 reads=[BR[b1], rtmp[tr_]], writes=[rtmp[t2]])
                            ew_tt(pool, TMP[t1][:, 0:Tn], TMP[t1][:, 0:Tn], C64, ALU.mult, reads=[rtmp[t1], rrt[hb]], writes=[rtmp[t1]])
                            ew_tt(pool, TMP[t2][:, 0:Tn], TMP[t2][:, 0:Tn], S64, ALU.mult, reads=[rtmp[t2], rrt[hb]], writes=[rtmp[t2]])
                            ew_tt(pool, OST[oi][:, 0:Tn], TMP[t1][:, 0:Tn], TMP[t2][:, 0:Tn], ALU.add,
                                  reads=[rtmp[t1], rtmp[t2]], writes=[rost[oi]])
                            store(oi, 128, Tn, dst, c0)
                        b0 = proj(GOFF[14], 128, hb, Tn)
                        tr_ = rstd_from([b0], ones1[:], 1.0 / 128, Tn)
                        oi = next_ost()
                        k.op(dve, lambda: V.scalar_tensor_tensor(OST[oi][:, 0:Tn], bank(b0)[:, 0:Tn], VEC[:, 34:35], TMP[tr_][:, 0:Tn], ALU.mult, ALU.mult),
                             reads=[BR[b0], rtmp[tr_]], writes=[rost[oi]])
                        store(oi, 128, Tn, 'ckv', c0)
                        b0 = proj(KR_OFF, 32, hb, Tn)
                        b1 = proj(KRS_OFF, 32, hb, Tn)
                        t1 = next_tmp(); t2 = next_tmp(); oi = next_ost()
                        ew_tt(dve, TMP[t1][0:32, 0:Tn], bank(b0)[0:32, 0:Tn], C32[0:32], ALU.mult, reads=[BR[b0], rrt[hb]], writes=[rtmp[t1]])
                        ew_tt(dve, TMP[t2][0:32, 0:Tn], bank(b1)[0:32, 0:Tn], S32[0:32], ALU.mult, reads=[BR[b1], rrt[hb]], writes=[rtmp[t2]])
                        ew_tt(pool, OST[oi][0:32, 0:Tn], TMP[t1][0:32, 0:Tn], TMP[t2][0:32, 0:Tn], ALU.add,
                              reads=[rtmp[t1], rtmp[t2]], writes=[rost[oi]])
                        store(oi, 32, Tn, 'kr', c0)
                        b0 = proj(GOFF[12], 128, hb, Tn)
                        b1 = proj(GOFF[13], 128, hb, Tn)
                        tr_ = rstd_from([b0, b1], ones1[:], 1.0 / 256, Tn)
                        k.op(dve, lambda: V.scalar_tensor_tensor(CQN[:, 0, 0:Tn], bank(b0)[:, 0:Tn], VEC[:, 32:33], TMP[tr_][:, 0:Tn], ALU.mult, ALU.mult),
                             reads=[BR[b0], rtmp[tr_]], writes=[rcqn])
                        k.op(dve, lambda: V.scalar_tensor_tensor(CQN[:, 1, 0:Tn], bank(b1)[:, 0:Tn], VEC[:, 33:34], TMP[tr_][:, 0:Tn], ALU.mult, ALU.mult),
                             reads=[BR[b1], rtmp[tr_]], writes=[rcqn])
                        for h in range(8):
                            bq = pbank[0]; pbank[0] = (pbank[0] + 1) % 6
                            bs = pbank[0]; pbank[0] = (pbank[0] + 1) % 6
                            mm_group(bank(bq)[0:96, 0:Tn], [(WUQ[:, kc, h * 96:(h + 1) * 96], CQN[:, kc, 0:Tn]) for kc in range(2)],
                                     reads=[rwuq, rcqn], writes=[BR[bq]])
                            mm_group(bank(bs)[0:96, 0:Tn], [(WUQ[:, kc, 768 + h * 96:768 + (h + 1) * 96], CQN[:, kc, 0:Tn]) for kc in range(2)],
                                     reads=[rwuq, rcqn], writes=[BR[bs]])
                            t1 = next_tmp(); t2 = next_tmp(); oi = next_ost()
                            k.op(act, lambda: S.copy(OST[oi][0:64, 0:Tn], bank(bq)[0:64, 0:Tn]), reads=[BR[bq]], writes=[rost[oi]])
                            ew_tt(dve, TMP[t1][64:96, 0:Tn], bank(bq)[64:96, 0:Tn], C32[64:96], ALU.mult, reads=[BR[bq], rrt[hb]], writes=[rtmp[t1]])
                            ew_tt(dve, TMP[t2][64:96, 0:Tn], bank(bs)[64:96, 0:Tn], S32[64:96], ALU.mult, reads=[BR[bs], rrt[hb]], writes=[rtmp[t2]])
                            ew_tt(pool, OST[oi][64:96, 0:Tn], TMP[t1][64:96, 0:Tn], TMP[t2][64:96, 0:Tn], ALU.add,
                                  reads=[rtmp[t1], rtmp[t2]], writes=[rost[oi]])
                            store(oi, 96, Tn, QS[512 + h * 96:512 + (h + 1) * 96, :], c0)
                        ntile = (Tn + 127) // 128
                        for tt in range(ntile):
                            rows = min(128, Tn - tt * 128)
                            mm_group(bank(7)[0:rows, 0:256], [(HT[hb][:, kc, tt * 128:tt * 128 + rows], WIN[:, kc, V_OFF:V_OFF + 256]) for kc in range(8)],
                                     reads=[rwin] + rht[hb], writes=[BR[7]])
                            for mx in range(2):
                                srcv = bank(7)[0:rows, mx * 128:(mx + 1) * 128].rearrange("p (a b) -> p a b", a=2)
                                dstv = VSTG[0:rows, tt, mx * 192:(mx + 1) * 192].rearrange("p (a b) -> p a b", a=3)[:, 0:3:2, :]
                                k.op(act, lambda s=srcv, d=dstv: S.copy(d, s), reads=[BR[7]], writes=[rvstg])
                        rows = min(128, Tn)
                        vdst = dap(KVX[ci], VREG * Tn, [[192, rows], [128 * 192, ntile], [1, 192]])
                        vadst = dap(VAL, c0 * 192, [[192, rows], [128 * 192, ntile], [1, 192]])
                        k.dma(sp, [(vdst, VSTG[0:rows, 0:ntile, 192:384]), (vadst, VSTG[0:rows, 0:ntile, 0:192])],
                              reads=[rvstg], writes=[], sem_res=rvstg)
                        k.wait_all_dma(pool)
                        if DEBUG.get("nocc"):
                            k.wait_all_dma(sp)
                            k.dma(sp, [(KVG[ci][r_ * KVROWS:(r_ + 1) * KVROWS, :], KVX[ci]) for r_ in range(4)], writes=[Res("kvgdbg")])
                        else:
                            G.collective_compute("AllGather", ALU.bypass, replica_groups=[[0, 1, 2, 3], [4, 5, 6, 7]],
                                                 ins=[KVX[ci].opt()], outs=[KVG[ci].opt()]).then_inc(ccs[ci])
                        if pos_ == 2:
                            k.wait_all_dma(sp)
                            redge = Res("aex")
                            k.dma(sp, [(AEX[0:128, 0:128], KAL[:, 0:128]), (AEX[0:128, 128:256], KAL[:, NL - 128:NL]), (AEX[0:128, 256:NE], KAL[:, NL:NT]),
                                       (dap(AEX, 128 * NE, [[192, 128], [1, 192]]), VAL[0:128, :]),
                                       (dap(AEX, 128 * NE + 128 * 192, [[192, 128], [1, 192]]), VAL[NL - 128:NL, :]),
                                       (dap(AEX, 128 * NE + 256 * 192, [[192, 64], [1, 192]]), VAL[NL:NT, :])], writes=[redge])
                            k.wait_all_dma(pool)
                            if DEBUG.get("nocc"):
                                k.wait_all_dma(sp)
                                k.dma(sp, [(AEG[r_ * 320:(r_ + 1) * 320, :], AEX) for r_ in range(4)], writes=[Res("aegdbg")])
                            else:
                                G.collective_compute("AllGather", ALU.bypass, replica_groups=[[0, 1, 2, 3], [4, 5, 6, 7]],
                                                     ins=[AEX.opt()], outs=[AEG.opt()]).then_inc(ccs["edge"])
                    k.extra_toks = [] if DEBUG.get('nocc') else [(ccs["edge"], 1, None)]
                    late_toks = [] if DEBUG.get('nocc') else [(ccs[c_], 1, None) for c_ in range(len(CHUNKS))]
                    k.barrier()
                    if check_stop('p1', l):
                        return
                ph1.close()

                with ExitStack() as pha:
                    ATT = sbt(pha, "att", [128, 8, NT], BF16)
                    PT = [sbt(pha, f"pt{i}", [128, 2, 512], BF16) for i in range(4)]
                    rpt = [Res(f"pt{i}") for i in range(4)]
                    QT = [sbt(pha, f"qt{i}", [128, 512], BF16) for i in range(2)]
                    rqt = [Res("qt0"), Res("qt1")]
                    RL = [sbt(pha, f"rl{i}", [128, 512], F32) for i in range(2)]
                    rrl = [Res("rl0"), Res("rl1")]
                    cnt = {"pt": 0, "q": 0, "s": 0, "o": 0, "rl": 0}

                    def kv_pairs(dst, dcol, r, grp, nrows, c0, n):
                        out = []
                        t = c0
                        while t < c0 + n:
                            ci = chunk_of(t)
                            cb, cT = CHUNKS[ci]
                            m = min(c0 + n, cb + cT) - t
                            row0 = r * KVROWS + XROW[grp]
                            out.append((dst[:, dcol + (t - c0):dcol + (t - c0) + m], KVG[ci][row0:row0 + nrows, t - cb:t - cb + m]))
                            t += m
                        return out

                    def v_src(r, tok0, ntok, col0, ncol, tiles=None, own=False):
                        ci = chunk_of(tok0)
                        cb, cT = CHUNKS[ci]
                        assert tok0 + (ntok if tiles is None else tiles * 128) <= cb + cT
                        buf = KVX[ci] if own else KVG[ci]
                        base = ((0 if own else r * KVROWS) + VREG) * cT + (tok0 - cb) * 192 + col0
                        if tiles is None:
                            return dap(buf, base, [[192, ntok], [1, ncol]])
                        return dap(buf, base, [[192, 128], [128 * 192, tiles], [1, ncol]])

                    def run_pipeline(items, front, back, depth=1):
                        pend = []
                        for it in items:
                            front(it)
                            pend.append(it)
                            if len(pend) > depth:
                                back(pend.pop(0))
                        for it in pend:
                            back(it)

                    def finalize(ob, e, chunk, c0, Tn):
                        ri = cnt["rl"] % 2
                        cnt["rl"] += 1
                        o0, l0 = (0, 64) if e == 0 else (64, 0)
                        k.op(dve, lambda: V.reciprocal(RL[ri][o0:o0 + 64, 0:Tn], bank(ob)[l0:l0 + 64, 0:Tn]), reads=[BR[ob]], writes=[rrl[ri]])
                        ew_tt(dve, ATT[o0:o0 + 64, chunk, c0:c0 + Tn], bank(ob)[o0:o0 + 64, 0:Tn], RL[ri][o0:o0 + 64, 0:Tn], ALU.mult,
                              reads=[BR[ob], rrl[ri]])

                    phB = ExitStack()
                    KB = sbt(phB, "kb", [128, 8448], BF16)
                    VB = sbt(phB, "vb", [128, 66, 192], BF16)
                    OSB = [sbt(phB, f"osb{i}", [128, 512], F32) for i in range(2)]
                    RLB = [sbt(phB, f"rlb{i}", [128, 512], F32) for i in range(2)]
                    rosb = [Res("osb0"), Res("osb1")]
                    rrlb = [Res("rlb0"), Res("rlb1")]
                    rkb = Res("kb")
                    for t_ in late_toks:
                        pool.wait(t_)
                    for r in range(4):
                        prs = kv_pairs(KB, r * NL, r, 'kB', 128, 0, NL)
                        prs += kv_pairs(KB, 8192 + r * 64, r, 'kB', 128, NL, 64)
                        for c_ in range(4):
                            prs.append((VB[:, r * 16 + c_ * 4:r * 16 + c_ * 4 + 4, :], v_src(r, c_ * 512, 128, 0, 192, tiles=4)))
                        prs.append((VB[(r % 2) * 64:(r % 2) * 64 + 64, 64 + r // 2, :], v_src(r, NL, 64, 0, 192)))
                        k.dma(pool, prs, writes=[rkb])
                    with ExitStack() as ph:
                        KA = sbt(ph, "ka", [128, NL], BF16)
                        KAC = sbt(ph, "kac", [128, 8, 128], BF16)
                        KAX = sbt(ph, "kax", [128, 256], BF16)
                        VA = sbt(ph, "va", [128, 16, 192], BF16)
                        VAC = sbt(ph, "vac", [128, 8, 192], BF16)
                        VAX = sbt(ph, "vax", [128, 2, 192], BF16)
                        MSK = sbt(ph, "msk", [128, 10, 128], BF16)
                        rka = Res("ka")
                        k.dma(sp, [(KA[:], KAL[:, 0:NL]),
                                   (VA[:], dap(VAL, 0, [[192, 128], [128 * 192, 16], [1, 192]])),
                                   (MSK[:], amask)], writes=[rka])
                        prs = []
                        for r in range(4):
                            eb = r * 320
                            vb_ = (eb + 128) * NE
                            prs.append((KAC[:, r, :], AEG[eb:eb + 128, 128:256]))
                            prs.append((KAC[:, 4 + r, :], AEG[eb:eb + 128, 0:128]))
                            prs.append((KAX[:, r * 64:(r + 1) * 64], AEG[eb:eb + 128, 256:NE]))
                            prs.append((VAC[:, r, :], dap(AEG, vb_ + 128 * 192, [[192, 128], [1, 192]])))
                            prs.append((VAC[:, 4 + r, :], dap(AEG, vb_, [[192, 128], [1, 192]])))
                            prs.append((VAX[(r % 2) * 64:(r % 2) * 64 + 64, r // 2, :], dap(AEG, vb_ + 256 * 192, [[192, 64], [1, 192]])))
                        k.dma(sp, prs, writes=[rka])
                        nblk = 16 if last else 17
                        itemsA = []
                        for pc in range(2):
                            for n in range(nblk):
                                c0 = n * 128
                                Tn = 128 if n < 16 else 64
                                tiles = []
                                if n < 16:
                                    if n == 0:
                                        for r in range(4):
                                            tiles.append((KAC[:, r, :], VAC[:, r, :], 2 + r))
                                    else:
                                        tiles.append((KA[:, (n - 1) * 128:n * 128], VA[:, n - 1, :], 0))
                                    tiles.append((KA[:, n * 128:(n + 1) * 128], VA[:, n, :], None))
                                    if n == 15:
                                        for r in range(4):
                                            tiles.append((KAC[:, 4 + r, :], VAC[:, 4 + r, :], 6 + r))
                                    else:
                                        tiles.append((KA[:, (n + 1) * 128:(n + 2) * 128], VA[:, n + 1, :], 1))
                                tiles.append((KAX[:, 0:128], VAX[:, 0, :], None))
                                tiles.append((KAX[:, 128:256], VAX[:, 1, :], None))
                                blk = {"pc": pc, "c0": c0, "Tn": Tn}
                                for ti, (kt, vt, mi) in enumerate(tiles):
                                    itemsA.append({"blk": blk, "kt": kt, "vt": vt, "mi": mi, "first": ti == 0, "last": ti == len(tiles) - 1})

                        def frontA(it):
                            blk = it["blk"]
                            Tn = blk["Tn"]
                            if it["first"]:
                                qi = cnt["q"] % 2
                                cnt["q"] += 1
                                blk["qi"] = qi
                                k.dma(sp, [(QT[qi][:, 0:Tn], QS[blk["pc"] * 128:(blk["pc"] + 1) * 128, blk["c0"]:blk["c0"] + Tn])], writes=[rqt[qi]])
                                blk["ob"] = [4 + (cnt["o"] % 2) * 2, 5 + (cnt["o"] % 2) * 2]
                                cnt["o"] += 1
                            qi = blk["qi"]
                            sp_ = cnt["s"] % 2
                            cnt["s"] += 1
                            s0, s1 = 2 * sp_, 2 * sp_ + 1
                            kt = it["kt"]

                            def qk():
                                T.matmul(bank(s0)[:, 0:Tn], kt[0:64, :], QT[qi][0:64, 0:Tn], start=True, stop=True)
                                return T.matmul(bank(s1)[:, 0:Tn], kt[64:128, :], QT[qi][64:128, 0:Tn], start=True, stop=True)
                            k.op(pe, qk, reads=[rka, rqt[qi]], writes=[BR[s0], BR[s1]])
                            pi = cnt["pt"] % 3
                            cnt["pt"] += 1
                            it["pi"] = pi
                            sv = PP[sp_][:].rearrange("p (e t) -> p e t", e=2)[:, :, 0:Tn]
                            pv_ = PT[pi][:, :, 0:Tn]
                            k.op(act, lambda: S.activation(pv_, sv, AF.Exp, scale=0.125), reads=[BR[s0], BR[s1]], writes=[rpt[pi]])
                            if it["mi"] is not None:
                                mk = MSK[:, it["mi"], 0:Tn].unsqueeze(1).to_broadcast([128, 2, Tn])
                                ew_tt(dve, pv_, pv_, mk, ALU.mult, reads=[rpt[pi], rka], writes=[rpt[pi]])

                        def backA(it):
                            blk = it["blk"]
                            Tn, ob, pi, vt = blk["Tn"], blk["ob"], it["pi"], it["vt"]
                            for e in range(2):
                                h = blk["pc"] + 2 * e

                                def pvm():
                                    inst = T.matmul(bank(ob[e])[:, 0:Tn], vt[:, e * 64:e * 64 + 128], PT[pi][:, e, 0:Tn], start=it["first"], stop=False)
                                    if it["last"]:
                                        sl = SINKL[0:1, 0:128] if e == 0 else SINKL[0:1, 64:192]
                                        inst = T.matmul(bank(ob[e])[:, 0:Tn], sl, ESROW[0:1, h, 0:Tn], start=False, stop=True)
                                    return inst
                                k.op(pe, pvm, reads=[rpt[pi], rka, r_const], writes=[BR[ob[e]]])
                            if it["last"]:
                                for e in range(2):
                                    finalize(ob[e], e, blk["pc"], blk["c0"], Tn)
                        run_pipeline(itemsA, frontA, backA)
                        k.extra_toks = k.extra_toks + late_toks
                        k.barrier()
                        if check_stop('a', l):
                            return

                    with ExitStack() as ph:
                        itemsB = []
                        blocksB = []
                        for pc in range(2):
                            for ci, (c0, Tn) in enumerate(CHUNKS):
                                if ci == 4 and last:
                                    continue
                                jl = list(range(66)) if ci < 4 else [64, 65]
                                blk = {"pc": pc, "c0": c0, "Tn": Tn, "bidx": len(blocksB)}
                                blocksB.append(blk)
                                for ji, j in enumerate(jl):
                                    itemsB.append({"blk": blk, "j": j, "first": ji == 0, "last": ji == len(jl) - 1})

                        def loadqB(blk):
                            qi = blk["bidx"] % 2
                            blk["qi"] = qi
                            Tn = blk["Tn"]
                            k.dma(sp, [(QT[qi][:, 0:Tn], QS[256 + blk["pc"] * 128:256 + (blk["pc"] + 1) * 128, blk["c0"]:blk["c0"] + Tn])], writes=[rqt[qi]])
                        loadqB(blocksB[0])

                        def frontB(it):
                            blk = it["blk"]
                            Tn, j = blk["Tn"], it["j"]
                            if it["first"] and blk["bidx"] + 1 < len(blocksB):
                                loadqB(blocksB[blk["bidx"] + 1])
                            qi = blk["qi"]
                            sp_ = cnt["s"] % 3
                            cnt["s"] += 1
                            s0, s1 = 2 * sp_, 2 * sp_ + 1

                            def qk():
                                T.matmul(bank(s0)[:, 0:Tn], KB[0:64, j * 128:(j + 1) * 128], QT[qi][0:64, 0:Tn], start=True, stop=True)
                                return T.matmul(bank(s1)[:, 0:Tn], KB[64:128, j * 128:(j + 1) * 128], QT[qi][64:128, 0:Tn], start=True, stop=True)
                            k.op(pe, qk, reads=[rkb, rqt[qi]], writes=[BR[s0], BR[s1]])
                            pi = cnt["pt"] % 4
                            cnt["pt"] += 1
                            it["pi"] = pi
                            sv = PP[sp_][:].rearrange("p (e t) -> p e t", e=2)[:, :, 0:Tn]
                            k.op(act, lambda: S.activation(PT[pi][:, :, 0:Tn], sv, AF.Exp, scale=0.125),
                                 reads=[BR[s0], BR[s1]], writes=[rpt[pi]])

                        def backB(it):
                            blk = it["blk"]
                            Tn, pi, j = blk["Tn"], it["pi"], it["j"]
                            ob = [6, 7]

                            def pvm():
                                T.matmul(bank(ob[0])[:, 0:Tn], VB[:, j, 0:128], PT[pi][:, 0, 0:Tn], start=it["first"], stop=it["last"])
                                return T.matmul(bank(ob[1])[:, 0:Tn], VB[:, j, 64:192], PT[pi][:, 1, 0:Tn], start=it["first"], stop=it["last"])
                            k.op(pe, pvm, reads=[rpt[pi], rkb], writes=[BR[ob[0]], BR[ob[1]]])
                            if it["last"]:
                                for e in range(2):
                                    k.op(dve, lambda e=e: V.tensor_copy(OSB[e][:, 0:Tn], bank(ob[e])[:, 0:Tn]), reads=[BR[ob[e]]], writes=[rosb[e]])
                                for e in range(2):
                                    o0, l0 = (0, 64) if e == 0 else (64, 0)
                                    k.op(dve, lambda e=e, o0=o0, l0=l0: V.reciprocal(RLB[e][o0:o0 + 64, 0:Tn], OSB[e][l0:l0 + 64, 0:Tn]),
                                         reads=[rosb[e]], writes=[rrlb[e]])
                                    ew_tt(dve, ATT[o0:o0 + 64, 2 + blk["pc"], blk["c0"]:blk["c0"] + Tn], OSB[e][o0:o0 + 64, 0:Tn],
                                          RLB[e][o0:o0 + 64, 0:Tn], ALU.mult, reads=[rosb[e], rrlb[e]])
                        run_pipeline(itemsB, frontB, backB, depth=2)
                        k.barrier()
                        if check_stop('b', l):
                            return
                    phB.close()

                    with ExitStack() as ph:
                        CKV = sbt(ph, "ckv", [128, 8448], BF16)
                        KC = [sbt(ph, f"kc{i}", [96, 8448], BF16) for i in range(2)]
                        VC = [sbt(ph, f"vc{i}", [128, 66, 128], BF16) for i in range(2)]
                        WUK = sbt(ph, "wuk", [128, 512], BF16)
                        WUV = sbt(ph, "wuv", [128, 512], BF16)
                        rckv, rw = Res("ckv"), Res("wukv")
                        rkc = [Res("kc0"), Res("kc1")]
                        rvc = [Res("vc0"), Res("vc1")]
                        k.dma(pool, [(WUK[:], w_uk[l]), (WUV[:], w_uv[l])], writes=[rw])
                        prs = []
                        for r in range(4):
                            prs += kv_pairs(CKV, r * NL, r, 'ckv', 128, 0, NL)
                            prs += kv_pairs(CKV, 8192 + r * 64, r, 'ckv', 128, NL, 64)
                        k.dma(sp, prs, writes=[rckv])
                        for i in range(2):
                            prs = []
                            for r in range(4):
                                prs += kv_pairs(KC[i][64:96, :], r * NL, r, 'kr', 32, 0, NL)
                                prs += kv_pairs(KC[i][64:96, :], 8192 + r * 64, r, 'kr', 32, NL, 64)
                            k.dma(sp, prs, writes=[rkc[i]])
                        k.op(pool, lambda: G.memset(VC[0][:, :, 64:128], 1.0), writes=[rvc[0]])
                        k.op(pool, lambda: G.memset(VC[1][:, :, 0:64], 1.0), writes=[rvc[1]])

                        def prep_pieces(h):
                            nb = h % 2
                            voff = 0 if nb == 0 else 64
                            pcs = []
                            for kc0 in range(0, 8448, 512):
                                n = min(512, 8448 - kc0)

                                def pk(bi, kc0=kc0, n=n):
                                    mm_group(bank(bi)[0:64, 0:n], [(WUK[:, h * 64:(h + 1) * 64], CKV[:, kc0:kc0 + n])], reads=[rckv, rw], writes=[BR[bi]])
                                    k.op(dve, lambda: V.tensor_copy(KC[nb][0:64, kc0:kc0 + n], bank(bi)[0:64, 0:n]), reads=[BR[bi]], writes=[rkc[nb]])
                                pcs.append(pk)
                            for j0 in range(0, 66, 8):
                                nj = min(8, 66 - j0)

                                def pv(bi, j0=j0, nj=nj):
                                    def vmm():
                                        inst = None
                                        for a in range(nj):
                                            inst = T.matmul(bank(bi)[:, a * 64:(a + 1) * 64], CKV[:, (j0 + a) * 128:(j0 + a + 1) * 128],
                                                            WUV[:, h * 64:(h + 1) * 64], start=True, stop=True)
                                        return inst
                                    k.op(pe, vmm, reads=[rckv, rw], writes=[BR[bi]])
                                    srcv = bank(bi)[:, 0:nj * 64].rearrange("p (a c) -> p a c", c=64)
                                    dstv = VC[nb][:, j0:j0 + nj, voff:voff + 64]
                                    k.op(dve, lambda: V.tensor_copy(dstv, srcv), reads=[BR[bi]], writes=[rvc[nb]])
                                pcs.append(pv)
                            return pcs

                        for pi_, pc_ in enumerate(prep_pieces(0)):
                            pc_(6 + pi_ % 2)
                        itemsC = []
                        blocksC = []
                        for h in range(8):
                            hitems = []
                            for ci, (c0, Tn) in enumerate(CHUNKS):
                                if ci == 4 and last:
                                    continue
                                jl = list(range(0, 66, 2)) if ci < 4 else [64]
                                blk = {"h": h, "c0": c0, "Tn": Tn, "bidx": len(blocksC)}
                                blocksC.append(blk)
                                for ji, j in enumerate(jl):
                                    hitems.append({"blk": blk, "j": j, "ji": ji, "first": ji == 0, "last": ji == len(jl) - 1, "prep": None})
                            if h < 7:
                                pcs = prep_pieces(h + 1)
                                cand = [it for it in hitems if it["ji"] >= 4 and not it["last"]]
                                step = max(1, len(cand) // len(pcs))
                                for pi_, pc_ in enumerate(pcs):
                                    cand[min(pi_ * step, len(cand) - 1 - (len(pcs) - 1 - pi_))]["prep"] = pc_
                            itemsC.extend(hitems)

                        def loadqC(blk):
                            qi = blk["bidx"] % 2
                            blk["qi"] = qi
                            Tn, h = blk["Tn"], blk["h"]
                            k.dma(sp, [(QT[qi][0:96, 0:Tn], QS[512 + h * 96:512 + (h + 1) * 96, blk["c0"]:blk["c0"] + Tn])], writes=[rqt[qi]])
                        loadqC(blocksC[0])

                        def frontC(it):
                            blk = it["blk"]
                            Tn, j, h = blk["Tn"], it["j"], blk["h"]
                            kb_ = h % 2
                            if it["first"] and blk["bidx"] + 1 < len(blocksC):
                                loadqC(blocksC[blk["bidx"] + 1])
                            if it["prep"] is not None:
                                it["prep"](6 + (blk["bidx"] + 1) % 2)
                            qi = blk["qi"]
                            sp_ = cnt["s"] % 3
                            cnt["s"] += 1
                            s0, s1 = 2 * sp_, 2 * sp_ + 1

                            def qk():
                                T.matmul(bank(s0)[:, 0:Tn], KC[kb_][:, j * 128:(j + 1) * 128], QT[qi][0:96, 0:Tn], start=True, stop=True)
                                return T.matmul(bank(s1)[:, 0:Tn], KC[kb_][:, (j + 1) * 128:(j + 2) * 128], QT[qi][0:96, 0:Tn], start=True, stop=True)
                            k.op(pe, qk, reads=[rkc[kb_], rqt[qi]], writes=[BR[s0], BR[s1]])
                            pi = cnt["pt"] % 4
                            cnt["pt"] += 1
                            it["pi"] = pi
                            sv = PP[sp_][:].rearrange("p (e t) -> p e t", e=2)[:, :, 0:Tn]
                            k.op(act, lambda: S.activation(PT[pi][:, :, 0:Tn], sv, AF.Exp, scale=float(96 ** -0.5)),
                                 reads=[BR[s0], BR[s1]], writes=[rpt[pi]])

                        def backC(it):
                            blk = it["blk"]
                            Tn, pi, j, h = blk["Tn"], it["pi"], it["j"], blk["h"]
                            ob = 6 + blk["bidx"] % 2
                            nb = h % 2

                            def pvm():
                                T.matmul(bank(ob)[:, 0:Tn], VC[nb][:, j, :], PT[pi][:, 0, 0:Tn], start=it["first"], stop=False)
                                return T.matmul(bank(ob)[:, 0:Tn], VC[nb][:, j + 1, :], PT[pi][:, 1, 0:Tn], start=False, stop=it["last"])
                            k.op(pe, pvm, reads=[rpt[pi], rvc[nb]], writes=[BR[ob]])
                            if it["last"]:
                                finalize(ob, nb, 4 + h // 2, blk["c0"], Tn)
                        run_pipeline(itemsC, frontC, backC, depth=2)
                        k.barrier()
                        if check_stop('c', l):
                            return

                    with ExitStack() as ph:
                        WOUT = sbt(ph, "wout", [128, 8, D], BF16)
                        SQ2 = [sbt(ph, f"sq{i}", [128, 8, 512], F32) for i in range(2)]
                        RS2 = [sbt(ph, f"rs{i}", [128, 512], F32) for i in range(2)]
                        lnr2 = [([Res(f"sq{i}") for i in range(8)], Res("rs")) for _ in range(2)]
                        rwo = Res("wout")
                        k.dma(pool, [(WOUT[:, 0:4, :], w_outx[l].rearrange("(c p) n -> p c n", p=128)[:, 0:4, :]),
                                     (WOUT[:, 4:8, :], w_outx[l].rearrange("(c p) n -> p c n", p=128)[:, 4:8, :])], writes=[rwo])
                        pend = None
                        for ci, (c0, Tn) in enumerate(CHUNKS):
                            if ci == 4 and last:
                                continue
                            isctx = 1 if ci == 4 else 0
                            rz = [Res(f"z{c}") for c in range(8)]
                            for oc in range(8):
                                bi = oc % 4
                                mm_group(bank(bi)[:, 0:Tn], [(WOUT[:, ic, oc * 128:(oc + 1) * 128], ATT[:, ic, c0:c0 + Tn]) for ic in range(8)],
                                         reads=[rwo], writes=[BR[bi]])
                                k.op(dve, lambda oc=oc, bi=bi: V.scalar_tensor_tensor(XS[:, oc, c0:c0 + Tn], bank(bi)[:, 0:Tn],
                                                                                       MOD[isctx][:, 16 + oc:17 + oc], XS[:, oc, c0:c0 + Tn], ALU.mult, ALU.add),
                                     reads=[BR[bi]], writes=[rz[oc]])
                            pp_ = ci % 2
                            largs = (k, nc, bank, BR, XS, onesm, epsc, r_const, c0, Tn, rz, SQ2[pp_], RS2[pp_], DER[:, 32:40], DER[:, 40:48], lnr2[pp_])
                            lkw = dict(mb=6 - 2 * pp_, vb=7 - 2 * pp_)
                            _ln_stage1(*largs, **lkw)
                            if pend is not None:
                                _ln_stage2(*pend[0], **pend[1])
                            pend = (largs, lkw)
                        if pend is not None:
                            _ln_stage2(*pend[0], **pend[1])
                        k.barrier()
                        if check_stop('p3', l):
                            return

                sblocks = [[CHUNKS[0], CHUNKS[1]], [CHUNKS[2], CHUNKS[3]] + ([] if last else [CHUNKS[4]])]
                with ExitStack() as p4:
                    W1S = [sbt(p4, f"w1s{i}", [128, 8, 512], BF16) for i in range(2)]
                    W2S = [sbt(p4, f"w2s{i}", [128, 32, 128], BF16) for i in range(2)]
                    rw1 = [Res("w1s0"), Res("w1s1")]
                    rw2 = [Res("w2s0"), Res("w2s1")]
                    nsb = len(sblocks)

                    def load_w1(idx):
                        if idx >= nsb * 8:
                            return
                        fs, wb = idx % 8, idx % 2
                        k.dma(pool, [(W1S[wb][:, 0:4, :], w_fc1[l, fs][:, 0:4, :]), (W1S[wb][:, 4:8, :], w_fc1[l, fs][:, 4:8, :])], writes=[rw1[wb]])

                    def load_w2(idx):
                        if idx >= nsb * 8:
                            return
                        oc, wb = idx % 8, idx % 2
                        k.dma(pool, [(W2S[wb][:, 0:16, :], w_fc2[l, oc][:, 0:16, :]), (W2S[wb][:, 16:32, :], w_fc2[l, oc][:, 16:32, :])], writes=[rw2[wb]])
                    load_w1(0)
                    load_w1(1)
                    load_w2(0)
                    load_w2(1)
                    for sbi, sbk_ in enumerate(sblocks):
                        base = sbk_[0][0]
                        rzs = {}
                        with ExitStack() as ph:
                            H2 = sbt(ph, "h2", [128, 8, 1088], BF16)
                            HID = sbt(ph, "hid", [128, 32, 1088], BF16)
                            RLU = [sbt(ph, f"rlu{i}", [128, 512], F32) for i in range(2)]
                            rh2 = [Res(f"h2_{c}") for c in range(8)]
                            rhid = [Res(f"hid{c}") for c in range(32)]
                            rrlu = [Res("rlu0"), Res("rlu1")]
                            for (c0, Tn) in sbk_:
                                isctx = 1 if c0 == NL else 0
                                for c in range(8):
                                    ew_ts(dve if c % 2 == 0 else pool, H2[:, c, c0 - base:c0 - base + Tn], XS[:, c, c0:c0 + Tn], A2[isctx][:, c:c + 1],
                                          MOD[isctx][:, 24 + c:25 + c], ALU.mult, ALU.add, writes=[rh2[c]])
                            bi_ = 0
                            ru = 0
                            for fs in range(8):
                                idx = sbi * 8 + fs
                                wb = idx % 2
                                for fj in range(4):
                                    fc = fs * 4 + fj
                                    for (c0, Tn) in sbk_:
                                        o0 = c0 - base
                                        bi = bi_ % 4
                                        bi_ += 1
                                        mm_group(bank(bi)[:, 0:Tn], [(W1S[wb][:, kc, fj * 128:(fj + 1) * 128], H2[:, kc, o0:o0 + Tn]) for kc in range(8)],
                                                 reads=[rw1[wb]] + rh2, writes=[BR[bi]])
                                        ri = ru % 2
                                        ru += 1
                                        k.op(act, lambda bi=bi, ri=ri, Tn=Tn: S.activation(RLU[ri][:, 0:Tn], bank(bi)[:, 0:Tn], AF.Relu),
                                             reads=[BR[bi]], writes=[rrlu[ri]])
                                        ew_tt(dve if fc % 2 == 0 else pool, HID[:, fc, o0:o0 + Tn], RLU[ri][:, 0:Tn], RLU[ri][:, 0:Tn], ALU.mult,
                                              reads=[rrlu[ri]], writes=[rhid[fc]])
                                load_w1(idx + 2)
                            for oc in range(8):
                                idx = sbi * 8 + oc
                                wb = idx % 2
                                for (c0, Tn) in sbk_:
                                    isctx = 1 if c0 == NL else 0
                                    o0 = c0 - base
                                    bi = 4 + bi_ % 4
                                    bi_ += 1
                                    mm_group(bank(bi)[:, 0:Tn], [(W2S[wb][:, fc, :], HID[:, fc, o0:o0 + Tn]) for fc in range(32)],
                                             reads=[rw2[wb]] + rhid, writes=[BR[bi]])
                                    rzs[(c0, oc)] = Res("z")
                                    k.op(dve, lambda oc=oc, bi=bi, c0=c0, Tn=Tn, isctx=isctx: V.scalar_tensor_tensor(
                                        XS[:, oc, c0:c0 + Tn], bank(bi)[:, 0:Tn], MOD[isctx][:, 40 + oc:41 + oc], XS[:, oc, c0:c0 + Tn], ALU.mult, ALU.add),
                                        reads=[BR[bi]], writes=[rzs[(c0, oc)]])
                                load_w2(idx + 2)
                            k.barrier()
                        with ExitStack() as ph:
                            SQ2 = [sbt(ph, f"sq2{i}", [128, 8, 512], F32) for i in range(2)]
                            RS2 = [sbt(ph, f"rs2{i}", [128, 512], F32) for i in range(2)]
                            lnr2 = [([Res(f"sq{i}") for i in range(8)], Res("rs")) for _ in range(2)]
                            pend = None
                            for cj, (c0, Tn) in enumerate(sbk_):
                                rz = [rzs[(c0, oc)] for oc in range(8)]
                                pp_ = cj % 2
                                largs = (k, nc, bank, BR, XS, onesm, epsc, r_const, c0, Tn, rz, SQ2[pp_], RS2[pp_], DER[:, 48:56], DER[:, 56:64], lnr2[pp_])
                                lkw = dict(mb=6 - 2 * pp_, vb=7 - 2 * pp_)
                                _ln_stage1(*largs, **lkw)
                                if pend is not None:
                                    _ln_stage2(*pend[0], **pend[1])
                                pend = (largs, lkw)
                            if pend is not None:
                                _ln_stage2(*pend[0], **pend[1])
                            k.barrier()

            run_layers()
            with ExitStack() as ph:
                YST = [sbt(ph, f"yst{i}", [128, D], F32) for i in range(2)]
                ry = [Res("yst0"), Res("yst1")]
                rout = Res("yout")
                for t in range(16):
                    buf = t % 2
                    for half in range(2):
                        bi = (t * 2 + half) % 4

                        def tr(bi=bi, half=half, t=t):
                            inst = None
                            for c in range(4):
                                inst = T.transpose(bank(bi)[:, c * 128:(c + 1) * 128], XS[:, half * 4 + c, t * 128:(t + 1) * 128], ident[:])
                            return inst
                        k.op(pe, tr, reads=[r_const], writes=[BR[bi]])
                        if half == 0:
                            k.op(act, lambda bi=bi, buf=buf, half=half: S.copy(YST[buf][:, half * 512:(half + 1) * 512], bank(bi)[:, 0:512]),
                                 reads=[BR[bi]], writes=[ry[buf]])
                        else:
                            k.op(dve, lambda bi=bi, buf=buf, half=half: V.tensor_copy(YST[buf][:, half * 512:(half + 1) * 512], bank(bi)[:, 0:512]),
                                 reads=[BR[bi]], writes=[ry[buf]])
                    k.dma(sp, [(y_out[t * 128:(t + 1) * 128, :], YST[buf][:])], reads=[ry[buf]], writes=[rout], sem_res=ry[buf])
                k.wait_all_dma(sp)
                k.barrier()
        print("nsem", k.nsem, "counts", {e.name: e.count for e in k.engs})
    return nc


def _ln_stage1(k, nc, bank, BR, XS, onesm, epsc, r_const, c0, Tn, rz, SQ, RS, Gc, Bc, lnr, mb=6, vb=7):
    V, S, G, T = nc.vector, nc.scalar, nc.gpsimd, nc.tensor
    pe, act, dve, pool = k.pe, k.act, k.dve, k.pool

    def mean_mm():
        inst = None
        for oc in range(8):
            inst = T.matmul(bank(mb)[:, 0:Tn], onesm[:], XS[:, oc, c0:c0 + Tn], start=(oc == 0), stop=(oc == 7))
        return inst
    k.op(pe, mean_mm, reads=list(rz) + [r_const], writes=[BR[mb]])
    rsq = lnr[0]
    for oc in range(8):
        k.op(dve, lambda oc=oc: V.tensor_tensor(XS[:, oc, c0:c0 + Tn], XS[:, oc, c0:c0 + Tn], bank(mb)[:, 0:Tn], op=ALU.subtract),
             reads=[rz[oc], BR[mb]], writes=[rz[oc]])
        k.op(pool, lambda oc=oc: G.tensor_tensor(SQ[:, oc, 0:Tn], XS[:, oc, c0:c0 + Tn], XS[:, oc, c0:c0 + Tn], op=ALU.mult),
             reads=[rz[oc]], writes=[rsq[oc]])


def _ln_stage2(k, nc, bank, BR, XS, onesm, epsc, r_const, c0, Tn, rz, SQ, RS, Gc, Bc, lnr, mb=6, vb=7):
    V, S, G, T = nc.vector, nc.scalar, nc.gpsimd, nc.tensor
    pe, act, dve, pool = k.pe, k.act, k.dve, k.pool
    rsq = lnr[0]

    def var_mm():
        inst = None
        for oc in range(8):
            inst = T.matmul(bank(vb)[:, 0:Tn], onesm[:], SQ[:, oc, 0:Tn], start=(oc == 0), stop=(oc == 7))
        return inst
    k.op(pe, var_mm, reads=rsq + [r_const], writes=[BR[vb]])
    rrs = lnr[1]
    k.op(act, lambda: S.activation(RS[:, 0:Tn], bank(vb)[:, 0:Tn], AF.Ln, bias=epsc[:], scale=1.0), reads=[BR[vb], r_const], writes=[rrs])
    k.op(act, lambda: S.activation(RS[:, 0:Tn], RS[:, 0:Tn], AF.Exp, scale=-0.5), reads=[rrs], writes=[rrs])
    for oc in range(8):
        k.op(dve, lambda oc=oc: V.scalar_tensor_tensor(XS[:, oc, c0:c0 + Tn], XS[:, oc, c0:c0 + Tn], Gc[:, oc:oc + 1], RS[:, 0:Tn], ALU.mult, ALU.mult),
             reads=[rz[oc], rrs], writes=[rz[oc]])
        k.op(act, lambda oc=oc: S.activation(XS[:, oc, c0:c0 + Tn], XS[:, oc, c0:c0 + Tn], AF.Identity, bias=Bc[:, oc:oc + 1], scale=1.0),
             reads=[rz[oc]], writes=[rz[oc]])


def _rope_tables(qd):
    GRID_W = 64
    pos = np.arange(NL, dtype=np.int64) + qd * NL
    row = (pos // GRID_W).astype(np.float32)
    col = (pos % GRID_W).astype(np.float32)

    def tab(rot_dim):
        nf = rot_dim // 4
        inv = (np.float32(10000.0) ** (-np.arange(nf, dtype=np.float32) / np.float32(nf))).astype(np.float32)
        ang = np.concatenate([row[:, None] * inv, col[:, None] * inv], axis=-1).astype(np.float32)
        c = np.cos(ang).astype(np.float32).T
        s = np.sin(ang).astype(np.float32).T
        C = np.concatenate([c, c], 0)
        Sg = np.concatenate([-s, s], 0)
        Cf = np.ones((rot_dim, NT), np.float32)
        Sf = np.zeros((rot_dim, NT), np.float32)
        Cf[:, :NL] = C
        Sf[:, :NL] = Sg
        return Cf, Sf
    c64, s64 = tab(64)
    c32, s32 = tab(32)
    out = np.zeros((4, 128, NT), np.float32)
    out[0] = np.concatenate([c64, c64], 0)
    out[1] = np.concatenate([s64, s64], 0)
    out[2, 0:32] = c32
    out[2, 64:96] = c32
    out[3, 0:32] = s32
    out[3, 64:96] = s32
    return out


def _masks(qd):
    kk = np.arange(128)[:, None]
    qq = np.arange(128)[None, :]
    ML = (kk >= qq).astype(np.float32)
    MR = (kk <= qq).astype(np.float32)
    m = np.zeros((128, 10, 128), np.float32)
    m[:, 0] = ML
    m[:, 1] = MR
    for r in range(4):
        if r == qd - 1:
            m[:, 2 + r] = ML
        if r == qd + 1:
            m[:, 6 + r] = MR
    return m.astype(ml_dtypes.bfloat16)


def _prep_weights(w_in, w_uq, w_out, q_norm_b, k_norm_b, mla_q_norm, mla_kv_norm, ln1_g, ln1_b, ln2_g, ln2_b):
    def sw(cols, half):
        cols = np.asarray(cols).reshape(-1, 2, half)
        return cols[:, ::-1, :].reshape(-1)
    aq = np.arange(0, 256).reshape(4, 64)
    ak = np.arange(256, 384)
    av = np.arange(384, 512)
    bq = np.arange(512, 768).reshape(4, 64)
    bk = np.arange(768, 896)
    bv = np.arange(896, 1024)
    cq = np.arange(1024, 1280)
    ckv = np.arange(1280, 1408)
    ckr = np.arange(1408, 1440)
    gA = [np.concatenate([aq[0], aq[2]]), np.concatenate([aq[1], aq[3]]), ak]
    gB = [np.concatenate([bq[0], bq[2]]), np.concatenate([bq[1], bq[3]]), bk]
    groups = {}
    for i in range(3):
        groups[i] = gA[i]
        groups[3 + i] = sw(gA[i], 32)
        groups[6 + i] = gB[i]
        groups[9 + i] = sw(gB[i], 32)
    groups[12] = cq[:128]
    groups[13] = cq[128:]
    groups[14] = ckv
    cols = np.concatenate([groups[g] for g in range(15)] + [ckr, sw(ckr, 16), av, bv])
    assert cols.shape[0] == NWIN
    w_inx = np.ascontiguousarray(w_in[:, :, cols])
    uq_cols = np.arange(768).reshape(8, 96)
    uq_sw = uq_cols.copy()
    for h in range(8):
        uq_sw[h, 64:96] = sw(uq_cols[h, 64:96], 16)
    w_uqx = np.ascontiguousarray(np.concatenate([w_uq, w_uq[:, :, uq_sw.reshape(-1)]], axis=2))
    hr = lambda base, h: np.arange(base + h * 64, base + (h + 1) * 64)
    rows = np.concatenate([hr(0, 0), hr(0, 2), hr(0, 1), hr(0, 3), hr(256, 0), hr(256, 2), hr(256, 1), hr(256, 3), np.arange(512, 1024)])
    w_outx = np.ascontiguousarray(w_out[:, rows, :])
    Ln = w_in.shape[0]
    vecs = np.zeros((Ln, 40, 128), np.float32)
    vecs[:, 0:8] = ln1_g.reshape(Ln, 8, 128)
    vecs[:, 8:16] = ln1_b.reshape(Ln, 8, 128)
    vecs[:, 16:24] = ln2_g.reshape(Ln, 8, 128)
    vecs[:, 24:32] = ln2_b.reshape(Ln, 8, 128)
    vecs[:, 32:34] = mla_q_norm.reshape(Ln, 2, 128)
    vecs[:, 34] = mla_kv_norm
    swi = sw(np.arange(64), 32)
    vecs[:, 35] = np.concatenate([q_norm_b, q_norm_b], 1)
    vecs[:, 36] = np.concatenate([q_norm_b[:, swi], q_norm_b[:, swi]], 1)
    vecs[:, 37] = np.concatenate([k_norm_b, k_norm_b], 1)
    vecs[:, 38] = np.concatenate([k_norm_b[:, swi], k_norm_b[:, swi]], 1)
    return w_inx, w_uqx, w_outx, vecs


_CACHE = {}


def kernel(x, c, ctx, c_ctx, w_mod, b_mod, w_in, sink_a, q_norm_b, k_norm_b, mla_q_norm, mla_kv_norm,
           w_uq, w_uk, w_uv, w_out, ln1_g, ln1_b, w_fc1, w_fc2, ln2_g, ln2_b):
    f = lambda a: np.ascontiguousarray(np.asarray(a, dtype=np.float32))
    x, c, ctx, c_ctx = f(x), f(c), f(ctx), f(c_ctx)
    w_mod, b_mod, w_in, sink_a = f(w_mod), f(b_mod), f(w_in), f(sink_a)
    w_uq, w_uk, w_uv, w_out, w_fc1, w_fc2 = f(w_uq), f(w_uk), f(w_uv), f(w_out), f(w_fc1), f(w_fc2)
    w_fc1 = np.ascontiguousarray(w_fc1.reshape(L, 8, 128, 8, 512).transpose(0, 3, 2, 1, 4))
    w_fc2 = np.ascontiguousarray(w_fc2.reshape(L, 32, 128, 8, 128).transpose(0, 3, 2, 1, 4))
    w_inx, w_uqx, w_outx, vecs = _prep_weights(w_in, w_uq, w_out, f(q_norm_b), f(k_norm_b), f(mla_q_norm), f(mla_kv_norm),
                                               f(ln1_g), f(ln1_b), f(ln2_g), f(ln2_b))
    w_mod_sh, b_mod_sh = [], []
    for qd_ in range(4):
        fch = [s_ * 8 + 2 * qd_ + j_ for s_ in range(6) for j_ in range(2)]
        cols = np.concatenate([np.arange(f_ * 128, (f_ + 1) * 128) for f_ in fch])
        w_mod_sh.append(np.ascontiguousarray(w_mod[:, :, cols]))
        b_mod_sh.append(np.ascontiguousarray(b_mod[:, cols].reshape(L * 12, 128)))
    if "nc" not in _CACHE:
        _CACHE["nc"] = build_program()
    nc = _CACHE["nc"]
    in_maps = []
    for core in range(8):
        b, qd = core // 4, core % 4
        cv = np.stack([c[b].reshape(8, 128), c_ctx.reshape(8, 128)], axis=1).reshape(16, 128)
        in_maps.append({
            "x": np.ascontiguousarray(x[b, qd * NL:(qd + 1) * NL]),
            "ctx": np.ascontiguousarray(ctx[b, qd * NCX:(qd + 1) * NCX]),
            "cvec": np.ascontiguousarray(cv),
            "w_mod": w_mod_sh[qd], "b_mod": b_mod_sh[qd],
            "w_inx": w_inx, "w_uqx": w_uqx, "w_uk": w_uk, "w_uv": w_uv, "w_outx": w_outx,
            "w_fc1": w_fc1, "w_fc2": w_fc2, "vecs": vecs, "sink": sink_a.reshape(L, 1, 4),
            "ropes": _rope_tables(qd), "amask": _masks(qd),
        })
    res = run_bass_kernel_spmd(nc, in_maps, core_ids=list(range(8)))
    out = np.zeros((2, 8192, D), np.float32)
    for core in range(8):
        b, qd = core // 4, core % 4
        out[b, qd * NL:(qd + 1) * NL] = res.results[core]["y"]
    return out
```

```python
import numpy as np
import ml_dtypes
from contextlib import ExitStack
import concourse.bass as bass
import concourse.mybir as mybir
from concourse.bass_utils import run_bass_kernel_spmd

F32 = mybir.dt.float32
BF16 = mybir.dt.bfloat16
AF = mybir.ActivationFunctionType
ALU = mybir.AluOpType

L = 2
D = 1024
NL = 2048
NCX = 64
NT = NL + NCX
ALPHA = float((2 * L) ** 0.25)
EPS = 1e-6
EPOCH = 8000
NWIN = 15 * 128 + 64 + 256
GOFF = {g: g * 128 for g in range(15)}
KR_OFF = 1920
KRS_OFF = 1952
V_OFF = 1984
CHUNKS = [(0, 512), (512, 512), (1024, 512), (1536, 512), (2048, 64)]
KVROWS = 480
VREG = 288
NE = 320
DEBUG = {}


class _Stop(Exception):
    pass


def check_stop(name, l=0):
    return DEBUG.get("stop") == name and l == DEBUG.get("stop_layer", 0)


class Res:
    __slots__ = ("name", "w", "r", "dsem", "dcnt")

    def __init__(self, name=""):
        self.name = name
        self.w = None
        self.r = []
        self.dsem = None
        self.dcnt = 0


class Eng:
    def __init__(self, K, handle, name):
        self.K = K
        self.h = handle
        self.name = name
        self.sems = []
        self.count = 0
        self.waited = {}
        self.last = None

    def next_token(self):
        e = self.count // EPOCH
        while len(self.sems) <= e:
            self.sems.append(self.K.new_sem(f"{self.name}_e{len(self.sems)}"))
        v = self.count % EPOCH + 1
        self.count += 1
        self.last = (self.sems[e], v, self)
        return self.last

    def wait(self, tok):
        sem, val = tok[0], tok[1]
        key = id(sem)
        if self.waited.get(key, 0) >= val:
            return
        self.h.wait_ge(sem, val)
        self.waited[key] = val


class K:
    def __init__(self, nc, stack):
        self.nc = nc
        self.stack = stack
        self.nsem = 0
        self.pe = Eng(self, nc.tensor, "pe")
        self.act = Eng(self, nc.scalar, "act")
        self.dve = Eng(self, nc.vector, "dve")
        self.pool = Eng(self, nc.gpsimd, "pool")
        self.sp = Eng(self, nc.sync, "sp")
        self.engs = [self.pe, self.act, self.dve, self.pool, self.sp]
        self.dma_toks = {}
        self.extra_toks = []
        self.uid = 0

    def new_sem(self, name):
        self.nsem += 1
        return self.stack.enter_context(self.nc.semaphore(name))

    def _deps(self, eng, reads, writes):
        for r in reads:
            if r.w is not None:
                eng.wait(r.w)
        for w in writes:
            if w.w is not None:
                eng.wait(w.w)
            for t in w.r:
                eng.wait(t)

    def op(self, eng, build, reads=(), writes=()):
        self._deps(eng, reads, writes)
        inst = build()
        tok = eng.next_token()
        inst.then_inc(tok[0], 1)
        for r in reads:
            r.r.append(tok)
        for w in writes:
            w.w = tok
            w.r = []
        return tok

    def dma(self, q, pairs, reads=(), writes=(), sem_res=None):
        self._deps(q, reads, writes)
        sr = sem_res if sem_res is not None else writes[0]
        if sr.dsem is None:
            self.uid += 1
            sr.dsem = self.new_sem(f"d{self.uid}")
        if sr.dcnt > 0:
            q.wait((sr.dsem, sr.dcnt, None))
        for (o, i) in pairs:
            q.h.dma_start(out=o, in_=i).then_inc(sr.dsem, 16)
            sr.dcnt += 16
        tok = (sr.dsem, sr.dcnt, None)
        self.dma_toks[id(sr.dsem)] = tok
        for r in reads:
            r.r.append(tok)
        for w in writes:
            w.w = tok
            w.r = []
        return tok

    def wait_all_dma(self, eng):
        for t in self.dma_toks.values():
            eng.wait(t)

    def barrier(self):
        toks = [e.last for e in self.engs if e.last is not None]
        for e in self.engs:
            for t in toks:
                if e is not self.sp or t[2] is not e:
                    e.wait(t)
            for t in self.dma_toks.values():
                e.wait(t)
            for t in self.extra_toks:
                e.wait(t)


def build_program():
    nc = bass.Bass("TRN2", target_bir_lowering=False)

    def din(name, shape, dt=F32):
        return nc.dram_tensor(name, shape, dt, kind="ExternalInput").ap()

    x_in = din("x", [NL, D])
    ctx_in = din("ctx", [NCX, D])
    cvec = din("cvec", [16, 128])
    w_mod = din("w_mod", [L, D, 12 * 128])
    b_mod = din("b_mod", [L * 12, 128])
    w_inx = din("w_inx", [L, D, NWIN])
    w_uqx = din("w_uqx", [L, 256, 1536])
    w_uk = din("w_uk", [L, 128, 512])
    w_uv = din("w_uv", [L, 128, 512])
    w_outx = din("w_outx", [L, D, D])
    w_fc1 = din("w_fc1", [L, 8, 128, 8, 512])
    w_fc2 = din("w_fc2", [L, 8, 128, 32, 128])
    vecs = din("vecs", [L, 40, 128])
    sink = din("sink", [L, 1, 4])
    ropes = din("ropes", [4, 128, NT])
    amask = din("amask", [128, 10, 128], BF16)
    y_out = nc.dram_tensor("y", [NL, D], F32, kind="ExternalOutput").ap()
    QS = nc.dram_tensor("qs", [4 * 128 + 8 * 96, NT], BF16).ap()
    KVX = [nc.dram_tensor(f"kvx{c}", [KVROWS, Tc], BF16).ap() for c, (_, Tc) in enumerate(CHUNKS)]
    KVG = [nc.dram_tensor(f"kvg{c}", [4 * KVROWS, Tc], BF16).ap() for c, (_, Tc) in enumerate(CHUNKS)]
    XROW = {"kB": 0, "ckv": 128, "kr": 256}
    KAL = nc.dram_tensor("kal", [128, NT], BF16).ap()
    VAL = nc.dram_tensor("val", [NT, 192], BF16).ap()
    AEX = nc.dram_tensor("aex", [128 + 192, NE], BF16).ap()
    AEG = nc.dram_tensor("aeg", [4 * (128 + 192), NE], BF16).ap()
    MEX = nc.dram_tensor("mex", [128, L * 24], F32).ap()
    MEG = nc.dram_tensor("meg", [4 * 128, L * 24], F32).ap()

    def chunk_of(tok):
        return 4 if tok >= NL else tok // 512

    def dap(t, off, dims):
        return bass.AP(t.tensor, off, [list(d) for d in dims])

    with ExitStack() as stack:
        k = K(nc, stack)
        pe, act, dve, pool, sp = k.pe, k.act, k.dve, k.pool, k.sp
        V, S, G, T = nc.vector, nc.scalar, nc.gpsimd, nc.tensor

        uidc = [0]

        def sbt(st, name, shape, dt):
            uidc[0] += 1
            return st.enter_context(nc.sbuf_tensor(f"{name}_{uidc[0]}", shape, dt))

        XS = sbt(stack, "XS", [128, 8, NT], F32)
        ident = sbt(stack, "ident", [128, 128], F32)
        onesm = sbt(stack, "onesm", [128, 128], F32)
        ones1 = sbt(stack, "ones1", [128, 128], F32)
        bd64 = sbt(stack, "bd64", [128, 128], F32)
        epsc = sbt(stack, "epsc", [128, 1], F32)
        SL = sbt(stack, "SL", [128, 8, 2], BF16)
        VEC = sbt(stack, "VEC", [128, 40], F32)
        MODL = sbt(stack, "MODL", [128, 48], F32)
        MODC = sbt(stack, "MODC", [128, 48], F32)
        DER = sbt(stack, "DER", [128, 64], F32)
        ESROW = sbt(stack, "ESROW", [1, 4, 128], F32)
        SINKL = sbt(stack, "SINKL", [1, 192], F32)
        MG = sbt(stack, "MG", [128, 4, L * 24], F32)
        PP = [stack.enter_context(nc.psum_tensor(f"pp{i}", [128, 1024], F32)) for i in range(4)]

        def bank(i):
            return PP[i // 2][:, (i % 2) * 512:(i % 2) * 512 + 512]

        BR = [Res(f"bank{i}") for i in range(8)]
        rr = [0]

        def alt(*engs):
            rr[0] += 1
            return engs[rr[0] % len(engs)]

        def ew_tt(eng, out, in0, in1, op, reads=(), writes=()):
            h = eng.h
            return k.op(eng, lambda: h.tensor_tensor(out, in0, in1, op=op), reads, writes)

        def ew_ts(eng, out, in0, s1, s2, op0, op1, reads=(), writes=()):
            h = eng.h
            if s2 is None:
                return k.op(eng, lambda: h.tensor_scalar(out, in0, s1, None, op0), reads, writes)
            return k.op(eng, lambda: h.tensor_scalar(out, in0, s1, s2, op0, op1), reads, writes)

        def mm_group(out, pairs, reads, writes):
            n = len(pairs)

            def b():
                inst = None
                for i, (a, r) in enumerate(pairs):
                    inst = T.matmul(out, a, r, start=(i == 0), stop=(i == n - 1))
                return inst
            return k.op(pe, b, reads, writes)

        with nc.Block() as block:
            r_const = Res("const")
            k.op(pool, lambda: G.memset(ident[:], 0.0), writes=[r_const])
            k.op(pool, lambda: G.affine_select(ident[:], ident[:], pattern=[[-1, 128]], compare_op=ALU.not_equal,
                                               fill=1.0, base=0, channel_multiplier=1), reads=[r_const], writes=[r_const])
            k.op(pool, lambda: G.memset(onesm[:], 1.0 / D), writes=[r_const])
            k.op(pool, lambda: G.memset(ones1[:], 1.0), writes=[r_const])
            k.op(pool, lambda: G.memset(bd64[:], 0.0), writes=[r_const])
            k.op(pool, lambda: G.memset(bd64[0:64, 0:64], 1.0), writes=[r_const])
            k.op(pool, lambda: G.memset(bd64[64:128, 64:128], 1.0), writes=[r_const])
            k.op(pool, lambda: G.memset(epsc[:], EPS), writes=[r_const])
            k.op(pool, lambda: G.memset(SINKL[:], 0.0), writes=[r_const])
            k.op(pool, lambda: G.memset(SINKL[:, 64:128], 1.0), writes=[r_const])

            with ExitStack() as ph:
                XST = [sbt(ph, f"xst{i}", [128, D], F32) for i in range(2)]
                CVS = sbt(ph, "cvs", [16, 128], F32)
                CT = sbt(ph, "ct", [128, 16], F32)
                CT2 = sbt(ph, "ct2", [128, 16], F32)
                rx = [Res("xst0"), Res("xst1")]
                rcv = Res("cvs")
                k.dma(sp, [(CVS[:], cvec)], writes=[rcv])
                WMS = [sbt(ph, f"wms{i}", [128, 8, 1536], BF16) for i in range(L)]
                BMS = sbt(ph, "bms", [L * 12, 128], F32)
                BMT = sbt(ph, "bmt", [128, L * 12], F32)
                MSH = sbt(ph, "msh", [128, L * 12, 2], F32)
                rwms = [Res(f"wms{i}") for i in range(L)]
                rbms = Res("bms")
                for ll in range(L):
                    wsrc = w_mod[ll].rearrange("(c p) n -> p c n", p=128)
                    k.dma(pool, [(WMS[ll][:, 0:4, :], wsrc[:, 0:4, :]), (WMS[ll][:, 4:8, :], wsrc[:, 4:8, :])], writes=[rwms[ll]])
                k.dma(sp, [(BMS[:], b_mod)], writes=[rbms])
                k.op(pe, lambda: T.transpose(bank(4)[:, 0:16], CVS[:], ident[0:16, 0:16]), reads=[rcv, r_const], writes=[BR[4]])
                rct = Res("ct")
                k.op(dve, lambda: V.tensor_copy(CT[:], bank(4)[:, 0:16]), reads=[BR[4]], writes=[rct])
                k.op(act, lambda: S.activation(CT2[:], CT[:], AF.Exp, scale=-1.0), reads=[rct], writes=[rct])
                ew_ts(dve, CT2[:], CT2[:], 1.0, None, ALU.add, None, reads=[rct], writes=[rct])
                k.op(dve, lambda: V.reciprocal(CT2[:], CT2[:]), reads=[rct], writes=[rct])
                ew_tt(dve, SL[:].rearrange("p a b -> p (a b)"), CT[:], CT2[:], ALU.mult, reads=[rct], writes=[rct])
                ccm = k.new_sem("ccm")
                rmex = Res("mex")
                def emit_mod_shard():
                    k.op(pe, lambda: T.transpose(bank(5)[:, 0:L * 12], BMS[:], ident[0:L * 12, 0:L * 12]), reads=[rbms, r_const], writes=[BR[5]])
                    rbmt = Res("bmt")
                    k.op(dve, lambda: V.tensor_copy(BMT[:], bank(5)[:, 0:L * 12]), reads=[BR[5]], writes=[rbmt])
                    MP = bank(6)
                    for ll in range(L):
                        for t_ in range(12):
                            col = (ll * 12 + t_) * 2
                            mm_group(MP[:, col:col + 2], [(WMS[ll][:, kc, t_ * 128:(t_ + 1) * 128], SL[:, kc, :]) for kc in range(8)],
                                     reads=[rwms[ll], rct], writes=[BR[6]])
                    rmsh = Res("msh")
                    ew_tt(dve, MSH[:], MP[:, 0:L * 24].rearrange("p (f v) -> p f v", v=2), BMT[:].unsqueeze(2).to_broadcast([128, L * 12, 2]), ALU.add,
                          reads=[BR[6], rbmt], writes=[rmsh])
                    k.dma(pool, [(MEX, MSH[:].rearrange("p f v -> p (f v)"))], reads=[rmsh], writes=[rmex])
                    k.wait_all_dma(pool)
                    if not DEBUG.get("nocc"):
                        G.collective_compute("AllGather", ALU.bypass, replica_groups=[[0, 1, 2, 3], [4, 5, 6, 7]],
                                             ins=[MEX.opt()], outs=[MEG.opt()]).then_inc(ccm)
                for t in range(17):
                    if t == 8:
                        emit_mod_shard()
                    rows = 128 if t < 16 else 64
                    src = x_in[t * 128:(t + 1) * 128, :] if t < 16 else ctx_in
                    buf = t % 2
                    k.dma(sp, [(XST[buf][0:rows, :], src)], writes=[rx[buf]])
                    for half in range(2):
                        bi = (t * 2 + half) % 4
                        bk = bank(bi)

                        def tr(bk=bk, buf=buf, half=half, rows=rows):
                            inst = None
                            for c in range(4):
                                inst = T.transpose(bk[:, c * 128:c * 128 + rows],
                                                   XST[buf][0:rows, (half * 4 + c) * 128:(half * 4 + c + 1) * 128],
                                                   ident[0:rows, 0:rows])
                            return inst
                        k.op(pe, tr, reads=[rx[buf], r_const], writes=[BR[bi]])
                        src_v = bk.rearrange("p (c t) -> p c t", c=4)[:, :, 0:rows]
                        dst_v = XS[:, half * 4:half * 4 + 4, t * 128:t * 128 + rows]
                        if (t + half) % 2 == 0:
                            k.op(act, lambda s=src_v, d=dst_v: S.activation(d, s, AF.Identity, scale=ALPHA), reads=[BR[bi]])
                        else:
                            ew_ts(dve, dst_v, src_v, ALPHA, None, ALU.mult, None, reads=[BR[bi]])
                rmg = Res("mg")
                if DEBUG.get("nocc"):
                    rmeg = Res("megdbg")
                    k.dma(sp, [(MEG[r_ * 128:(r_ + 1) * 128, :], MEX) for r_ in range(4)], reads=[rmex], writes=[rmeg])
                    k.dma(sp, [(MG[:], MEG.rearrange("(r p) n -> p r n", p=128))], reads=[rmeg], writes=[rmg])
                else:
                    sp.wait((ccm, 1, None))
                    k.dma(sp, [(MG[:], MEG.rearrange("(r p) n -> p r n", p=128))], writes=[rmg])
                k.barrier()

            def run_layers():
              for l in range(L):
                last = (l == L - 1)
                if check_stop('setup', l):
                    return
                ph1 = ExitStack()
                WIN = sbt(ph1, "win", [128, 8, NWIN], BF16)
                WUQ = sbt(ph1, "wuq", [128, 2, 1536], BF16)
                rwin, rwuq = Res("win"), Res("wuq")
                wsrc = w_inx[l].rearrange("(c p) n -> p c n", p=128)
                k.dma(pool, [(WIN[:, 2 * i:2 * i + 2, :], wsrc[:, 2 * i:2 * i + 2, :]) for i in range(4)], writes=[rwin])
                k.dma(pool, [(WUQ[:], w_uqx[l].rearrange("(c p) n -> p c n", p=128))], writes=[rwuq])
                with ExitStack() as ph:
                    VST = sbt(ph, "vst", [40, 128], F32)
                    SKS = sbt(ph, "sks", [1, 4], F32)
                    rv = Res("vst")
                    k.dma(sp, [(VST[:], vecs[l]), (SKS[:], sink[l])], writes=[rv])
                    k.op(pe, lambda: T.transpose(bank(5)[:, 0:40], VST[:], ident[0:40, 0:40]), reads=[rv, r_const], writes=[BR[5]])
                    k.op(dve, lambda: V.tensor_copy(VEC[:], bank(5)[:, 0:40]), reads=[BR[5]])
                    k.op(act, lambda: S.activation(SKS[:], SKS[:], AF.Exp), reads=[rv], writes=[rv])
                    k.op(dve, lambda: V.tensor_copy(ESROW[:], SKS[:].unsqueeze(2).to_broadcast([1, 4, 128])), reads=[rv])
                    for r_ in range(4):
                        for v_, dstt in ((0, MODL), (1, MODC)):
                            srcv = MG[:, r_, l * 24:(l + 1) * 24].rearrange("p (s j v) -> p s j v", s=6, j=2, v=2)[:, :, :, v_]
                            dstv = dstt[:].rearrange("p (s r j) -> p s r j", s=6, r=4, j=2)[:, :, r_, :]
                            k.op(dve, lambda srcv=srcv, dstv=dstv: V.tensor_copy(dstv, srcv), reads=[])
                    k.barrier()
                    ew_ts(dve, DER[:, 0:8], MODL[:, 8:16], 1.0, 1.0 / ALPHA, ALU.add, ALU.mult)
                    ew_ts(dve, DER[:, 8:16], MODC[:, 8:16], 1.0, 1.0 / ALPHA, ALU.add, ALU.mult)
                    ew_ts(dve, DER[:, 16:24], MODL[:, 32:40], 1.0, 1.0 / ALPHA, ALU.add, ALU.mult)
                    ew_ts(dve, DER[:, 24:32], MODC[:, 32:40], 1.0, 1.0 / ALPHA, ALU.add, ALU.mult)
                    ew_ts(dve, DER[:, 32:48], VEC[:, 0:16], ALPHA, None, ALU.mult, None)
                    s2 = 1.0 if last else ALPHA
                    ew_ts(dve, DER[:, 48:64], VEC[:, 16:32], s2, None, ALU.mult, None)
                    k.barrier()
                    if check_stop('mod', l):
                        return
                A1 = (DER[:, 0:8], DER[:, 8:16])
                A2 = (DER[:, 16:24], DER[:, 24:32])
                MOD = (MODL, MODC)

                with ExitStack() as ph:
                    HT = [sbt(ph, f"ht{i}", [128, 8, 512], BF16) for i in range(2)]
                    RT = [sbt(ph, f"rt{i}", [128, 4, 512], F32) for i in range(2)]
                    OST = [sbt(ph, f"ost{i}", [128, 512], BF16) for i in range(4)]
                    TMP = [sbt(ph, f"tmp{i}", [128, 512], F32) for i in range(6)]
                    CQN = sbt(ph, "cqn", [128, 2, 512], BF16)
                    VSTG = sbt(ph, "vstg", [128, 4, 384], BF16)
                    rht = [[Res(f"ht0_{c}") for c in range(8)], [Res(f"ht1_{c}") for c in range(8)]]
                    rrt = [Res("rt0"), Res("rt1")]
                    rost = [Res(f"ost{i}") for i in range(4)]
                    rtmp = [Res(f"tmp{i}") for i in range(6)]
                    rcqn, rvstg = Res("cqn"), Res("vstg")
                    rkvx = Res("kvx")
                    k.op(dve, lambda: V.memset(VSTG[:, :, 64:128], 1.0), writes=[rvstg])
                    k.op(dve, lambda: V.memset(VSTG[:, :, 256:320], 1.0), writes=[rvstg])
                    ost_i = [0]
                    tmp_i = [0]
                    ccs = {c_: k.new_sem(f"cc{l}_{c_}") for c_ in list(range(len(CHUNKS))) + ["edge"]}

                    def next_ost():
                        ost_i[0] = (ost_i[0] + 1) % 4
                        return ost_i[0]

                    def next_tmp():
                        tmp_i[0] = (tmp_i[0] + 1) % 6
                        return tmp_i[0]

                    gq = VEC[:, 35:36]; gqs = VEC[:, 36:37]; gk = VEC[:, 37:38]; gks = VEC[:, 38:39]
                    pbank = [0]

                    def proj(col, M, hb, Tn):
                        bi = pbank[0]
                        pbank[0] = (pbank[0] + 1) % 6
                        mm_group(bank(bi)[0:M, 0:Tn], [(WIN[:, kc, col:col + M], HT[hb][:, kc, 0:Tn]) for kc in range(8)],
                                 reads=[rwin] + rht[hb], writes=[BR[bi]])
                        return bi

                    def rstd_from(bi_list, ones_mat, scale, Tn, M=128):
                        sqs = []
                        for bi in bi_list:
                            ti = next_tmp()
                            k.op(act, lambda bi=bi, ti=ti: S.activation(TMP[ti][:, 0:Tn], bank(bi)[:, 0:Tn], AF.Square),
                                 reads=[BR[bi]], writes=[rtmp[ti]])
                            sqs.append(ti)
                        mm_group(bank(6)[:, 0:Tn], [(ones_mat, TMP[ti][:, 0:Tn]) for ti in sqs],
                                 reads=[rtmp[ti] for ti in sqs] + [r_const], writes=[BR[6]])
                        tr_ = next_tmp()
                        k.op(act, lambda: S.activation(TMP[tr_][:, 0:Tn], bank(6)[:, 0:Tn], AF.Ln, bias=epsc[:], scale=scale),
                             reads=[BR[6], r_const], writes=[rtmp[tr_]])
                        k.op(act, lambda: S.activation(TMP[tr_][:, 0:Tn], TMP[tr_][:, 0:Tn], AF.Exp, scale=-0.5),
                             reads=[rtmp[tr_]], writes=[rtmp[tr_]])
                        return tr_

                    def store(oi, M, Tn, dst_rows, c0):
                        if isinstance(dst_rows, str):
                            dst = KVX[chunk_of(c0)][XROW[dst_rows]:XROW[dst_rows] + M, 0:Tn]
                        else:
                            dst = dst_rows[:, c0:c0 + Tn]
                        k.dma(sp, [(dst, OST[oi][0:M, 0:Tn])], reads=[rost[oi]], writes=[], sem_res=rost[oi])

                    for pos_, ci in enumerate((3, 4, 0, 1, 2)):
                        c0, Tn = CHUNKS[ci]
                        hb = pos_ % 2
                        isctx = 1 if ci == 4 else 0
                        for c in range(8):
                            ew_ts(dve if c % 2 == 0 else pool, HT[hb][:, c, 0:Tn], XS[:, c, c0:c0 + Tn], A1[isctx][:, c:c + 1],
                                  MOD[isctx][:, c:c + 1], ALU.mult, ALU.add, writes=[rht[hb][c]])
                        k.dma(sp, [(RT[hb][:, :, 0:Tn], ropes[:, :, c0:c0 + Tn].rearrange("a p t -> p a t"))], writes=[rrt[hb]])
                        C64 = RT[hb][:, 0, 0:Tn]; S64 = RT[hb][:, 1, 0:Tn]
                        C32 = RT[hb][:, 2, 0:Tn]; S32 = RT[hb][:, 3, 0:Tn]
                        for gi, dst in ((0, QS[0:128, :]), (1, QS[128:256, :]), (2, KAL)):
                            b0 = proj(GOFF[gi], 128, hb, Tn)
                            b1 = proj(GOFF[gi + 3], 128, hb, Tn)
                            t1 = next_tmp(); t2 = next_tmp(); oi = next_ost()
                            ew_tt(dve, TMP[t1][:, 0:Tn], bank(b0)[:, 0:Tn], C64, ALU.mult, reads=[BR[b0], rrt[hb]], writes=[rtmp[t1]])
                            ew_tt(dve, TMP[t2][:, 0:Tn], bank(b1)[:, 0:Tn], S64, ALU.mult, reads=[BR[b1], rrt[hb]], writes=[rtmp[t2]])
                            ew_tt(pool, OST[oi][:, 0:Tn], TMP[t1][:, 0:Tn], TMP[t2][:, 0:Tn], ALU.add,
                                  reads=[rtmp[t1], rtmp[t2]], writes=[rost[oi]])
                            store(oi, 128, Tn, dst, c0)
                        for gi, dst, g_, gs_ in ((6, QS[256:384, :], gq, gqs), (7, QS[384:512, :], gq, gqs), (8, 'kB', gk, gks)):
                            b0 = proj(GOFF[gi], 128, hb, Tn)
                            b1 = proj(GOFF[gi + 3], 128, hb, Tn)
                            tr_ = rstd_from([b0], bd64[:], 1.0 / 64, Tn)
                            t1 = next_tmp(); t2 = next_tmp(); oi = next_ost()
                            k.op(dve, lambda: V.scalar_tensor_tensor(TMP[t1][:, 0:Tn], bank(b0)[:, 0:Tn], g_, TMP[tr_][:, 0:Tn], ALU.mult, ALU.mult),
                                 reads=[BR[b0], rtmp[tr_]], writes=[rtmp[t1]])
                            k.op(dve, lambda: V.scalar_tensor_tensor(TMP[t2][:, 0:Tn], bank(b1)[:, 0:Tn], gs_, TMP[tr_][:, 0:Tn], ALU.mult, ALU.mult),
                                 reads=[BR[b1], rtmp[tr_]], writes=[rtmp[t2]])
                            ew_tt(pool, TMP[t1][:, 0:Tn], TMP[t1][:, 0:Tn], C64, ALU.mult, reads=[rtmp[t1], rrt[hb]], writes=[rtmp[t1]])
                            ew_tt(pool, TMP[t2][:, 0:Tn], TMP[t2][:, 0:Tn], S64, ALU.mult, reads=[rtmp[t2], rrt[hb]], writes=[rtmp[t2]])
                            ew_tt(pool, OST[oi][:, 0:Tn], TMP[t1][:, 0:Tn], TMP[t2][:, 0:Tn], ALU.add,
                                  reads=[rtmp[t1], rtmp[t2]], writes=[rost[oi]])
                            store(oi, 128, Tn, dst, c0)
                        b0 = proj(GOFF[14], 128, hb, Tn)
                        tr_ = rstd_from([b0], ones1[:], 1.0 / 128, Tn)
                        oi = next_ost()
                        k.op(dve, lambda: V.scalar_tensor_tensor(OST[oi][:, 0:Tn], bank(b0)[:, 0:Tn], VEC[:, 34:35], TMP[tr_][:, 0:Tn], ALU.mult, ALU.mult),
                             reads=[BR[b0], rtmp[tr_]], writes=[rost[oi]])
                        store(oi, 128, Tn, 'ckv', c0)
                        b0 = proj(KR_OFF, 32, hb, Tn)
                        b1 = proj(KRS_OFF, 32, hb, Tn)
                        t1 = next_tmp(); t2 = next_tmp(); oi = next_ost()
                        ew_tt(dve, TMP[t1][0:32, 0:Tn], bank(b0)[0:32, 0:Tn], C32[0:32], ALU.mult, reads=[BR[b0], rrt[hb]], writes=[rtmp[t1]])
                        ew_tt(dve, TMP[t2][0:32, 0:Tn], bank(b1)[0:32, 0:Tn], S32[0:32], ALU.mult, reads=[BR[b1], rrt[hb]], writes=[rtmp[t2]])
                        ew_tt(pool, OST[oi][0:32, 0:Tn], TMP[t1][0:32, 0:Tn], TMP[t2][0:32, 0:Tn], ALU.add,
                              reads=[rtmp[t1], rtmp[t2]], writes=[rost[oi]])
                        store(oi, 32, Tn, 'kr', c0)
                        b0 = proj(GOFF[12], 128, hb, Tn)
                        b1 = proj(GOFF[13], 128, hb, Tn)
                        tr_ = rstd_from([b0, b1], ones1[:], 1.0 / 256, Tn)
                        k.op(dve, lambda: V.scalar_tensor_tensor(CQN[:, 0, 0:Tn], bank(b0)[:, 0:Tn], VEC[:, 32:33], TMP[tr_][:, 0:Tn], ALU.mult, ALU.mult),
                             reads=[BR[b0], rtmp[tr_]], writes=[rcqn])
                        k.op(dve, lambda: V.scalar_tensor_tensor(CQN[:, 1, 0:Tn], bank(b1)[:, 0:Tn], VEC[:, 33:34], TMP[tr_][:, 0:Tn], ALU.mult, ALU.mult),
                             reads=[BR[b1], rtmp[tr_]], writes=[rcqn])
                        for h in range(8):
                            bq = pbank[0]; pbank[0] = (pbank[0] + 1) % 6
                            bs = pbank[0]; pbank[0] = (pbank[0] + 1) % 6
                            mm_group(bank(bq)[0:96, 0:Tn], [(WUQ[:, kc, h * 96:(h + 1) * 96], CQN[:, kc, 0:Tn]) for kc in range(2)],
                                     reads=[rwuq, rcqn], writes=[BR[bq]])
                            mm_group(bank(bs)[0:96, 0:Tn], [(WUQ[:, kc, 768 + h * 96:768 + (h + 1) * 96], CQN[:, kc, 0:Tn]) for kc in range(2)],
                                     reads=[rwuq, rcqn], writes=[BR[bs]])
                            t1 = next_tmp(); t2 = next_tmp(); oi = next_ost()
                            k.op(act, lambda: S.copy(OST[oi][0:64, 0:Tn], bank(bq)[0:64, 0:Tn]), reads=[BR[bq]], writes=[rost[oi]])
                            ew_tt(dve, TMP[t1][64:96, 0:Tn], bank(bq)[64:96, 0:Tn], C32[64:96], ALU.mult, reads=[BR[bq], rrt[hb]], writes=[rtmp[t1]])
                            ew_tt(dve, TMP[t2][64:96, 0:Tn], bank(bs)[64:96, 0:Tn], S32[64:96], ALU.mult, reads=[BR[bs], rrt[hb]], writes=[rtmp[t2]])
                            ew_tt(pool, OST[oi][64:96, 0:Tn], TMP[t1][64:96, 0:Tn], TMP[t2][64:96, 0:Tn], ALU.add,
                                  reads=[rtmp[t1], rtmp[t2]], writes=[rost[oi]])
                            store(oi, 96, Tn, QS[512 + h * 96:512 + (h + 1) * 96, :], c0)
                        ntile = (Tn + 127) // 128
                        for tt in range(ntile):
                            rows = min(128, Tn - tt * 128)
                            mm_group(bank(7)[0:rows, 0:256], [(HT[hb][:, kc, tt * 128:tt * 128 + rows], WIN[:, kc, V_OFF:V_OFF + 256]) for kc in range(8)],
                                     reads=[rwin] + rht[hb], writes=[BR[7]])
                            for mx in range(2):
                                srcv = bank(7)[0:rows, mx * 128:(mx + 1) * 128].rearrange("p (a b) -> p a b", a=2)
                                dstv = VSTG[0:rows, tt, mx * 192:(mx + 1) * 192].rearrange("p (a b) -> p a b", a=3)[:, 0:3:2, :]
                                k.op(act, lambda s=srcv, d=dstv: S.copy(d, s), reads=[BR[7]], writes=[rvstg])
                        rows = min(128, Tn)
                        vdst = dap(KVX[ci], VREG * Tn, [[192, rows], [128 * 192, ntile], [1, 192]])
                        vadst = dap(VAL, c0 * 192, [[192, rows], [128 * 192, ntile], [1, 192]])
                        k.dma(sp, [(vdst, VSTG[0:rows, 0:ntile, 192:384]), (vadst, VSTG[0:rows, 0:ntile, 0:192])],
                              reads=[rvstg], writes=[], sem_res=rvstg)
                        k.wait_all_dma(pool)
                        if DEBUG.get("nocc"):
                            k.wait_all_dma(sp)
                            k.dma(sp, [(KVG[ci][r_ * KVROWS:(r_ + 1) * KVROWS, :], KVX[ci]) for r_ in range(4)], writes=[Res("kvgdbg")])
                        else:
                            G.collective_compute("AllGather", ALU.bypass, replica_groups=[[0, 1, 2, 3], [4, 5, 6, 7]],
                                                 ins=[KVX[ci].opt()], outs=[KVG[ci].opt()]).then_inc(ccs[ci])
                        if pos_ == 2:
                            k.wait_all_dma(sp)
                            redge = Res("aex")
                            k.dma(sp, [(AEX[0:128, 0:128], KAL[:, 0:128]), (AEX[0:128, 128:256], KAL[:, NL - 128:NL]), (AEX[0:128, 256:NE], KAL[:, NL:NT]),
                                       (dap(AEX, 128 * NE, [[192, 128], [1, 192]]), VAL[0:128, :]),
                                       (dap(AEX, 128 * NE + 128 * 192, [[192, 128], [1, 192]]), VAL[NL - 128:NL, :]),
                                       (dap(AEX, 128 * NE + 256 * 192, [[192, 64], [1, 192]]), VAL[NL:NT, :])], writes=[redge])
                            k.wait_all_dma(pool)
                            if DEBUG.get("nocc"):
                                k.wait_all_dma(sp)
                                k.dma(sp, [(AEG[r_ * 320:(r_ + 1) * 320, :], AEX) for r_ in range(4)], writes=[Res("aegdbg")])
                            else:
                                G.collective_compute("AllGather", ALU.bypass, replica_groups=[[0, 1, 2, 3], [4, 5, 6, 7]],
                                                     ins=[AEX.opt()], outs=[AEG.opt()]).then_inc(ccs["edge"])
                    k.extra_toks = [] if DEBUG.get('nocc') else [(ccs["edge"], 1, None)]
                    late_toks = [] if DEBUG.get('nocc') else [(ccs[c_], 1, None) for c_ in range(len(CHUNKS))]
                    k.barrier()
                    if check_stop('p1', l):
                        return
                ph1.close()

                with ExitStack() as pha:
                    ATT = sbt(pha, "att", [128, 8, NT], BF16)
                    PT = [sbt(pha, f"pt{i}", [128, 2, 512], BF16) for i in range(4)]
                    rpt = [Res(f"pt{i}") for i in range(4)]
                    QT = [sbt(pha, f"qt{i}", [128, 512], BF16) for i in range(2)]
                    rqt = [Res("qt0"), Res("qt1")]
                    RL = [sbt(pha, f"rl{i}", [128, 512], F32) for i in range(2)]
                    rrl = [Res("rl0"), Res("rl1")]
                    cnt = {"pt": 0, "q": 0, "s": 0, "o": 0, "rl": 0}

                    def kv_pairs(dst, dcol, r, grp, nrows, c0, n):
                        out = []
                        t = c0
                        while t < c0 + n:
                            ci = chunk_of(t)
                            cb, cT = CHUNKS[ci]
                            m = min(c0 + n, cb + cT) - t
                            row0 = r * KVROWS + XROW[grp]
                            out.append((dst[:, dcol + (t - c0):dcol + (t - c0) + m], KVG[ci][row0:row0 + nrows, t - cb:t - cb + m]))
                            t += m
                        return out

                    def v_src(r, tok0, ntok, col0, ncol, tiles=None, own=False):
                        ci = chunk_of(tok0)
                        cb, cT = CHUNKS[ci]
                        assert tok0 + (ntok if tiles is None else tiles * 128) <= cb + cT
                        buf = KVX[ci] if own else KVG[ci]
                        base = ((0 if own else r * KVROWS) + VREG) * cT + (tok0 - cb) * 192 + col0
                        if tiles is None:
                            return dap(buf, base, [[192, ntok], [1, ncol]])
                        return dap(buf, base, [[192, 128], [128 * 192, tiles], [1, ncol]])

                    def run_pipeline(items, front, back, depth=1):
                        pend = []
                        for it in items:
                            front(it)
                            pend.append(it)
                            if len(pend) > depth:
                                back(pend.pop(0))
                        for it in pend:
                            back(it)

                    def finalize(ob, e, chunk, c0, Tn):
                        ri = cnt["rl"] % 2
                        cnt["rl"] += 1
                        o0, l0 = (0, 64) if e == 0 else (64, 0)
                        k.op(dve, lambda: V.reciprocal(RL[ri][o0:o0 + 64, 0:Tn], bank(ob)[l0:l0 + 64, 0:Tn]), reads=[BR[ob]], writes=[rrl[ri]])
                        ew_tt(dve, ATT[o0:o0 + 64, chunk, c0:c0 + Tn], bank(ob)[o0:o0 + 64, 0:Tn], RL[ri][o0:o0 + 64, 0:Tn], ALU.mult,
                              reads=[BR[ob], rrl[ri]])

                    CKV = sbt(pha, "ckv", [128, 8448], BF16)
                    rckv = Res("ckv")
                    phB = ExitStack()
                    KB = sbt(phB, "kb", [128, 8448], BF16)
                    VB = sbt(phB, "vb", [128, 66, 192], BF16)
                    OSB = [sbt(phB, f"osb{i}", [128, 512], F32) for i in range(2)]
                    RLB = [sbt(phB, f"rlb{i}", [128, 512], F32) for i in range(2)]
                    rosb = [Res("osb0"), Res("osb1")]
                    rrlb = [Res("rlb0"), Res("rlb1")]
                    rkb = Res("kb")
                    for t_ in late_toks:
                        pool.wait(t_)
                    for r in range(4):
                        prs = kv_pairs(KB, r * NL, r, 'kB', 128, 0, NL)
                        prs += kv_pairs(KB, 8192 + r * 64, r, 'kB', 128, NL, 64)
                        for c_ in range(4):
                            prs.append((VB[:, r * 16 + c_ * 4:r * 16 + c_ * 4 + 4, :], v_src(r, c_ * 512, 128, 0, 192, tiles=4)))
                        prs.append((VB[(r % 2) * 64:(r % 2) * 64 + 64, 64 + r // 2, :], v_src(r, NL, 64, 0, 192)))
                        k.dma(pool, prs, writes=[rkb])
                    for r in range(4):
                        k.dma(pool, kv_pairs(CKV, r * NL, r, 'ckv', 128, 0, NL) + kv_pairs(CKV, 8192 + r * 64, r, 'ckv', 128, NL, 64), writes=[rckv])
                    with ExitStack() as ph:
                        KA = sbt(ph, "ka", [128, NL], BF16)
                        KAC = sbt(ph, "kac", [128, 8, 128], BF16)
                        KAX = sbt(ph, "kax", [128, 256], BF16)
                        VA = sbt(ph, "va", [128, 16, 192], BF16)
                        VAC = sbt(ph, "vac", [128, 8, 192], BF16)
                        VAX = sbt(ph, "vax", [128, 2, 192], BF16)
                        MSK = sbt(ph, "msk", [128, 10, 128], BF16)
                        rka = Res("ka")
                        k.dma(sp, [(KA[:], KAL[:, 0:NL]),
                                   (VA[:], dap(VAL, 0, [[192, 128], [128 * 192, 16], [1, 192]])),
                                   (MSK[:], amask)], writes=[rka])
                        prs = []
                        for r in range(4):
                            eb = r * 320
                            vb_ = (eb + 128) * NE
                            prs.append((KAC[:, r, :], AEG[eb:eb + 128, 128:256]))
                            prs.append((KAC[:, 4 + r, :], AEG[eb:eb + 128, 0:128]))
                            prs.append((KAX[:, r * 64:(r + 1) * 64], AEG[eb:eb + 128, 256:NE]))
                            prs.append((VAC[:, r, :], dap(AEG, vb_ + 128 * 192, [[192, 128], [1, 192]])))
                            prs.append((VAC[:, 4 + r, :], dap(AEG, vb_, [[192, 128], [1, 192]])))
                            prs.append((VAX[(r % 2) * 64:(r % 2) * 64 + 64, r // 2, :], dap(AEG, vb_ + 256 * 192, [[192, 64], [1, 192]])))
                        k.dma(sp, prs, writes=[rka])
                        nblk = 16 if last else 17
                        itemsA = []
                        for pc in range(2):
                            for n in range(nblk):
                                c0 = n * 128
                                Tn = 128 if n < 16 else 64
                                tiles = []
                                if n < 16:
                                    if n == 0:
                                        for r in range(4):
                                            tiles.append((KAC[:, r, :], VAC[:, r, :], 2 + r))
                                    else:
                                        tiles.append((KA[:, (n - 1) * 128:n * 128], VA[:, n - 1, :], 0))
                                    tiles.append((KA[:, n * 128:(n + 1) * 128], VA[:, n, :], None))
                                    if n == 15:
                                        for r in range(4):
                                            tiles.append((KAC[:, 4 + r, :], VAC[:, 4 + r, :], 6 + r))
                                    else:
                                        tiles.append((KA[:, (n + 1) * 128:(n + 2) * 128], VA[:, n + 1, :], 1))
                                tiles.append((KAX[:, 0:128], VAX[:, 0, :], None))
                                tiles.append((KAX[:, 128:256], VAX[:, 1, :], None))
                                blk = {"pc": pc, "c0": c0, "Tn": Tn}
                                for ti, (kt, vt, mi) in enumerate(tiles):
                                    itemsA.append({"blk": blk, "kt": kt, "vt": vt, "mi": mi, "first": ti == 0, "last": ti == len(tiles) - 1})

                        def frontA(it):
                            blk = it["blk"]
                            Tn = blk["Tn"]
                            if it["first"]:
                                qi = cnt["q"] % 2
                                cnt["q"] += 1
                                blk["qi"] = qi
                                k.dma(sp, [(QT[qi][:, 0:Tn], QS[blk["pc"] * 128:(blk["pc"] + 1) * 128, blk["c0"]:blk["c0"] + Tn])], writes=[rqt[qi]])
                                blk["ob"] = [4 + (cnt["o"] % 2) * 2, 5 + (cnt["o"] % 2) * 2]
                                cnt["o"] += 1
                            qi = blk["qi"]
                            sp_ = cnt["s"] % 2
                            cnt["s"] += 1
                            s0, s1 = 2 * sp_, 2 * sp_ + 1
                            kt = it["kt"]

                            def qk():
                                T.matmul(bank(s0)[:, 0:Tn], kt[0:64, :], QT[qi][0:64, 0:Tn], start=True, stop=True)
                                return T.matmul(bank(s1)[:, 0:Tn], kt[64:128, :], QT[qi][64:128, 0:Tn], start=True, stop=True)
                            k.op(pe, qk, reads=[rka, rqt[qi]], writes=[BR[s0], BR[s1]])
                            pi = cnt["pt"] % 3
                            cnt["pt"] += 1
                            it["pi"] = pi
                            sv = PP[sp_][:].rearrange("p (e t) -> p e t", e=2)[:, :, 0:Tn]
                            pv_ = PT[pi][:, :, 0:Tn]
                            k.op(act, lambda: S.activation(pv_, sv, AF.Exp, scale=0.125), reads=[BR[s0], BR[s1]], writes=[rpt[pi]])
                            if it["mi"] is not None:
                                mk = MSK[:, it["mi"], 0:Tn].unsqueeze(1).to_broadcast([128, 2, Tn])
                                ew_tt(dve, pv_, pv_, mk, ALU.mult, reads=[rpt[pi], rka], writes=[rpt[pi]])

                        def backA(it):
                            blk = it["blk"]
                            Tn, ob, pi, vt = blk["Tn"], blk["ob"], it["pi"], it["vt"]
                            for e in range(2):
                                h = blk["pc"] + 2 * e

                                def pvm():
                                    inst = T.matmul(bank(ob[e])[:, 0:Tn], vt[:, e * 64:e * 64 + 128], PT[pi][:, e, 0:Tn], start=it["first"], stop=False)
                                    if it["last"]:
                                        sl = SINKL[0:1, 0:128] if e == 0 else SINKL[0:1, 64:192]
                                        inst = T.matmul(bank(ob[e])[:, 0:Tn], sl, ESROW[0:1, h, 0:Tn], start=False, stop=True)
                                    return inst
                                k.op(pe, pvm, reads=[rpt[pi], rka, r_const], writes=[BR[ob[e]]])
                            if it["last"]:
                                for e in range(2):
                                    finalize(ob[e], e, blk["pc"], blk["c0"], Tn)
                        run_pipeline(itemsA, frontA, backA)
                        k.extra_toks = k.extra_toks + late_toks
                        k.barrier()
                        if check_stop('a', l):
                            return

                    with ExitStack() as ph:
                        itemsB = []
                        blocksB = []
                        for pc in range(2):
                            for ci, (c0, Tn) in enumerate(CHUNKS):
                                if ci == 4 and last:
                                    continue
                                jl = list(range(66)) if ci < 4 else [64, 65]
                                blk = {"pc": pc, "c0": c0, "Tn": Tn, "bidx": len(blocksB)}
                                blocksB.append(blk)
                                for ji, j in enumerate(jl):
                                    itemsB.append({"blk": blk, "j": j, "first": ji == 0, "last": ji == len(jl) - 1})

                        def loadqB(blk):
                            qi = blk["bidx"] % 2
                            blk["qi"] = qi
                            Tn = blk["Tn"]
                            k.dma(sp, [(QT[qi][:, 0:Tn], QS[256 + blk["pc"] * 128:256 + (blk["pc"] + 1) * 128, blk["c0"]:blk["c0"] + Tn])], writes=[rqt[qi]])
                        loadqB(blocksB[0])

                        def frontB(it):
                            blk = it["blk"]
                            Tn, j = blk["Tn"], it["j"]
                            if it["first"] and blk["bidx"] + 1 < len(blocksB):
                                loadqB(blocksB[blk["bidx"] + 1])
                            qi = blk["qi"]
                            sp_ = cnt["s"] % 3
                            cnt["s"] += 1
                            s0, s1 = 2 * sp_, 2 * sp_ + 1

                            def qk():
                                T.matmul(bank(s0)[:, 0:Tn], KB[0:64, j * 128:(j + 1) * 128], QT[qi][0:64, 0:Tn], start=True, stop=True)
                                return T.matmul(bank(s1)[:, 0:Tn], KB[64:128, j * 128:(j + 1) * 128], QT[qi][64:128, 0:Tn], start=True, stop=True)
                            k.op(pe, qk, reads=[rkb, rqt[qi]], writes=[BR[s0], BR[s1]])
                            pi = cnt["pt"] % 4
                            cnt["pt"] += 1
                            it["pi"] = pi
                            sv = PP[sp_][:].rearrange("p (e t) -> p e t", e=2)[:, :, 0:Tn]
                            k.op(act, lambda: S.activation(PT[pi][:, :, 0:Tn], sv, AF.Exp, scale=0.125),
                                 reads=[BR[s0], BR[s1]], writes=[rpt[pi]])

                        def backB(it):
                            blk = it["blk"]
                            Tn, pi, j = blk["Tn"], it["pi"], it["j"]
                            ob = [6, 7]

                            def pvm():
                                T.matmul(bank(ob[0])[:, 0:Tn], VB[:, j, 0:128], PT[pi][:, 0, 0:Tn], start=it["first"], stop=it["last"])
                                return T.matmul(bank(ob[1])[:, 0:Tn], VB[:, j, 64:192], PT[pi][:, 1, 0:Tn], start=it["first"], stop=it["last"])
                            k.op(pe, pvm, reads=[rpt[pi], rkb], writes=[BR[ob[0]], BR[ob[1]]])
                            if it["last"]:
                                for e in range(2):
                                    k.op(dve, lambda e=e: V.tensor_copy(OSB[e][:, 0:Tn], bank(ob[e])[:, 0:Tn]), reads=[BR[ob[e]]], writes=[rosb[e]])
                                for e in range(2):
                                    o0, l0 = (0, 64) if e == 0 else (64, 0)
                                    k.op(dve, lambda e=e, o0=o0, l0=l0: V.reciprocal(RLB[e][o0:o0 + 64, 0:Tn], OSB[e][l0:l0 + 64, 0:Tn]),
                                         reads=[rosb[e]], writes=[rrlb[e]])
                                    ew_tt(dve, ATT[o0:o0 + 64, 2 + blk["pc"], blk["c0"]:blk["c0"] + Tn], OSB[e][o0:o0 + 64, 0:Tn],
                                          RLB[e][o0:o0 + 64, 0:Tn], ALU.mult, reads=[rosb[e], rrlb[e]])
                        run_pipeline(itemsB, frontB, backB, depth=2)
                        k.barrier()
                        if check_stop('b', l):
                            return
                    phB.close()

                    with ExitStack() as ph:
                        KC = [sbt(ph, f"kc{i}", [96, 8448], BF16) for i in range(2)]
                        VC = [sbt(ph, f"vc{i}", [128, 66, 128], BF16) for i in range(2)]
                        WUK = sbt(ph, "wuk", [128, 512], BF16)
                        WUV = sbt(ph, "wuv", [128, 512], BF16)
                        rw = Res("wukv")
                        rkc = [Res("kc0"), Res("kc1")]
                        rvc = [Res("vc0"), Res("vc1")]
                        k.dma(pool, [(WUK[:], w_uk[l]), (WUV[:], w_uv[l])], writes=[rw])
                        for i in range(2):
                            prs = []
                            for r in range(4):
                                prs += kv_pairs(KC[i][64:96, :], r * NL, r, 'kr', 32, 0, NL)
                                prs += kv_pairs(KC[i][64:96, :], 8192 + r * 64, r, 'kr', 32, NL, 64)
                            k.dma(sp, prs, writes=[rkc[i]])
                        k.op(pool, lambda: G.memset(VC[0][:, :, 64:128], 1.0), writes=[rvc[0]])
                        k.op(pool, lambda: G.memset(VC[1][:, :, 0:64], 1.0), writes=[rvc[1]])

                        def prep_pieces(h):
                            nb = h % 2
                            voff = 0 if nb == 0 else 64
                            pcs = []
                            for kc0 in range(0, 8448, 512):
                                n = min(512, 8448 - kc0)

                                def pk(bi, kc0=kc0, n=n):
                                    mm_group(bank(bi)[0:64, 0:n], [(WUK[:, h * 64:(h + 1) * 64], CKV[:, kc0:kc0 + n])], reads=[rckv, rw], writes=[BR[bi]])
                                    k.op(dve, lambda: V.tensor_copy(KC[nb][0:64, kc0:kc0 + n], bank(bi)[0:64, 0:n]), reads=[BR[bi]], writes=[rkc[nb]])
                                pcs.append(pk)
                            for j0 in range(0, 66, 8):
                                nj = min(8, 66 - j0)

                                def pv(bi, j0=j0, nj=nj):
                                    def vmm():
                                        inst = None
                                        for a in range(nj):
                                            inst = T.matmul(bank(bi)[:, a * 64:(a + 1) * 64], CKV[:, (j0 + a) * 128:(j0 + a + 1) * 128],
                                                            WUV[:, h * 64:(h + 1) * 64], start=True, stop=True)
                                        return inst
                                    k.op(pe, vmm, reads=[rckv, rw], writes=[BR[bi]])
                                    srcv = bank(bi)[:, 0:nj * 64].rearrange("p (a c) -> p a c", c=64)
                                    dstv = VC[nb][:, j0:j0 + nj, voff:voff + 64]
                                    k.op(dve, lambda: V.tensor_copy(dstv, srcv), reads=[BR[bi]], writes=[rvc[nb]])
                                pcs.append(pv)
                            return pcs

                        for pi_, pc_ in enumerate(prep_pieces(0)):
                            pc_(6 + pi_ % 2)
                        itemsC = []
                        blocksC = []
                        for h in range(8):
                            hitems = []
                            for ci, (c0, Tn) in enumerate(CHUNKS):
                                if ci == 4 and last:
                                    continue
                                jl = list(range(0, 66, 2)) if ci < 4 else [64]
                                blk = {"h": h, "c0": c0, "Tn": Tn, "bidx": len(blocksC)}
                                blocksC.append(blk)
                                for ji, j in enumerate(jl):
                                    hitems.append({"blk": blk, "j": j, "ji": ji, "first": ji == 0, "last": ji == len(jl) - 1, "prep": None})
                            if h < 7:
                                pcs = prep_pieces(h + 1)
                                cand = [it for it in hitems if it["ji"] >= 4 and not it["last"]]
                                step = max(1, len(cand) // len(pcs))
                                for pi_, pc_ in enumerate(pcs):
                                    cand[min(pi_ * step, len(cand) - 1 - (len(pcs) - 1 - pi_))]["prep"] = pc_
                            itemsC.extend(hitems)

                        def loadqC(blk):
                            qi = blk["bidx"] % 2
                            blk["qi"] = qi
                            Tn, h = blk["Tn"], blk["h"]
                            k.dma(sp, [(QT[qi][0:96, 0:Tn], QS[512 + h * 96:512 + (h + 1) * 96, blk["c0"]:blk["c0"] + Tn])], writes=[rqt[qi]])
                        loadqC(blocksC[0])

                        def frontC(it):
                            blk = it["blk"]
                            Tn, j, h = blk["Tn"], it["j"], blk["h"]
                            kb_ = h % 2
                            if it["first"] and blk["bidx"] + 1 < len(blocksC):
                                loadqC(blocksC[blk["bidx"] + 1])
                            if it["prep"] is not None:
                                it["prep"](6 + (blk["bidx"] + 1) % 2)
                            qi = blk["qi"]
                            sp_ = cnt["s"] % 3
                            cnt["s"] += 1
                            s0, s1 = 2 * sp_, 2 * sp_ + 1

                            def qk():
                                T.matmul(bank(s0)[:, 0:Tn], KC[kb_][:, j * 128:(j + 1) * 128], QT[qi][0:96, 0:Tn], start=True, stop=True)
                                return T.matmul(bank(s1)[:, 0:Tn], KC[kb_][:, (j + 1) * 128:(j + 2) * 128], QT[qi][0:96, 0:Tn], start=True, stop=True)
                            k.op(pe, qk, reads=[rkc[kb_], rqt[qi]], writes=[BR[s0], BR[s1]])
                            pi = cnt["pt"] % 4
                            cnt["pt"] += 1
                            it["pi"] = pi
                            sv = PP[sp_][:].rearrange("p (e t) -> p e t", e=2)[:, :, 0:Tn]
                            k.op(act, lambda: S.activation(PT[pi][:, :, 0:Tn], sv, AF.Exp, scale=float(96 ** -0.5)),
                                 reads=[BR[s0], BR[s1]], writes=[rpt[pi]])

                        def backC(it):
                            blk = it["blk"]
                            Tn, pi, j, h = blk["Tn"], it["pi"], it["j"], blk["h"]
                            ob = 6 + blk["bidx"] % 2
                            nb = h % 2

                            def pvm():
                                T.matmul(bank(ob)[:, 0:Tn], VC[nb][:, j, :], PT[pi][:, 0, 0:Tn], start=it["first"], stop=False)
                                return T.matmul(bank(ob)[:, 0:Tn], VC[nb][:, j + 1, :], PT[pi][:, 1, 0:Tn], start=False, stop=it["last"])
                            k.op(pe, pvm, reads=[rpt[pi], rvc[nb]], writes=[BR[ob]])
                            if it["last"]:
                                finalize(ob, nb, 4 + h // 2, blk["c0"], Tn)
                        run_pipeline(itemsC, frontC, backC, depth=2)
                        k.barrier()
                        if check_stop('c', l):
                            return

                    with ExitStack() as ph:
                        WOUT = sbt(ph, "wout", [128, 8, D], BF16)
                        SQ2 = [sbt(ph, f"sq{i}", [128, 8, 512], F32) for i in range(2)]
                        RS2 = [sbt(ph, f"rs{i}", [128, 512], F32) for i in range(2)]
                        lnr2 = [([Res(f"sq{i}") for i in range(8)], Res("rs")) for _ in range(2)]
                        rwo = Res("wout")
                        k.dma(pool, [(WOUT[:, 0:4, :], w_outx[l].rearrange("(c p) n -> p c n", p=128)[:, 0:4, :]),
                                     (WOUT[:, 4:8, :], w_outx[l].rearrange("(c p) n -> p c n", p=128)[:, 4:8, :])], writes=[rwo])
                        pend = None
                        for ci, (c0, Tn) in enumerate(CHUNKS):
                            if ci == 4 and last:
                                continue
                            isctx = 1 if ci == 4 else 0
                            rz = [Res(f"z{c}") for c in range(8)]
                            for oc in range(8):
                                bi = oc % 4
                                mm_group(bank(bi)[:, 0:Tn], [(WOUT[:, ic, oc * 128:(oc + 1) * 128], ATT[:, ic, c0:c0 + Tn]) for ic in range(8)],
                                         reads=[rwo], writes=[BR[bi]])
                                k.op(dve, lambda oc=oc, bi=bi: V.scalar_tensor_tensor(XS[:, oc, c0:c0 + Tn], bank(bi)[:, 0:Tn],
                                                                                       MOD[isctx][:, 16 + oc:17 + oc], XS[:, oc, c0:c0 + Tn], ALU.mult, ALU.add),
                                     reads=[BR[bi]], writes=[rz[oc]])
                            pp_ = ci % 2
                            largs = (k, nc, bank, BR, XS, onesm, epsc, r_const, c0, Tn, rz, SQ2[pp_], RS2[pp_], DER[:, 32:40], DER[:, 40:48], lnr2[pp_])
                            lkw = dict(mb=6 - 2 * pp_, vb=7 - 2 * pp_)
                            _ln_stage1(*largs, **lkw)
                            if pend is not None:
                                _ln_stage2(*pend[0], **pend[1])
                            pend = (largs, lkw)
                        if pend is not None:
                            _ln_stage2(*pend[0], **pend[1])
                        k.barrier()
                        if check_stop('p3', l):
                            return

                sblocks = [[CHUNKS[0], CHUNKS[1]], [CHUNKS[2], CHUNKS[3]] + ([] if last else [CHUNKS[4]])]
                with ExitStack() as p4:
                    W1S = [sbt(p4, f"w1s{i}", [128, 8, 512], BF16) for i in range(2)]
                    W2S = [sbt(p4, f"w2s{i}", [128, 32, 128], BF16) for i in range(2)]
                    rw1 = [Res("w1s0"), Res("w1s1")]
                    rw2 = [Res("w2s0"), Res("w2s1")]
                    nsb = len(sblocks)

                    def load_w1(idx):
                        if idx >= nsb * 8:
                            return
                        fs, wb = idx % 8, idx % 2
                        k.dma(pool, [(W1S[wb][:, 0:4, :], w_fc1[l, fs][:, 0:4, :]), (W1S[wb][:, 4:8, :], w_fc1[l, fs][:, 4:8, :])], writes=[rw1[wb]])

                    def load_w2(idx):
                        if idx >= nsb * 8:
                            return
                        oc, wb = idx % 8, idx % 2
                        k.dma(pool, [(W2S[wb][:, 0:16, :], w_fc2[l, oc][:, 0:16, :]), (W2S[wb][:, 16:32, :], w_fc2[l, oc][:, 16:32, :])], writes=[rw2[wb]])
                    load_w1(0)
                    load_w1(1)
                    load_w2(0)
                    load_w2(1)
                    for sbi, sbk_ in enumerate(sblocks):
                        base = sbk_[0][0]
                        rzs = {}
                        with ExitStack() as ph:
                            H2 = sbt(ph, "h2", [128, 8, 1088], BF16)
                            HID = sbt(ph, "hid", [128, 32, 1088], BF16)
                            RLU = [sbt(ph, f"rlu{i}", [128, 512], F32) for i in range(2)]
                            rh2 = [Res(f"h2_{c}") for c in range(8)]
                            rhid = [Res(f"hid{c}") for c in range(32)]
                            rrlu = [Res("rlu0"), Res("rlu1")]
                            for (c0, Tn) in sbk_:
                                isctx = 1 if c0 == NL else 0
                                for c in range(8):
                                    ew_ts(dve if c % 2 == 0 else pool, H2[:, c, c0 - base:c0 - base + Tn], XS[:, c, c0:c0 + Tn], A2[isctx][:, c:c + 1],
                                          MOD[isctx][:, 24 + c:25 + c], ALU.mult, ALU.add, writes=[rh2[c]])
                            bi_ = 0
                            ru = 0
                            for fs in range(8):
                                idx = sbi * 8 + fs
                                wb = idx % 2
                                for fj in range(4):
                                    fc = fs * 4 + fj
                                    for (c0, Tn) in sbk_:
                                        o0 = c0 - base
                                        bi = bi_ % 4
                                        bi_ += 1
                                        mm_group(bank(bi)[:, 0:Tn], [(W1S[wb][:, kc, fj * 128:(fj + 1) * 128], H2[:, kc, o0:o0 + Tn]) for kc in range(8)],
                                                 reads=[rw1[wb]] + rh2, writes=[BR[bi]])
                                        ri = ru % 2
                                        ru += 1
                                        k.op(act, lambda bi=bi, ri=ri, Tn=Tn: S.activation(RLU[ri][:, 0:Tn], bank(bi)[:, 0:Tn], AF.Relu),
                                             reads=[BR[bi]], writes=[rrlu[ri]])
                                        ew_tt(dve if fc % 2 == 0 else pool, HID[:, fc, o0:o0 + Tn], RLU[ri][:, 0:Tn], RLU[ri][:, 0:Tn], ALU.mult,
                                              reads=[rrlu[ri]], writes=[rhid[fc]])
                                load_w1(idx + 2)
                            for oc in range(8):
                                idx = sbi * 8 + oc
                                wb = idx % 2
                                for (c0, Tn) in sbk_:
                                    isctx = 1 if c0 == NL else 0
                                    o0 = c0 - base
                                    bi = 4 + bi_ % 4
                                    bi_ += 1
                                    mm_group(bank(bi)[:, 0:Tn], [(W2S[wb][:, fc, :], HID[:, fc, o0:o0 + Tn]) for fc in range(32)],
                                             reads=[rw2[wb]] + rhid, writes=[BR[bi]])
                                    rzs[(c0, oc)] = Res("z")
                                    k.op(dve, lambda oc=oc, bi=bi, c0=c0, Tn=Tn, isctx=isctx: V.scalar_tensor_tensor(
                                        XS[:, oc, c0:c0 + Tn], bank(bi)[:, 0:Tn], MOD[isctx][:, 40 + oc:41 + oc], XS[:, oc, c0:c0 + Tn], ALU.mult, ALU.add),
                                        reads=[BR[bi]], writes=[rzs[(c0, oc)]])
                                load_w2(idx + 2)
                            k.barrier()
                        with ExitStack() as ph:
                            SQ2 = [sbt(ph, f"sq2{i}", [128, 8, 512], F32) for i in range(2)]
                            RS2 = [sbt(ph, f"rs2{i}", [128, 512], F32) for i in range(2)]
                            lnr2 = [([Res(f"sq{i}") for i in range(8)], Res("rs")) for _ in range(2)]
                            pend = None
                            for cj, (c0, Tn) in enumerate(sbk_):
                                rz = [rzs[(c0, oc)] for oc in range(8)]
                                pp_ = cj % 2
                                largs = (k, nc, bank, BR, XS, onesm, epsc, r_const, c0, Tn, rz, SQ2[pp_], RS2[pp_], DER[:, 48:56], DER[:, 56:64], lnr2[pp_])
                                lkw = dict(mb=6 - 2 * pp_, vb=7 - 2 * pp_)
                                _ln_stage1(*largs, **lkw)
                                if pend is not None:
                                    _ln_stage2(*pend[0], **pend[1])
                                pend = (largs, lkw)
                            if pend is not None:
                                _ln_stage2(*pend[0], **pend[1])
                            k.barrier()

            run_layers()
            with ExitStack() as ph:
                YST = [sbt(ph, f"yst{i}", [128, D], F32) for i in range(2)]
                ry = [Res("yst0"), Res("yst1")]
                rout = Res("yout")
                for t in range(16):
                    buf = t % 2
                    for half in range(2):
                        bi = (t * 2 + half) % 4

                        def tr(bi=bi, half=half, t=t):
                            inst = None
                            for c in range(4):
                                inst = T.transpose(bank(bi)[:, c * 128:(c + 1) * 128], XS[:, half * 4 + c, t * 128:(t + 1) * 128], ident[:])
                            return inst
                        k.op(pe, tr, reads=[r_const], writes=[BR[bi]])
                        if half == 0:
                            k.op(act, lambda bi=bi, buf=buf, half=half: S.copy(YST[buf][:, half * 512:(half + 1) * 512], bank(bi)[:, 0:512]),
                                 reads=[BR[bi]], writes=[ry[buf]])
                        else:
                            k.op(dve, lambda bi=bi, buf=buf, half=half: V.tensor_copy(YST[buf][:, half * 512:(half + 1) * 512], bank(bi)[:, 0:512]),
                                 reads=[BR[bi]], writes=[ry[buf]])
                    k.dma(sp, [(y_out[t * 128:(t + 1) * 128, :], YST[buf][:])], reads=[ry[buf]], writes=[rout], sem_res=ry[buf])
                k.wait_all_dma(sp)
                k.barrier()
        print("nsem", k.nsem, "counts", {e.name: e.count for e in k.engs})
    return nc


def _ln_stage1(k, nc, bank, BR, XS, onesm, epsc, r_const, c0, Tn, rz, SQ, RS, Gc, Bc, lnr, mb=6, vb=7):
    V, S, G, T = nc.vector, nc.scalar, nc.gpsimd, nc.tensor
    pe, act, dve, pool = k.pe, k.act, k.dve, k.pool

    def mean_mm():
        inst = None
        for oc in range(8):
            inst = T.matmul(bank(mb)[:, 0:Tn], onesm[:], XS[:, oc, c0:c0 + Tn], start=(oc == 0), stop=(oc == 7))
        return inst
    k.op(pe, mean_mm, reads=list(rz) + [r_const], writes=[BR[mb]])
    rsq = lnr[0]
    for oc in range(8):
        k.op(dve, lambda oc=oc: V.tensor_tensor(XS[:, oc, c0:c0 + Tn], XS[:, oc, c0:c0 + Tn], bank(mb)[:, 0:Tn], op=ALU.subtract),
             reads=[rz[oc], BR[mb]], writes=[rz[oc]])
        k.op(pool, lambda oc=oc: G.tensor_tensor(SQ[:, oc, 0:Tn], XS[:, oc, c0:c0 + Tn], XS[:, oc, c0:c0 + Tn], op=ALU.mult),
             reads=[rz[oc]], writes=[rsq[oc]])


def _ln_stage2(k, nc, bank, BR, XS, onesm, epsc, r_const, c0, Tn, rz, SQ, RS, Gc, Bc, lnr, mb=6, vb=7):
    V, S, G, T = nc.vector, nc.scalar, nc.gpsimd, nc.tensor
    pe, act, dve, pool = k.pe, k.act, k.dve, k.pool
    rsq = lnr[0]

    def var_mm():
        inst = None
        for oc in range(8):
            inst = T.matmul(bank(vb)[:, 0:Tn], onesm[:], SQ[:, oc, 0:Tn], start=(oc == 0), stop=(oc == 7))
        return inst
    k.op(pe, var_mm, reads=rsq + [r_const], writes=[BR[vb]])
    rrs = lnr[1]
    k.op(act, lambda: S.activation(RS[:, 0:Tn], bank(vb)[:, 0:Tn], AF.Ln, bias=epsc[:], scale=1.0), reads=[BR[vb], r_const], writes=[rrs])
    k.op(act, lambda: S.activation(RS[:, 0:Tn], RS[:, 0:Tn], AF.Exp, scale=-0.5), reads=[rrs], writes=[rrs])
    for oc in range(8):
        k.op(dve, lambda oc=oc: V.scalar_tensor_tensor(XS[:, oc, c0:c0 + Tn], XS[:, oc, c0:c0 + Tn], Gc[:, oc:oc + 1], RS[:, 0:Tn], ALU.mult, ALU.mult),
             reads=[rz[oc], rrs], writes=[rz[oc]])
        k.op(act, lambda oc=oc: S.activation(XS[:, oc, c0:c0 + Tn], XS[:, oc, c0:c0 + Tn], AF.Identity, bias=Bc[:, oc:oc + 1], scale=1.0),
             reads=[rz[oc]], writes=[rz[oc]])


def _rope_tables(qd):
    GRID_W = 64
    pos = np.arange(NL, dtype=np.int64) + qd * NL
    row = (pos // GRID_W).astype(np.float32)
    col = (pos % GRID_W).astype(np.float32)

    def tab(rot_dim):
        nf = rot_dim // 4
        inv = (np.float32(10000.0) ** (-np.arange(nf, dtype=np.float32) / np.float32(nf))).astype(np.float32)
        ang = np.concatenate([row[:, None] * inv, col[:, None] * inv], axis=-1).astype(np.float32)
        c = np.cos(ang).astype(np.float32).T
        s = np.sin(ang).astype(np.float32).T
        C = np.concatenate([c, c], 0)
        Sg = np.concatenate([-s, s], 0)
        Cf = np.ones((rot_dim, NT), np.float32)
        Sf = np.zeros((rot_dim, NT), np.float32)
        Cf[:, :NL] = C
        Sf[:, :NL] = Sg
        return Cf, Sf
    c64, s64 = tab(64)
    c32, s32 = tab(32)
    out = np.zeros((4, 128, NT), np.float32)
    out[0] = np.concatenate([c64, c64], 0)
    out[1] = np.concatenate([s64, s64], 0)
    out[2, 0:32] = c32
    out[2, 64:96] = c32
    out[3, 0:32] = s32
    out[3, 64:96] = s32
    return out


def _masks(qd):
    kk = np.arange(128)[:, None]
    qq = np.arange(128)[None, :]
    ML = (kk >= qq).astype(np.float32)
    MR = (kk <= qq).astype(np.float32)
    m = np.zeros((128, 10, 128), np.float32)
    m[:, 0] = ML
    m[:, 1] = MR
    for r in range(4):
        if r == qd - 1:
            m[:, 2 + r] = ML
        if r == qd + 1:
            m[:, 6 + r] = MR
    return m.astype(ml_dtypes.bfloat16)


def _prep_weights(w_in, w_uq, w_out, q_norm_b, k_norm_b, mla_q_norm, mla_kv_norm, ln1_g, ln1_b, ln2_g, ln2_b):
    def sw(cols, half):
        cols = np.asarray(cols).reshape(-1, 2, half)
        return cols[:, ::-1, :].reshape(-1)
    aq = np.arange(0, 256).reshape(4, 64)
    ak = np.arange(256, 384)
    av = np.arange(384, 512)
    bq = np.arange(512, 768).reshape(4, 64)
    bk = np.arange(768, 896)
    bv = np.arange(896, 1024)
    cq = np.arange(1024, 1280)
    ckv = np.arange(1280, 1408)
    ckr = np.arange(1408, 1440)
    gA = [np.concatenate([aq[0], aq[2]]), np.concatenate([aq[1], aq[3]]), ak]
    gB = [np.concatenate([bq[0], bq[2]]), np.concatenate([bq[1], bq[3]]), bk]
    groups = {}
    for i in range(3):
        groups[i] = gA[i]
        groups[3 + i] = sw(gA[i], 32)
        groups[6 + i] = gB[i]
        groups[9 + i] = sw(gB[i], 32)
    groups[12] = cq[:128]
    groups[13] = cq[128:]
    groups[14] = ckv
    cols = np.concatenate([groups[g] for g in range(15)] + [ckr, sw(ckr, 16), av, bv])
    assert cols.shape[0] == NWIN
    w_inx = np.ascontiguousarray(w_in[:, :, cols])
    uq_cols = np.arange(768).reshape(8, 96)
    uq_sw = uq_cols.copy()
    for h in range(8):
        uq_sw[h, 64:96] = sw(uq_cols[h, 64:96], 16)
    w_uqx = np.ascontiguousarray(np.concatenate([w_uq, w_uq[:, :, uq_sw.reshape(-1)]], axis=2))
    hr = lambda base, h: np.arange(base + h * 64, base + (h + 1) * 64)
    rows = np.concatenate([hr(0, 0), hr(0, 2), hr(0, 1), hr(0, 3), hr(256, 0), hr(256, 2), hr(256, 1), hr(256, 3), np.arange(512, 1024)])
    w_outx = np.ascontiguousarray(w_out[:, rows, :])
    Ln = w_in.shape[0]
    vecs = np.zeros((Ln, 40, 128), np.float32)
    vecs[:, 0:8] = ln1_g.reshape(Ln, 8, 128)
    vecs[:, 8:16] = ln1_b.reshape(Ln, 8, 128)
    vecs[:, 16:24] = ln2_g.reshape(Ln, 8, 128)
    vecs[:, 24:32] = ln2_b.reshape(Ln, 8, 128)
    vecs[:, 32:34] = mla_q_norm.reshape(Ln, 2, 128)
    vecs[:, 34] = mla_kv_norm
    swi = sw(np.arange(64), 32)
    vecs[:, 35] = np.concatenate([q_norm_b, q_norm_b], 1)
    vecs[:, 36] = np.concatenate([q_norm_b[:, swi], q_norm_b[:, swi]], 1)
    vecs[:, 37] = np.concatenate([k_norm_b, k_norm_b], 1)
    vecs[:, 38] = np.concatenate([k_norm_b[:, swi], k_norm_b[:, swi]], 1)
    return w_inx, w_uqx, w_outx, vecs


_CACHE = {}


def kernel(x, c, ctx, c_ctx, w_mod, b_mod, w_in, sink_a, q_norm_b, k_norm_b, mla_q_norm, mla_kv_norm,
           w_uq, w_uk, w_uv, w_out, ln1_g, ln1_b, w_fc1, w_fc2, ln2_g, ln2_b):
    f = lambda a: np.ascontiguousarray(np.asarray(a, dtype=np.float32))
    x, c, ctx, c_ctx = f(x), f(c), f(ctx), f(c_ctx)
    w_mod, b_mod, w_in, sink_a = f(w_mod), f(b_mod), f(w_in), f(sink_a)
    w_uq, w_uk, w_uv, w_out, w_fc1, w_fc2 = f(w_uq), f(w_uk), f(w_uv), f(w_out), f(w_fc1), f(w_fc2)
    w_fc1 = np.ascontiguousarray(w_fc1.reshape(L, 8, 128, 8, 512).transpose(0, 3, 2, 1, 4))
    w_fc2 = np.ascontiguousarray(w_fc2.reshape(L, 32, 128, 8, 128).transpose(0, 3, 2, 1, 4))
    w_inx, w_uqx, w_outx, vecs = _prep_weights(w_in, w_uq, w_out, f(q_norm_b), f(k_norm_b), f(mla_q_norm), f(mla_kv_norm),
                                               f(ln1_g), f(ln1_b), f(ln2_g), f(ln2_b))
    w_mod_sh, b_mod_sh = [], []
    for qd_ in range(4):
        fch = [s_ * 8 + 2 * qd_ + j_ for s_ in range(6) for j_ in range(2)]
        cols = np.concatenate([np.arange(f_ * 128, (f_ + 1) * 128) for f_ in fch])
        w_mod_sh.append(np.ascontiguousarray(w_mod[:, :, cols]))
        b_mod_sh.append(np.ascontiguousarray(b_mod[:, cols].reshape(L * 12, 128)))
    if "nc" not in _CACHE:
        _CACHE["nc"] = build_program()
    nc = _CACHE["nc"]
    in_maps = []
    for core in range(8):
        b, qd = core // 4, core % 4
        cv = np.stack([c[b].reshape(8, 128), c_ctx.reshape(8, 128)], axis=1).reshape(16, 128)
        in_maps.append({
            "x": np.ascontiguousarray(x[b, qd * NL:(qd + 1) * NL]),
            "ctx": np.ascontiguousarray(ctx[b, qd * NCX:(qd + 1) * NCX]),
            "cvec": np.ascontiguousarray(cv),
            "w_mod": w_mod_sh[qd], "b_mod": b_mod_sh[qd],
            "w_inx": w_inx, "w_uqx": w_uqx, "w_uk": w_uk, "w_uv": w_uv, "w_outx": w_outx,
            "w_fc1": w_fc1, "w_fc2": w_fc2, "vecs": vecs, "sink": sink_a.reshape(L, 1, 4),
            "ropes": _rope_tables(qd), "amask": _masks(qd),
        })
    res = run_bass_kernel_spmd(nc, in_maps, core_ids=list(range(8)))
    out = np.zeros((2, 8192, D), np.float32)
    for core in range(8):
        b, qd = core // 4, core % 4
        out[b, qd * NL:(qd + 1) * NL] = res.results[core]["y"]
    return out
```
